# Optimizing a Trainium2 kernel written in Bass

```python
import math
import jax, jax.numpy as jnp
from jax import lax
import numpy as np

D_MODEL = 1024
BATCH = 32
SEQ = 2048
DEPTH = 4

D_RNN = 1536
RNN_BLOCKS = 12
RNN_BLOCK = D_RNN // RNN_BLOCKS
RG_C = 8.0
CONV_K = 4
DN_HEADS = 8
DN_DK = 128
DN_DV = 128
DN_QK = DN_HEADS * DN_DK
DN_V = DN_HEADS * DN_DV
DN_QKV = 2 * DN_QK + DN_V
CHUNK = 64
D_FF = ((8 * D_MODEL + 3 * 256 - 1) // (3 * 256)) * 256
EPS = 1e-6

N_BA = 4 * DN_HEADS
N_GATES = 2 * D_MODEL
SPLIT_IDX = [D_RNN, 2 * D_RNN, 2 * D_RNN + DN_QKV, 2 * D_RNN + DN_QKV + DN_V,
             2 * D_RNN + DN_QKV + DN_V + N_BA]
N_IN = 2 * D_RNN + DN_QKV + DN_V + N_BA + N_GATES

kernel_name = "hybrid_rglru_gdn_parallel_encoder"


def rmsnorm(x, g):
    xf = x.astype(jnp.float32)
    y = xf * lax.rsqrt(jnp.mean(xf * xf, axis=-1, keepdims=True) + EPS)
    return (y * g.astype(jnp.float32)).astype(x.dtype)


def dwconv_centred(x, w):
    C = x.shape[-1]
    pad_l = CONV_K // 2
    return lax.conv_general_dilated(
        x, w[:, None, :].astype(x.dtype), window_strides=(1,),
        padding=[(pad_l, CONV_K - 1 - pad_l)],
        dimension_numbers=("NWC", "WIO", "NWC"), feature_group_count=C)


def _lin_combine(e1, e2):
    a1, b1 = e1
    a2, b2 = e2
    return a1 * a2, a2 * b1 + b2


def rglru_branch(rx, ry, conv_w, conv_b, wa, ba, wx, bx, lam):
    Bn, S, _ = rx.shape
    u = (dwconv_centred(rx, conv_w) + conv_b.astype(rx.dtype)).astype(jnp.float32)
    ub = u.reshape(Bn, S, RNN_BLOCKS, RNN_BLOCK)
    r = jax.nn.sigmoid(jnp.einsum("bsnk,dnkj->dbsnj", ub, wa.astype(jnp.float32)).reshape(2, Bn, S, D_RNN)
                       + ba.astype(jnp.float32)[:, None, None, :])
    i = jax.nn.sigmoid(jnp.einsum("bsnk,dnkj->dbsnj", ub, wx.astype(jnp.float32)).reshape(2, Bn, S, D_RNN)
                       + bx.astype(jnp.float32)[:, None, None, :])
    log_a = -RG_C * jax.nn.softplus(-lam.astype(jnp.float32))[:, None, None, :] * r
    a = jnp.exp(log_a)
    b = jnp.sqrt(-jnp.expm1(2.0 * log_a)) * (i * u[None])
    h_fwd = lax.associative_scan(_lin_combine, (a[0], b[0]), axis=1)[1]
    h_bwd = lax.associative_scan(_lin_combine, (a[1], b[1]), axis=1, reverse=True)[1]
    y = (h_fwd + h_bwd) * jax.nn.gelu(ry.astype(jnp.float32))
    return y.astype(rx.dtype)


def chunk_gated_delta(q, k, v, g, beta):
    N, L, H, dk = q.shape
    dv = v.shape[-1]
    nc = L // CHUNK

    def to_chunks(t):
        return jnp.moveaxis(t.reshape(N, nc, CHUNK, H, *t.shape[3:]), 3, 1)

    q, k, v, g, beta = (to_chunks(t) for t in (q, k, v, g, beta))
    gc = jnp.cumsum(g, axis=-1)
    idx = jnp.arange(CHUNK)
    incl = idx[:, None] >= idx[None, :]
    strict = idx[:, None] > idx[None, :]
    diff = gc[..., :, None] - gc[..., None, :]
    decay = jnp.exp(jnp.where(incl, diff, -jnp.inf))
    kb = k * beta[..., None]
    A = jnp.where(strict, jnp.einsum("nhcid,nhcjd->nhcij", kb, k) * decay, 0.0)
    lhs = A + jnp.eye(CHUNK, dtype=A.dtype)
    u = lax.linalg.triangular_solve(lhs, v * beta[..., None], left_side=True, lower=True, unit_diagonal=True)
    w = lax.linalg.triangular_solve(lhs, kb * jnp.exp(gc)[..., None], left_side=True, lower=True, unit_diagonal=True)
    att = jnp.einsum("nhcid,nhcjd->nhcij", q, k) * decay
    qg = q * jnp.exp(gc)[..., None]
    g_last = gc[..., -1]
    kd = k * jnp.exp(g_last[..., None] - gc)[..., None]

    def step(S, inp):
        qg_c, kd_c, u_c, w_c, att_c, gl_c = inp
        v_new = u_c - jnp.einsum("nhck,nhkv->nhcv", w_c, S)
        o = jnp.einsum("nhck,nhkv->nhcv", qg_c, S) + jnp.einsum("nhij,nhjv->nhiv", att_c, v_new)
        S = S * jnp.exp(gl_c)[..., None, None] + jnp.einsum("nhck,nhcv->nhkv", kd_c, v_new)
        return S, o

    xs = tuple(jnp.moveaxis(t, 2, 0) for t in (qg, kd, u, w, att, g_last))
    S0 = jnp.zeros((N, H, dk, dv), jnp.float32)
    _, o = lax.scan(step, S0, xs)
    return jnp.transpose(o, (1, 0, 3, 2, 4)).reshape(N, L, H, dv)


def deltanet_branch(qkv, z, ba, conv_w, a_log, dt_bias, norm_g):
    Bn, S, _ = qkv.shape
    c = jax.nn.silu(dwconv_centred(qkv, conv_w)).astype(jnp.float32)
    q, k, v = jnp.split(c, [DN_QK, 2 * DN_QK], axis=-1)
    q = q.reshape(Bn, S, DN_HEADS, DN_DK)
    k = k.reshape(Bn, S, DN_HEADS, DN_DK)
    v = v.reshape(Bn, S, DN_HEADS, DN_DV)
    q = q * lax.rsqrt(jnp.sum(q * q, -1, keepdims=True) + EPS) * (DN_DK ** -0.5)
    k = k * lax.rsqrt(jnp.sum(k * k, -1, keepdims=True) + EPS)
    bal = ba.astype(jnp.float32).reshape(Bn, S, 2, 2, DN_HEADS)
    beta = jax.nn.sigmoid(bal[:, :, 0])
    g = -jnp.exp(a_log.astype(jnp.float32)) * jax.nn.softplus(bal[:, :, 1] + dt_bias.astype(jnp.float32))
    def both(t_f, t_b):
        return jnp.concatenate([t_f, jnp.flip(t_b, 1)], axis=0)
    o = chunk_gated_delta(both(q, q), both(k, k), both(v, v),
                          both(g[:, :, 0], g[:, :, 1]), both(beta[:, :, 0], beta[:, :, 1]))
    o = o[:Bn] + jnp.flip(o[Bn:], 1)
    o = o * lax.rsqrt(jnp.mean(o * o, -1, keepdims=True) + EPS) * norm_g.astype(jnp.float32)
    o = o * jax.nn.silu(z.astype(jnp.float32).reshape(Bn, S, DN_HEADS, DN_DV))
    return o.reshape(Bn, S, DN_V).astype(qkv.dtype)


def setup_inputs(seed: int = 0) -> dict:
    key = jax.random.key(seed)
    ks = jax.random.split(key, 24)
    f32 = jnp.float32
    nrm = lambda k, shape, s: jax.random.normal(k, shape, f32) * s
    lam_u = jax.random.uniform(ks[9], (DEPTH, 2, D_RNN), f32, 0.9, 0.999) ** (1.0 / RG_C)
    dt = jnp.exp(jax.random.uniform(ks[14], (DEPTH, 2, DN_HEADS), f32, math.log(1e-3), math.log(1e-1)))
    return {
        "x": jax.random.normal(ks[0], (BATCH, SEQ, D_MODEL), f32),
        "mix_norm": 1.0 + nrm(ks[1], (DEPTH, D_MODEL), 0.02),
        "w_in": nrm(ks[2], (DEPTH, D_MODEL, N_IN), D_MODEL ** -0.5),
        "rg_conv_w": nrm(ks[3], (DEPTH, CONV_K, D_RNN), CONV_K ** -0.5),
        "rg_conv_b": nrm(ks[4], (DEPTH, D_RNN), 0.01),
        "rg_wa": nrm(ks[5], (DEPTH, 2, RNN_BLOCKS, RNN_BLOCK, RNN_BLOCK), RNN_BLOCK ** -0.5),
        "rg_ba": nrm(ks[6], (DEPTH, 2, D_RNN), 0.01),
        "rg_wx": nrm(ks[7], (DEPTH, 2, RNN_BLOCKS, RNN_BLOCK, RNN_BLOCK), RNN_BLOCK ** -0.5),
        "rg_bx": nrm(ks[8], (DEPTH, 2, D_RNN), 0.01),
        "rg_lambda": jnp.log(lam_u) - jnp.log1p(-lam_u),
        "w_rnn_proj": nrm(ks[10], (DEPTH, D_RNN, D_MODEL), D_RNN ** -0.5),
        "dn_conv_w": nrm(ks[11], (DEPTH, CONV_K, DN_QKV), CONV_K ** -0.5),
        "dn_a_log": jnp.log(jax.random.uniform(ks[12], (DEPTH, 2, DN_HEADS), f32, 1.0, 16.0)),
        "dn_dt_bias": jnp.log(jnp.expm1(dt)),
        "dn_norm": 1.0 + nrm(ks[13], (DEPTH, DN_DV), 0.02),
        "w_dn_proj": nrm(ks[15], (DEPTH, DN_V, D_MODEL), DN_V ** -0.5),
        "w_out": nrm(ks[16], (DEPTH, D_MODEL, D_MODEL), D_MODEL ** -0.5),
        "ffn_norm": 1.0 + nrm(ks[17], (DEPTH, D_MODEL), 0.02),
        "w_gate_up": nrm(ks[18], (DEPTH, D_MODEL, 2 * D_FF), D_MODEL ** -0.5),
        "w_down": nrm(ks[19], (DEPTH, D_FF, D_MODEL), D_FF ** -0.5),
        "final_norm": 1.0 + nrm(ks[20], (D_MODEL,), 0.02),
    }


def reference(x, mix_norm, w_in, rg_conv_w, rg_conv_b, rg_wa, rg_ba, rg_wx, rg_bx, rg_lambda,
              w_rnn_proj, dn_conv_w, dn_a_log, dn_dt_bias, dn_norm, w_dn_proj, w_out,
              ffn_norm, w_gate_up, w_down, final_norm):
    for l in range(DEPTH):
        h = rmsnorm(x, mix_norm[l])
        p = h @ w_in[l]
        rx, ry, qkv, z, ba, gates = jnp.split(p, SPLIT_IDX, axis=-1)
        y_rnn = rglru_branch(rx, ry, rg_conv_w[l], rg_conv_b[l], rg_wa[l], rg_ba[l],
                             rg_wx[l], rg_bx[l], rg_lambda[l]) @ w_rnn_proj[l]
        y_dn = deltanet_branch(qkv, z, ba, dn_conv_w[l], dn_a_log[l], dn_dt_bias[l],
                               dn_norm[l]) @ w_dn_proj[l]
        g_rnn, g_dn = jnp.split(jax.nn.sigmoid(gates), 2, axis=-1)
        x = x + (g_rnn * y_rnn + g_dn * y_dn) @ w_out[l]
        h = rmsnorm(x, ffn_norm[l])
        gu = h @ w_gate_up[l]
        gt, up = jnp.split(gu, 2, axis=-1)
        x = x + (jax.nn.silu(gt) * up) @ w_down[l]
    return rmsnorm(x, final_norm)
```

```python
import numpy as np
import concourse.bass as bass
import concourse.mybir as mybir
from concourse.bass_utils import run_bass_kernel_spmd

F32 = mybir.dt.float32
BF16 = mybir.dt.bfloat16
ALU = mybir.AluOpType
AF = mybir.ActivationFunctionType

NCORES = 8
D = 1024
SEQ = 2048
NLAYER = 4
DRNN = 1536
DFF = 2816
NIN = 9248
C_RY = 1536
C_QKV = 3072
C_Z = 6144
C_BA = 7168
C_GR = 7200
C_GD = 8224
EPS = 1e-6
NEG = -30000.0


class T:
    __slots__ = ("ap", "w", "r", "ps")

    def __init__(self, ap=None, ps=False):
        self.ap = ap
        self.w = None
        self.r = {}
        self.ps = ps


class _Rec:
    def __init__(self):
        self.call = None

    def __getattr__(self, name):
        def f(*a, **k):
            self.call = (name, a, k)
            return self
        return f


class Eng:
    def __init__(self, name, sem):
        self.name = name
        self.sem = sem
        self.key = ("e", name)
        self.count = 0
        self.waited = {}
        self.ops = []


class Prog:
    NDMASEM = 24

    def __init__(self, nc):
        self.nc = nc
        self.sems = {}
        self.E = {}
        for n in ("pe", "dve", "act", "pool", "sp"):
            s = nc.alloc_semaphore(name="s_" + n)
            e = Eng(n, s)
            self.E[n] = e
            self.sems[e.key] = s
        self.dsem = []
        for i in range(self.NDMASEM):
            s = nc.alloc_semaphore(name="d%d" % i)
            self.dsem.append([s, 0])
            self.sems[("d", i)] = s
        self.dnext = 0
        self.out_tokens = []
        self.ninst = 0

    def _deps(self, reads, writes):
        need = {}
        for t in reads:
            if t.w is not None:
                k, v = t.w
                if need.get(k, 0) < v:
                    need[k] = v
        for t in writes:
            if t.w is not None:
                k, v = t.w
                if need.get(k, 0) < v:
                    need[k] = v
            for k, v in t.r.items():
                if need.get(k, 0) < v:
                    need[k] = v
        return need

    def _emit_waits(self, e, need, skip_self=False):
        for k, v in need.items():
            if skip_self and k == e.key:
                continue
            if e.waited.get(k, 0) < v:
                e.waited[k] = v
                e.ops.append(("w", k, v))

    def _mark(self, tok, reads, writes):
        k, v = tok
        for t in reads:
            if t.r.get(k, 0) < v:
                t.r[k] = v
        for t in writes:
            t.w = tok
            t.r = {}

    def op(self, eng, fn, reads=(), writes=()):
        e = self.E[eng]
        psr = [t for t in reads if t.ps]
        if psr:
            reads = [t for t in reads if not t.ps]
            writes = list(writes) + psr
        need = self._deps(reads, writes)
        self._emit_waits(e, need, skip_self=(eng == "pe"))
        e.count += 1
        tok = (e.key, e.count)
        rec = _Rec()
        fn(rec)
        e.ops.append(("i", rec.call))
        self._mark(tok, reads, writes)
        self.ninst += 1
        return tok

    def dma(self, q, out_ap, in_ap, reads=(), writes=(), is_output=False):
        e = self.E[q]
        i = self.dnext
        self.dnext = (self.dnext + 1) % self.NDMASEM
        ds = self.dsem[i]
        need = self._deps(reads, writes)
        k = ("d", i)
        if ds[1] > 0 and need.get(k, 0) < ds[1]:
            need[k] = ds[1]
        self._emit_waits(e, need)
        ds[1] += 16
        tok = (k, ds[1])
        e.ops.append(("d", out_ap, in_ap, k))
        self._mark(tok, reads, writes)
        if is_output:
            self.out_tokens.append(tok)
        self.ninst += 1
        return tok

    def alias(self, new_tiles, old_tiles):
        acc = {}
        for t in old_tiles:
            if t.w is not None:
                k, v = t.w
                if acc.get(k, 0) < v:
                    acc[k] = v
            for k, v in t.r.items():
                if acc.get(k, 0) < v:
                    acc[k] = v
        for t in new_tiles:
            t.w = None
            t.r = dict(acc)

    def finish(self):
        e = self.E["sp"]
        need = {}
        for k, v in self.out_tokens:
            if need.get(k, 0) < v:
                need[k] = v
        self._emit_waits(e, need)
        nc = self.nc
        sems = self.sems

        def run(e, eng):
            for o in e.ops:
                if o[0] == "w":
                    eng.wait_ge(sems[o[1]], o[2])
                elif o[0] == "i":
                    name, a, k = o[1]
                    getattr(eng, name)(*a, **k).then_inc(e.sem, 1)
                else:
                    eng.dma_start(out=o[1], in_=o[2]).then_inc(sems[o[3]], 16)

        with nc.Block() as block:
            @block.tensor
            def _(eng):
                run(self.E["pe"], eng)

            @block.vector
            def _(eng):
                run(self.E["dve"], eng)

            @block.scalar
            def _(eng):
                run(self.E["act"], eng)

            @block.gpsimd
            def _(eng):
                run(self.E["pool"], eng)

            @block.sync
            def _(eng):
                run(self.E["sp"], eng)


CONST_NAMES = ["ident", "ones", "Lf", "Lb", "SUf", "SUb", "nsf", "nsb", "inf", "inb"]


def make_consts():
    t = np.arange(128)[:, None]
    i = np.arange(128)[None, :]
    c = {
        "ident": (t == i),
        "ones": np.ones((128, 128)),
        "Lf": (t <= i),
        "Lb": (t >= i),
        "SUf": (t > i),
        "SUb": (t < i),
        "nsf": -1.0 * (i > t),
        "nsb": -1.0 * (i < t),
        "inf": (i >= t),
        "inb": (i <= t),
    }
    return np.ascontiguousarray(np.stack([np.asarray(c[n], dtype=np.float32) for n in CONST_NAMES], axis=1))


class _Stop(Exception):
    pass


def build(nc, NSEQ=4, NL=NLAYER, dbg=None, stop_after=None, heads=8, rg_blocks=12):
    P = Prog(nc)
    try:
        _build(P, nc, NSEQ, NL, dbg, stop_after, heads, rg_blocks)
    except _Stop:
        pass
    P.finish()
    return P, None


def _build(P, nc, NSEQ, NL, dbg, stop_after, heads, rg_blocks):
    dbg = dbg or []
    dbg_out = {}

    def din(name, shape, dt=F32):
        return nc.dram_tensor(name, shape, dt, kind="ExternalInput").ap()

    xT = din("xT", [NSEQ, 128, 8, SEQ])
    outT = nc.dram_tensor("outT", [NSEQ, 128, 8, SEQ], F32, kind="ExternalOutput").ap()
    win = din("win", [NLAYER, 128, 8, NIN])
    gatew = din("gatew", [NLAYER, 12, 128, 4, 128])
    rgv_d = din("rgv", [128, NLAYER * 12, 11])
    wrnn = din("wrnn", [NLAYER, 128, 12, D])
    dncw_d = din("dncw", [128, NLAYER * 24, 4])
    dnv_d = din("dnv", [128, NLAYER, 2, 16])
    dnnorm_d = din("dnnorm", [128, NLAYER])
    wdn = din("wdn", [NLAYER, 128, 8, D])
    wout = din("wout", [NLAYER, 128, 8, D])
    norms_d = din("norms", [128, NLAYER * 2 * 8])
    fnorm_d = din("fnorm", [128, 8])
    wgu = din("wgu", [NLAYER, 128, 8, 2 * DFF])
    wdown = din("wdown", [NLAYER, 128, 22, D])
    consts_d = din("consts", [128, len(CONST_NAMES), 128])
    xspill = nc.dram_tensor("xspill", [128, 8, SEQ], F32, kind="Internal").ap()
    mixspill = nc.dram_tensor("mixspill", [8, 128, SEQ], BF16, kind="Internal").ap()
    XSt = [T() for _ in range(8)]
    MSt = [T() for _ in range(8)]

    def sb(name, shape, dt=F32):
        return nc.alloc_sbuf_tensor("s_" + name, shape, dt).ap()

    def dbg_dump(name, ap, tiles, shape, dt=F32):
        if name not in dbg:
            return
        o = nc.dram_tensor("dbg_" + name, shape, dt, kind="ExternalOutput").ap()
        P.dma("sp", o, ap, reads=tiles, is_output=True)
        dbg_out[name] = o

    RX = sb("RX", [128, 16384], F32)
    RH = sb("RH", [128, 8, SEQ], BF16)
    RY = sb("RY", [128, 24576], BF16)
    WP = [sb("WP%d" % i, [128, 2048], BF16) for i in range(3)]
    WPt = [T() for _ in range(3)]
    GW = [sb("GW%d" % i, [128, 4, 128], BF16) for i in range(2)]
    GWt = [T() for _ in range(2)]
    wp_i = [0]
    gw_i = [0]
    CM = sb("CM", [128, len(CONST_NAMES), 128], F32)
    CMt = T()
    cm = {n: CM[:, i, :] for i, n in enumerate(CONST_NAMES)}
    ones_bf = sb("ones_bf", [128, 128], BF16)
    ident_bf = sb("ident_bf", [128, 128], BF16)
    CBt = T()
    mskA = sb("mskA", [128, 2, 128], F32)
    mskB = sb("mskB", [128, 2, 128], F32)
    rgv = sb("rgv", [128, NLAYER * 12, 11], F32)
    rgd = sb("rgd", [128, NLAYER * 12, 10], F32)
    rgtmp = sb("rgtmp", [128, NLAYER * 12, 2], F32)
    dncw = sb("dncw", [128, NLAYER * 24, 4], F32)
    dnv = sb("dnv", [128, NLAYER, 2, 16], F32)
    nega = sb("nega", [128, NLAYER, 16], F32)
    dnnorm = sb("dnnorm", [128, NLAYER], F32)
    norms = sb("norms", [128, NLAYER * 2 * 8], F32)
    fnorm = sb("fnorm", [128, 8], F32)
    VEC = T()
    QGbuf = sb("QGbuf", [128, 4096], BF16)
    SMbuf = sb("SMbuf", [128, 512 + 8 * 256], F32)
    YQ = [[sb("YQ%d_%d" % (sl, j), [128, 2, 256], F32) for j in range(2)] for sl in range(2)]
    YTb = [[sb("YT%d_%d" % (sl, j), [128, 2, 128], F32) for j in range(2)] for sl in range(2)]
    KQA = [sb("KQA%d" % sl, [128, 2, 128], BF16) for sl in range(2)]
    KQB = [sb("KQB%d" % sl, [128, 2, 128], BF16) for sl in range(2)]
    YQt = [[T() for _ in range(2)] for _ in range(2)]
    YTt = [[T() for _ in range(2)] for _ in range(2)]
    KQAt = [T(), T()]
    KQBt = [T(), T()]

    PS = nc.alloc_psum_tensor("PS", [128, 4096], F32).ap()
    BK = [T(ps=True) for _ in range(8)]

    def bank(b, n=1):
        return PS[:, b * 512:(b + n) * 512]

    P.dma("sp", CM, consts_d, writes=[CMt])
    for dst, src in ((rgv, rgv_d), (dncw, dncw_d), (dnv, dnv_d), (dnnorm, dnnorm_d), (norms, norms_d), (fnorm, fnorm_d)):
        P.dma("sp", dst, src, writes=[VEC])
    P.op("dve", lambda e: e.tensor_copy(out=ones_bf, in_=cm["ones"]), reads=[CMt], writes=[CBt])
    P.op("dve", lambda e: e.tensor_copy(out=ident_bf, in_=cm["ident"]), reads=[CMt], writes=[CBt])
    P.op("dve", lambda e: e.tensor_copy(out=mskA[:, 0, :], in_=cm["nsf"]), reads=[CMt], writes=[CBt])
    P.op("dve", lambda e: e.tensor_copy(out=mskA[:, 1, :], in_=cm["inf"]), reads=[CMt], writes=[CBt])
    P.op("dve", lambda e: e.tensor_copy(out=mskB[:, 0, :], in_=cm["nsb"]), reads=[CMt], writes=[CBt])
    P.op("dve", lambda e: e.tensor_copy(out=mskB[:, 1, :], in_=cm["inb"]), reads=[CMt], writes=[CBt])
    P.op("act", lambda e: e.mul(out=rgd[:, :, 0:4], in_=rgv[:, :, 5:9], mul=0.5), reads=[VEC], writes=[VEC])
    P.op("act", lambda e: e.activation(out=rgtmp, in_=rgv[:, :, 9:11], func=AF.Exp, scale=-1.0), reads=[VEC], writes=[VEC])
    P.op("act", lambda e: e.activation(out=rgtmp, in_=rgtmp, func=AF.Ln, bias=1.0), reads=[VEC], writes=[VEC])
    P.op("act", lambda e: e.mul(out=rgd[:, :, 4:6], in_=rgtmp, mul=-4.0), reads=[VEC], writes=[VEC])
    P.op("act", lambda e: e.mul(out=rgd[:, :, 6:8], in_=rgtmp, mul=-8.0), reads=[VEC], writes=[VEC])
    P.op("act", lambda e: e.mul(out=rgd[:, :, 8:10], in_=rgtmp, mul=4.0), reads=[VEC], writes=[VEC])
    P.op("act", lambda e: e.activation(out=nega, in_=dnv[:, :, 0, :], func=AF.Exp), reads=[VEC], writes=[VEC])
    P.op("dve", lambda e: e.tensor_scalar(out=nega, in0=nega, scalar1=-1.0, scalar2=None, op0=ALU.mult), reads=[VEC], writes=[VEC])

    def load_w(src_ap, shape):
        i = wp_i[0]
        wp_i[0] = (i + 1) % len(WP)
        n = int(np.prod(shape[1:]))
        view = WP[i][:, 0:n].rearrange("p (a b) -> p a b", a=shape[1])
        P.dma("pool", view, src_ap, writes=[WPt[i]])
        return view, WPt[i]

    def proj(pb, ptiles, wv, wt, rhs_fn, rhs_tiles, nk, ncols=SEQ):
        nt = ncols // 512
        for tt in range(nt):
            for k in range(nk):
                P.op("pe", lambda e, tt=tt, k=k: e.matmul(pb[:, tt * 512:(tt + 1) * 512], lhsT=wv[:, k, :], rhs=rhs_fn(k, tt),
                                                            start=(k == 0), stop=(k == nk - 1)),
                     reads=[wt] + rhs_tiles, writes=[ptiles[tt]])

    Xv = RX.rearrange("p (c t) -> p c t", c=8)
    Xt = [T() for _ in range(8)]
    Ht = [T() for _ in range(8)]
    RYt = [T() for _ in range(12)]
    RYb = RY.rearrange("p (c t) -> p c t", c=12)
    RYf = RY.bitcast(F32)
    hfn = lambda k, tt: RH[:, k, tt * 512:(tt + 1) * 512]
    pgrp = [0]

    def grp():
        g_ = pgrp[0]
        pgrp[0] ^= 1
        return bank(4 * g_, 4), BK[4 * g_:4 * g_ + 4]

    def rms_stats(scale):
        tmp_t = [T() for _ in range(10)]
        P.alias(tmp_t, RYt)
        lnv = RYf[:, 8192:10240]
        rstd = RYf[:, 10240:12288]
        for c in range(8):
            P.op("act", lambda e, c=c: e.activation(out=RYb[:, c, :], in_=Xv[:, c, :], func=AF.Square), reads=[Xt[c]], writes=[tmp_t[c]])
        for tt in range(4):
            for c in range(8):
                P.op("pe", lambda e, c=c, tt=tt: e.matmul(bank(tt), lhsT=ones_bf, rhs=RYb[:, c, tt * 512:(tt + 1) * 512],
                                                            start=(c == 0), stop=(c == 7)),
                     reads=[CBt, tmp_t[c]], writes=[BK[tt]])
        P.op("act", lambda e: e.activation(out=lnv, in_=bank(0, 4), func=AF.Ln, bias=EPS, scale=scale), reads=BK[0:4], writes=[tmp_t[8]])
        P.op("act", lambda e: e.activation(out=rstd, in_=lnv, func=AF.Exp, scale=-0.5), reads=[tmp_t[8]], writes=[tmp_t[9]])
        return rstd, tmp_t

    rxp = RX[:, 0:2052]
    F0 = RX[:, 2052:4100]
    F1 = RX[:, 4100:6148]
    F2 = RX[:, 6148:8196]
    HF = RX[:, 8196:10244]
    HB = RX[:, 10244:12292]
    ub = RX[:, 12292:13316].bitcast(BF16)
    thi = RX[:, 13316:14340].bitcast(BF16)
    sqv = RX[:, 14340:15364].bitcast(BF16)
    QK = RX[:, 8196:10244].bitcast(BF16).rearrange("p (c two d) -> p c two d", c=16, two=2)
    QKf = RX[:, 8196:10244].bitcast(BF16).rearrange("p (c n) -> p c n", c=16)
    ktok = RX[:, 10244:11268].bitcast(BF16).rearrange("p (c d) -> p c d", c=16)
    vtok = RX[:, 11268:12292].bitcast(BF16).rearrange("p (c d) -> p c d", c=16)
    KDB = RX[:, 12292:14340].bitcast(BF16).rearrange("p (two c d) -> p two c d", two=2, c=16)
    tb = 14340
    S32 = [RX[:, tb:tb + 128], RX[:, tb + 128:tb + 256]]
    Sbf = [RX[:, tb + 256:tb + 320].bitcast(BF16), RX[:, tb + 320:tb + 384].bitcast(BF16)]
    rbf = [RX[:, tb + 384:tb + 448].bitcast(BF16), RX[:, tb + 448:tb + 512].bitcast(BF16)]
    xbf = [RX[:, tb + 512:tb + 576].bitcast(BF16), RX[:, tb + 576:tb + 640].bitcast(BF16)]
    Dg = [RX[:, tb + 640:tb + 896].rearrange("p (a b) -> p a b", a=2), RX[:, tb + 896:tb + 1152].rearrange("p (a b) -> p a b", a=2)]
    D2 = [RX[:, tb + 1152:tb + 1280].bitcast(BF16).rearrange("p (a b) -> p a b", a=2),
          RX[:, tb + 1280:tb + 1408].bitcast(BF16).rearrange("p (a b) -> p a b", a=2)]
    Dgq = [RX[:, tb + 1408:tb + 1536], RX[:, tb + 1536:tb + 1664]]
    QG = QGbuf.rearrange("p (two c d) -> p two c d", two=2, c=16)
    RYtail = RY[:, 16384:24576]
    qsb, ksb, vsb, sqb = (RYtail[:, 0:2048], RYtail[:, 2048:4096], RYtail[:, 4096:6144], RYtail[:, 6144:8192])
    T2T = RYtail[:, 0:4096].rearrange("p (c two d) -> p c two d", c=16, two=2)
    ATT = RYtail[:, 4096:8192].rearrange("p (c two d) -> p c two d", c=16, two=2)
    OT = RX[:, 0:2048]
    bt = SMbuf[:, 0:512].rearrange("p (c k) -> p c k", c=16)
    smv = lambda i: SMbuf[:, 512 + 256 * i:512 + 256 * (i + 1)].rearrange("p (c k) -> p c k", c=16)
    lnb, gg, gc, colE, colC, colD, egl, stmp = [smv(i) for i in range(8)]

    for s in range(NSEQ):
        for c in range(8):
            P.dma("sp", Xv[:, c, :], xT[s, :, c, :], writes=[Xt[c]])

        for l in range(NL):
            rstd, tmp_t = rms_stats(1.0 / D)
            for c in range(8):
                g = norms[:, (l * 2 + 0) * 8 + c:(l * 2 + 0) * 8 + c + 1]
                P.op("dve", lambda e, c=c, g=g: e.scalar_tensor_tensor(out=RH[:, c, :], in0=Xv[:, c, :], scalar=g, in1=rstd,
                                                                         op0=ALU.mult, op1=ALU.mult),
                     reads=[Xt[c], tmp_t[9], VEC], writes=[Ht[c]])
            P.alias(RYt, tmp_t)
            if s == 0 and l == 0:
                dbg_dump("h", RH, Ht, [128, 8, SEQ], BF16)
            for c in range(8):
                P.dma("sp", xspill[:, c, :], Xv[:, c, :], reads=[Xt[c]], writes=[XSt[c]])
            W = {k_: T() for k_ in ["rxp", "F0", "F1", "F2", "HF", "HB", "ub", "thi", "sq"]}
            P.alias(list(W.values()), Xt)
            P.op("pool", lambda e: e.memset(rxp[:, 0:2], 0.0), writes=[W["rxp"]])
            P.op("pool", lambda e: e.memset(rxp[:, 2050:2052], 0.0), writes=[W["rxp"]])

            for n in range(rg_blocks):
                ln_ = l * 12 + n
                wv, wt = load_w(win[l, :, :, n * 128:(n + 1) * 128], [128, 8, 128])
                gi = gw_i[0]
                gw_i[0] ^= 1
                P.dma("pool", GW[gi], gatew[l, n], writes=[GWt[gi]])
                pb, pt = grp()
                proj(pb, pt, wv, wt, hfn, Ht, 8)
                P.op("act", lambda e, pb=pb: e.copy(out=rxp[:, 2:2050], in_=pb), reads=pt, writes=[W["rxp"]])
                cw = lambda j, ln_=ln_: rgv[:, ln_, j:j + 1]
                P.op("dve", lambda e, cw=cw: e.tensor_scalar(out=F0, in0=rxp[:, 0:2048], scalar1=cw(0), scalar2=cw(4), op0=ALU.mult, op1=ALU.add),
                     reads=[W["rxp"], VEC], writes=[W["F0"]])
                P.op("dve", lambda e, cw=cw: e.scalar_tensor_tensor(out=F1, in0=rxp[:, 1:2049], scalar=cw(1), in1=F0, op0=ALU.mult, op1=ALU.add),
                     reads=[W["rxp"], VEC, W["F0"]], writes=[W["F1"]])
                P.op("dve", lambda e, cw=cw: e.scalar_tensor_tensor(out=F0, in0=rxp[:, 2:2050], scalar=cw(2), in1=F1, op0=ALU.mult, op1=ALU.add),
                     reads=[W["rxp"], VEC, W["F1"]], writes=[W["F0"]])
                P.op("dve", lambda e, cw=cw: e.scalar_tensor_tensor(out=ub, in0=rxp[:, 3:2051], scalar=cw(3), in1=F0, op0=ALU.mult, op1=ALU.add),
                     reads=[W["rxp"], VEC, W["F0"]], writes=[W["ub"]])
                ufn = lambda k, tt: ub[:, tt * 512:(tt + 1) * 512]
                gv = GW[gi]
                for d in range(2):
                    pb, pt = grp()
                    proj(pb, pt, gv[:, d:d + 1, :], GWt[gi], ufn, [W["ub"]], 1)
                    P.op("act", lambda e, pb=pb, d=d, ln_=ln_: e.activation(out=F0, in_=pb, func=AF.Tanh, scale=0.5, bias=rgd[:, ln_, d:d + 1]),
                         reads=pt + [VEC], writes=[W["F0"]])
                    pb, pt = grp()
                    proj(pb, pt, gv[:, 2 + d:3 + d, :], GWt[gi], ufn, [W["ub"]], 1)
                    P.op("act", lambda e, pb=pb, d=d, ln_=ln_: e.activation(out=thi, in_=pb, func=AF.Tanh, scale=0.5, bias=rgd[:, ln_, 2 + d:3 + d]),
                         reads=pt + [VEC], writes=[W["thi"]])
                    P.op("act", lambda e, d=d, ln_=ln_: e.activation(out=F1, in_=F0, func=AF.Exp, scale=rgd[:, ln_, 4 + d:5 + d], bias=rgd[:, ln_, 4 + d:5 + d]),
                         reads=[W["F0"], VEC], writes=[W["F1"]])
                    P.op("act", lambda e, d=d, ln_=ln_: e.activation(out=F2, in_=F0, func=AF.Tanh, scale=rgd[:, ln_, 8 + d:9 + d], bias=rgd[:, ln_, 8 + d:9 + d]),
                         reads=[W["F0"], VEC], writes=[W["F2"]])
                    P.op("act", lambda e, d=d, ln_=ln_: e.activation(out=F0, in_=F0, func=AF.Exp, scale=rgd[:, ln_, 6 + d:7 + d], bias=rgd[:, ln_, 6 + d:7 + d]),
                         reads=[W["F0"], VEC], writes=[W["F0"]])
                    P.op("dve", lambda e: e.scalar_tensor_tensor(out=F0, in0=F0, scalar=1.0, in1=F2, op0=ALU.add, op1=ALU.mult),
                         reads=[W["F0"], W["F2"]], writes=[W["F0"]])
                    P.op("act", lambda e: e.activation(out=F0, in_=F0, func=AF.Ln), reads=[W["F0"]], writes=[W["F0"]])
                    P.op("act", lambda e: e.activation(out=sqv, in_=F0, func=AF.Exp, scale=0.5), reads=[W["F0"]], writes=[W["sq"]])
                    P.op("dve", lambda e: e.scalar_tensor_tensor(out=F2, in0=thi, scalar=1.0, in1=ub, op0=ALU.add, op1=ALU.mult),
                         reads=[W["thi"], W["ub"]], writes=[W["F2"]])
                    P.op("dve", lambda e: e.scalar_tensor_tensor(out=F2, in0=F2, scalar=0.5, in1=sqv, op0=ALU.mult, op1=ALU.mult),
                         reads=[W["F2"], W["sq"]], writes=[W["F2"]])
                    if d == 0:
                        P.op("dve", lambda e: e.tensor_tensor_scan(out=HF, data0=F1, data1=F2, initial=0.0, op0=ALU.mult, op1=ALU.add),
                             reads=[W["F1"], W["F2"]], writes=[W["HF"]])
                    else:
                        P.op("dve", lambda e: e.tensor_tensor_scan(out=HB[:, ::-1], data0=F1[:, ::-1], data1=F2[:, ::-1], initial=0.0,
                                                                     op0=ALU.mult, op1=ALU.add),
                             reads=[W["F1"], W["F2"]], writes=[W["HB"]])
                wv, wt = load_w(win[l, :, :, C_RY + n * 128:C_RY + (n + 1) * 128], [128, 8, 128])
                pb, pt = grp()
                proj(pb, pt, wv, wt, hfn, Ht, 8)
                P.op("act", lambda e, pb=pb: e.activation(out=F0, in_=pb, func=AF.Square), reads=pt, writes=[W["F0"]])
                P.op("dve", lambda e: e.tensor_scalar(out=F0, in0=F0, scalar1=0.044715, scalar2=1.0, op0=ALU.mult, op1=ALU.add),
                     reads=[W["F0"]], writes=[W["F0"]])
                P.op("dve", lambda e, pb=pb: e.tensor_tensor(out=F0, in0=F0, in1=pb, op=ALU.mult), reads=[W["F0"]] + pt, writes=[W["F0"]])
                P.op("act", lambda e: e.activation(out=F1, in_=F0, func=AF.Tanh, scale=0.7978845608028654), reads=[W["F0"]], writes=[W["F1"]])
                P.op("dve", lambda e: e.tensor_tensor(out=F2, in0=HF, in1=HB, op=ALU.add), reads=[W["HF"], W["HB"]], writes=[W["F2"]])
                P.op("dve", lambda e, pb=pb: e.scalar_tensor_tensor(out=F0, in0=F1, scalar=1.0, in1=pb, op0=ALU.add, op1=ALU.mult),
                     reads=[W["F1"]] + pt, writes=[W["F0"]])
                P.op("dve", lambda e, n=n: e.scalar_tensor_tensor(out=RYb[:, n, :], in0=F0, scalar=0.5, in1=F2, op0=ALU.mult, op1=ALU.mult),
                     reads=[W["F0"], W["F2"]], writes=[RYt[n]])
            if s == 0 and l == 0:
                dbg_dump("yr", RYb, RYt, [128, 12, SEQ], BF16)
            if stop_after == "rg":
                raise _Stop()

            stg = [RX[:, 8196:9220].bitcast(BF16), RX[:, 9220:10244].bitcast(BF16)]
            stg_t = [T(), T()]
            P.alias(stg_t, [W["HF"]])
            yfn = lambda k, tt: RYb[:, k, tt * 512:(tt + 1) * 512]
            for n in range(8):
                wv, wt = load_w(wrnn[l, :, :, n * 128:(n + 1) * 128], [128, 12, 128])
                pb, pt = grp()
                proj(pb, pt, wv, wt, yfn, RYt, 12)
                wv2, wt2 = load_w(win[l, :, :, C_GR + n * 128:C_GR + (n + 1) * 128], [128, 8, 128])
                pb2, pt2 = grp()
                proj(pb2, pt2, wv2, wt2, hfn, Ht, 8)
                P.op("act", lambda e, pb2=pb2: e.activation(out=F0, in_=pb2, func=AF.Tanh, scale=0.5), reads=pt2, writes=[W["F0"]])
                si = n % 2
                P.op("dve", lambda e, pb=pb, si=si: e.scalar_tensor_tensor(out=stg[si], in0=F0, scalar=1.0, in1=pb, op0=ALU.add, op1=ALU.mult),
                     reads=[W["F0"]] + pt, writes=[stg_t[si]])
                P.dma("sp", mixspill[n], stg[si], reads=[stg_t[si]], writes=[MSt[n]])

            DN = {nm: T() for nm in ["QK", "ktok", "vtok", "KDB", "S0", "S1", "Sb0", "Sb1", "r0", "r1", "x0", "x1",
                                     "Dg0", "Dg1", "D20", "D21", "Dgq0", "Dgq1"]}
            P.alias(list(DN.values()), [W["HF"], W["HB"], W["ub"], W["thi"], W["sq"]] + stg_t)
            SM = {nm: T() for nm in ["bt", "lnb", "g", "gc", "colE", "colC", "colD", "egl", "tmp"]}
            DR = {nm: T() for nm in ["qs", "ks", "vs", "sqb"]}
            P.alias(list(DR.values()), RYt[8:12])
            QGt = [T(), T()]
            T2Tt = [T() for _ in range(16)]
            ATTt = [T() for _ in range(16)]
            OTt = [T() for _ in range(16)]
            wv, wt = load_w(win[l, :, :, C_BA:C_BA + 32], [128, 8, 32])
            pbt = bank(0).rearrange("p (c k) -> p c k", c=16)
            for c in range(16):
                for k in range(8):
                    P.op("pe", lambda e, c=c, k=k: e.matmul(pbt[:, c, :], lhsT=RH[:, k, c * 128:(c + 1) * 128], rhs=wv[:, k, :],
                                                              start=(k == 0), stop=(k == 7)),
                         reads=[Ht[k], wt], writes=[BK[0]])
            P.op("act", lambda e: e.copy(out=bt, in_=pbt), reads=[BK[0]], writes=[SM["bt"]])
            P.op("act", lambda e: e.activation(out=lnb, in_=bt[:, :, 0:16], func=AF.Exp, scale=-1.0), reads=[SM["bt"]], writes=[SM["lnb"]])
            P.op("act", lambda e: e.activation(out=lnb, in_=lnb, func=AF.Ln, bias=1.0), reads=[SM["lnb"]], writes=[SM["lnb"]])
            P.op("dve", lambda e: e.tensor_scalar(out=lnb, in0=lnb, scalar1=-1.0, scalar2=None, op0=ALU.mult), reads=[SM["lnb"]], writes=[SM["lnb"]])
            dtb = dnv[:, l, 1, :].unsqueeze(1).to_broadcast([128, 16, 16])
            ngb = nega[:, l, :].unsqueeze(1).to_broadcast([128, 16, 16])
            P.op("dve", lambda e: e.tensor_tensor(out=gg, in0=bt[:, :, 16:32], in1=dtb, op=ALU.add), reads=[SM["bt"], VEC], writes=[SM["g"]])
            P.op("act", lambda e: e.activation(out=gg, in_=gg, func=AF.Exp), reads=[SM["g"]], writes=[SM["g"]])
            P.op("act", lambda e: e.activation(out=gg, in_=gg, func=AF.Ln, bias=1.0), reads=[SM["g"]], writes=[SM["g"]])
            P.op("dve", lambda e: e.tensor_tensor(out=gg, in0=gg, in1=ngb, op=ALU.mult), reads=[SM["g"], VEC], writes=[SM["g"]])
            pgc = bank(1)[:, 0:256].rearrange("p (c k) -> p c k", c=16)
            pgl = bank(1)[:, 256:512].rearrange("p (c k) -> p c k", c=16)
            P.op("pe", lambda e: e.matmul(pgc[:, :, 0:8], lhsT=cm["Lf"], rhs=gg[:, :, 0:8], start=True, stop=True), reads=[CMt, SM["g"]], writes=[BK[1]])
            P.op("pe", lambda e: e.matmul(pgc[:, :, 8:16], lhsT=cm["Lb"], rhs=gg[:, :, 8:16], start=True, stop=True), reads=[CMt, SM["g"]], writes=[BK[1]])
            P.op("pe", lambda e: e.matmul(pgl, lhsT=cm["ones"], rhs=gg, start=True, stop=True), reads=[CMt, SM["g"]], writes=[BK[1]])
            P.op("act", lambda e: e.copy(out=gc, in_=pgc), reads=[BK[1]], writes=[SM["gc"]])
            P.op("act", lambda e: e.activation(out=colE, in_=pgc, func=AF.Exp), reads=[BK[1]], writes=[SM["colE"]])
            P.op("act", lambda e: e.activation(out=egl, in_=pgl, func=AF.Exp), reads=[BK[1]], writes=[SM["egl"]])
            P.op("dve", lambda e: e.tensor_scalar(out=colC, in0=colE, scalar1=-1.0, scalar2=None, op0=ALU.mult), reads=[SM["colE"]], writes=[SM["colC"]])
            P.op("dve", lambda e: e.tensor_tensor(out=stmp, in0=pgl, in1=gc, op=ALU.subtract), reads=[BK[1], SM["gc"]], writes=[SM["tmp"]])
            P.op("dve", lambda e: e.tensor_tensor(out=stmp, in0=stmp, in1=lnb, op=ALU.add), reads=[SM["tmp"], SM["lnb"]], writes=[SM["tmp"]])
            P.op("act", lambda e: e.activation(out=colD, in_=stmp, func=AF.Exp), reads=[SM["tmp"]], writes=[SM["colD"]])
            if s == 0 and l == 0:
                dbg_dump("sm", SMbuf, list(SM.values()), [128, 512 + 8 * 256], F32)
            if stop_after == "sm":
                raise _Stop()

            p1v = [bank(4 + sl).rearrange("p (g n) -> p g n", g=2) for sl in range(2)]
            p2f = [bank(6 + sl)[:, 0:256].rearrange("p (g n) -> p g n", g=2) for sl in range(2)]
            p2b = [bank(6 + sl)[:, 0:256].bitcast(BF16)[:, 0:256].rearrange("p (g n) -> p g n", g=2) for sl in range(2)]
            bA = bank(0, 4)
            bAt = BK[0:4]

            for hd in range(heads):
                dh = [hd, 8 + hd]
                for blk, dst, dkey in ((hd, qsb, "qs"), (8 + hd, ksb, "ks"), (16 + hd, vsb, "vs")):
                    wv, wt = load_w(win[l, :, :, C_QKV + blk * 128:C_QKV + (blk + 1) * 128], [128, 8, 128])
                    proj(bA, bAt, wv, wt, hfn, Ht, 8)
                    P.op("act", lambda e: e.copy(out=rxp[:, 2:2050], in_=bA), reads=bAt, writes=[W["rxp"]] + OTt)
                    cw = lambda j, r_=l * 24 + blk: dncw[:, r_, j:j + 1]
                    P.op("dve", lambda e, cw=cw: e.tensor_scalar(out=F0, in0=rxp[:, 0:2048], scalar1=cw(0), scalar2=None, op0=ALU.mult),
                         reads=[W["rxp"], VEC], writes=[W["F0"]])
                    P.op("dve", lambda e, cw=cw: e.scalar_tensor_tensor(out=F1, in0=rxp[:, 1:2049], scalar=cw(1), in1=F0, op0=ALU.mult, op1=ALU.add),
                         reads=[W["rxp"], VEC, W["F0"]], writes=[W["F1"]])
                    P.op("dve", lambda e, cw=cw: e.scalar_tensor_tensor(out=F0, in0=rxp[:, 2:2050], scalar=cw(2), in1=F1, op0=ALU.mult, op1=ALU.add),
                         reads=[W["rxp"], VEC, W["F1"]], writes=[W["F0"]])
                    P.op("dve", lambda e, cw=cw: e.scalar_tensor_tensor(out=F1, in0=rxp[:, 3:2051], scalar=cw(3), in1=F0, op0=ALU.mult, op1=ALU.add),
                         reads=[W["rxp"], VEC, W["F0"]], writes=[W["F1"]])
                    P.op("act", lambda e: e.activation(out=F2, in_=F1, func=AF.Tanh, scale=0.5), reads=[W["F1"]], writes=[W["F2"]])
                    P.op("dve", lambda e, dst=dst: e.scalar_tensor_tensor(out=dst, in0=F2, scalar=1.0, in1=F1, op0=ALU.add, op1=ALU.mult),
                         reads=[W["F2"], W["F1"]], writes=[DR[dkey]] + T2Tt + ATTt)
                for src, skey, slot, scl in ((ksb, "ks", 0, 1.0), (qsb, "qs", 1, 128.0 ** -0.5)):
                    P.op("act", lambda e, src=src: e.activation(out=sqb, in_=src, func=AF.Square), reads=[DR[skey]], writes=[DR["sqb"]])
                    for tt in range(4):
                        P.op("pe", lambda e, tt=tt: e.matmul(bank(tt), lhsT=ones_bf, rhs=sqb[:, tt * 512:(tt + 1) * 512], start=True, stop=True),
                             reads=[CBt, DR["sqb"]], writes=[BK[tt]])
                    P.op("act", lambda e: e.activation(out=F0, in_=bA, func=AF.Ln, bias=4.0 * EPS), reads=bAt, writes=[W["F0"]])
                    P.op("act", lambda e: e.activation(out=F0, in_=F0, func=AF.Exp, scale=-0.5), reads=[W["F0"]], writes=[W["F0"]])
                    P.op("dve", lambda e, src=src, slot=slot, scl=scl: e.scalar_tensor_tensor(
                        out=QK[:, :, slot, :], in0=src.rearrange("p (c d) -> p c d", c=16), scalar=scl,
                        in1=F0.rearrange("p (c d) -> p c d", c=16), op0=ALU.mult, op1=ALU.mult),
                         reads=[DR[skey], W["F0"]], writes=[DN["QK"]])
                ptr = bank(0, 2).bitcast(BF16).rearrange("p (c d) -> p c d", c=16)
                ptr2 = bank(2, 2).bitcast(BF16).rearrange("p (c d) -> p c d", c=16)
                for c in range(16):
                    P.op("pe", lambda e, c=c: e.transpose(ptr[:, c, :], QK[:, c, 0, :], ident_bf), reads=[DN["QK"], CBt], writes=[BK[0], BK[1]])
                P.op("act", lambda e: e.copy(out=ktok, in_=ptr), reads=[BK[0], BK[1]], writes=[DN["ktok"]])
                for c in range(16):
                    P.op("pe", lambda e, c=c: e.transpose(ptr2[:, c, :], vsb[:, c * 128:(c + 1) * 128], ident_bf), reads=[DR["vs"], CBt], writes=[BK[2], BK[3]])
                P.op("act", lambda e: e.mul(out=vtok, in_=ptr2, mul=0.5), reads=[BK[2], BK[3]], writes=[DN["vtok"]])
                for d in range(2):
                    cdb = colD[:, :, dh[d]:dh[d] + 1].to_broadcast([128, 16, 128])
                    P.op("dve", lambda e, d=d, cdb=cdb: e.tensor_tensor(out=KDB[:, d], in0=ktok, in1=cdb, op=ALU.mult),
                         reads=[DN["ktok"], SM["colD"]], writes=[DN["KDB"]])
                    for c4 in range(4):
                        prb = bank(3).rearrange("p (c d) -> p c d", c=4)
                        for cc in range(4):
                            c = c4 * 4 + cc
                            j = c % 2
                            P.op("pool", lambda e, c=c, d=d, j=j: e.tensor_scalar(out=Dgq[j], in0=cm["ident"], scalar1=colE[:, c, dh[d]:dh[d] + 1],
                                                                                   scalar2=None, op0=ALU.mult),
                                 reads=[CMt, SM["colE"]], writes=[DN["Dgq%d" % j]])
                            P.op("pe", lambda e, cc=cc, j=j: e.matmul(prb[:, cc, :], lhsT=cm["ones"], rhs=Dgq[j], start=True, stop=True),
                                 reads=[CMt, DN["Dgq%d" % j]], writes=[BK[3]])
                        P.op("dve", lambda e, d=d, c4=c4: e.tensor_tensor(out=QG[:, d, c4 * 4:(c4 + 1) * 4, :], in0=QK[:, c4 * 4:(c4 + 1) * 4, 1, :],
                                                                            in1=prb, op=ALU.mult),
                             reads=[DN["QK"], BK[3]], writes=[QGt[d]])

                if stop_after == "prep":
                    dbg_dump("QK", RX[:, 8196:10244].bitcast(BF16), [DN["QK"]], [128, 4096], BF16)
                    dbg_dump("ktok", RX[:, 10244:11268].bitcast(BF16), [DN["ktok"]], [128, 2048], BF16)
                    dbg_dump("vtok", RX[:, 11268:12292].bitcast(BF16), [DN["vtok"]], [128, 2048], BF16)
                    dbg_dump("KDB", RX[:, 12292:14340].bitcast(BF16), [DN["KDB"]], [128, 4096], BF16)
                    dbg_dump("QG", QGbuf, QGt, [128, 4096], BF16)
                    raise _Stop()
                def unit_setup(sl, c):
                    P.op("pe", lambda e: e.matmul(p1v[sl][:, 0, :], lhsT=QK[:, c, 0, :], rhs=QKf[:, c, :], start=True, stop=True),
                         reads=[DN["QK"]], writes=[BK[4 + sl]])
                    kq = p1v[sl][:, 0, :].rearrange("p (a b) -> p a b", a=2)
                    P.op("dve", lambda e: e.tensor_tensor(out=KQA[sl], in0=kq, in1=mskA, op=ALU.mult), reads=[BK[4 + sl], CBt], writes=[KQAt[sl]])
                    P.op("dve", lambda e: e.tensor_tensor(out=KQB[sl], in0=kq, in1=mskB, op=ALU.mult), reads=[BK[4 + sl], CBt], writes=[KQBt[sl]])
                    for d in range(2):
                        Lm = cm["Lf"] if d == 0 else cm["Lb"]
                        Sm = cm["SUf"] if d == 0 else cm["SUb"]
                        kq_ = KQA[sl] if d == 0 else KQB[sl]
                        kqt = KQAt[sl] if d == 0 else KQBt[sl]
                        pe_ = p1v[sl][:, 1, d * 128:(d + 1) * 128]
                        P.op("pool", lambda e, d=d, Lm=Lm: e.tensor_scalar(out=Dg[sl][:, d, :], in0=Lm, scalar1=gg[:, c, dh[d]:dh[d] + 1], scalar2=None, op0=ALU.mult),
                             reads=[CMt, SM["g"]], writes=[DN["Dg%d" % sl]])
                        P.op("pe", lambda e, d=d, Sm=Sm, pe_=pe_: e.matmul(pe_, lhsT=Sm, rhs=Dg[sl][:, d, :], start=True, stop=True),
                             reads=[CMt, DN["Dg%d" % sl]], writes=[BK[4 + sl]])
                        P.op("act", lambda e, d=d, pe_=pe_: e.activation(out=D2[sl][:, d, :], in_=pe_, func=AF.Exp, bias=lnb[:, c, dh[d]:dh[d] + 1]),
                             reads=[BK[4 + sl], SM["lnb"]], writes=[DN["D2%d" % sl]])
                        P.op("dve", lambda e, d=d, kq_=kq_: e.tensor_tensor(out=YQ[sl][0][:, d, 0:128], in0=kq_[:, 0, :], in1=D2[sl][:, d, :], op=ALU.mult),
                             reads=[kqt, DN["D2%d" % sl]], writes=[YQt[sl][0]])
                        P.op("dve", lambda e, d=d, kq_=kq_: e.tensor_tensor(out=ATT[:, c, d, :], in0=kq_[:, 1, :], in1=D2[sl][:, d, :], op=ALU.mult),
                             reads=[kqt, DN["D2%d" % sl]], writes=[ATTt[c], DR["vs"], DR["sqb"]])
                        P.op("pe", lambda e, d=d: e.transpose(p2f[sl][:, d, :], YQ[sl][0][:, d, 0:128], cm["ident"]),
                             reads=[YQt[sl][0], CMt], writes=[BK[6 + sl]])
                    P.op("act", lambda e: e.copy(out=YTb[sl][0], in_=p2f[sl]), reads=[BK[6 + sl]], writes=[YTt[sl][0]])

                def unit_level_mm(sl, c, k):
                    ci = k % 2
                    yq, yt = YQ[sl][ci], YTb[sl][ci]
                    rd = [YQt[sl][ci], YTt[sl][ci]]
                    for g_ in range(2):
                        if k == 0:
                            P.op("pe", lambda e, g_=g_: e.matmul(p1v[sl][:, g_, 0:128], lhsT=yt[:, g_, :], rhs=yq[:, g_, 0:128], start=True, stop=True),
                                 reads=rd, writes=[BK[4 + sl]])
                        elif k < 6:
                            P.op("pe", lambda e, g_=g_: e.matmul(p1v[sl][:, g_, :], lhsT=yt[:, g_, :], rhs=yq[:, g_, :], start=True, stop=True),
                                 reads=rd, writes=[BK[4 + sl]])
                        else:
                            P.op("pe", lambda e, g_=g_: e.matmul(p1v[sl][:, g_, 128:256], lhsT=yt[:, g_, :], rhs=yq[:, g_, 128:256], start=True, stop=True),
                                 reads=rd, writes=[BK[4 + sl]])
                    if k < 6:
                        for g_ in range(2):
                            P.op("pe", lambda e, g_=g_: e.matmul(p2f[sl][:, g_, :], lhsT=yq[:, g_, 0:128], rhs=yt[:, g_, :], start=True, stop=True),
                                 reads=rd, writes=[BK[6 + sl]])

                def unit_level_ev(sl, c, k):
                    ci = k % 2
                    ni = 1 - ci
                    yq = YQ[sl][ci]
                    if k < 6:
                        P.op("act", lambda e: e.copy(out=YQ[sl][ni][:, :, 0:128], in_=p1v[sl][:, :, 0:128]), reads=[BK[4 + sl]], writes=[YQt[sl][ni]])
                        P.op("act", lambda e: e.copy(out=YTb[sl][ni], in_=p2f[sl]), reads=[BK[6 + sl]], writes=[YTt[sl][ni]])
                        if k == 0:
                            idb = cm["ident"].unsqueeze(1).to_broadcast([128, 2, 128])
                            P.op("dve", lambda e: e.tensor_tensor(out=YQ[sl][ni][:, :, 128:256], in0=yq[:, :, 0:128], in1=idb, op=ALU.add),
                                 reads=[YQt[sl][ci], CMt], writes=[YQt[sl][ni]])
                        else:
                            P.op("dve", lambda e: e.tensor_tensor(out=YQ[sl][ni][:, :, 128:256], in0=yq[:, :, 128:256], in1=p1v[sl][:, :, 128:256], op=ALU.add),
                                 reads=[YQt[sl][ci], BK[4 + sl]], writes=[YQt[sl][ni]])
                    else:
                        P.op("dve", lambda e: e.tensor_tensor(out=T2T[:, c, :, :], in0=yq[:, :, 128:256], in1=p1v[sl][:, :, 128:256], op=ALU.add),
                             reads=[YQt[sl][ci], BK[4 + sl]], writes=[T2Tt[c], DR["qs"], DR["ks"]])

                for c0 in range(0, 16, 2):
                    for sl in range(2):
                        unit_setup(sl, c0 + sl)
                    for k in range(7):
                        for sl in range(2):
                            unit_level_mm(sl, c0 + sl, k)
                        for sl in range(2):
                            unit_level_ev(sl, c0 + sl, k)

                if stop_after == "units":
                    dbg_dump("T2T", RYtail[:, 0:4096], T2Tt, [128, 4096], BF16)
                    dbg_dump("ATT", RYtail[:, 4096:8192], ATTt, [128, 4096], BF16)
                    raise _Stop()
                for d in range(2):
                    P.op("pool", lambda e, d=d: e.memset(S32[d], 0.0), writes=[DN["S%d" % d]])
                    P.op("pool", lambda e, d=d: e.memset(Sbf[d], 0.0), writes=[DN["Sb%d" % d]])
                for step in range(16):
                    for d in range(2):
                        c = step if d == 0 else 15 - step
                        pk = bank(4 * d + 0)[:, 0:128]
                        px = bank(4 * d + 1)[:, 0:128]
                        po = bank(4 * d + 2)[:, 0:128]
                        pd = bank(4 * d + 3)[:, 0:128]
                        St, Sbt, rt, xt_ = DN["S%d" % d], DN["Sb%d" % d], DN["r%d" % d], DN["x%d" % d]
                        P.op("pe", lambda e, c=c, d=d, pk=pk: e.matmul(pk, lhsT=QK[:, c, 0, :], rhs=Sbf[d], start=True, stop=True),
                             reads=[DN["QK"], Sbt], writes=[BK[4 * d + 0]])
                        P.op("dve", lambda e, c=c, d=d, pk=pk: e.scalar_tensor_tensor(out=rbf[d], in0=pk, scalar=colC[:, c, dh[d]:dh[d] + 1], in1=vtok[:, c, :],
                                                                                       op0=ALU.mult, op1=ALU.add),
                             reads=[BK[4 * d + 0], SM["colC"], DN["vtok"]], writes=[rt])
                        P.op("pe", lambda e, c=c, d=d, px=px: e.matmul(px, lhsT=T2T[:, c, d, :], rhs=rbf[d], start=True, stop=True),
                             reads=[T2Tt[c], rt], writes=[BK[4 * d + 1]])
                        P.op("act", lambda e, d=d, px=px: e.copy(out=xbf[d], in_=px), reads=[BK[4 * d + 1]], writes=[xt_])
                        P.op("pe", lambda e, c=c, d=d, po=po: e.matmul(po, lhsT=Sbf[d], rhs=QG[:, d, c, :], start=True, stop=False),
                             reads=[Sbt, QGt[d]], writes=[BK[4 * d + 2]])
                        P.op("pe", lambda e, c=c, d=d, po=po: e.matmul(po, lhsT=xbf[d], rhs=ATT[:, c, d, :], start=False, stop=True),
                             reads=[xt_, ATTt[c]], writes=[BK[4 * d + 2]])
                        P.op("pe", lambda e, c=c, d=d, pd=pd: e.matmul(pd, lhsT=KDB[:, d, c, :], rhs=xbf[d], start=True, stop=True),
                             reads=[DN["KDB"], xt_], writes=[BK[4 * d + 3]])
                        oc = OT[:, c * 128:(c + 1) * 128]
                        if step < 8:
                            P.op("act", lambda e, oc=oc, po=po: e.copy(out=oc, in_=po), reads=[BK[4 * d + 2]], writes=[OTt[c], W["rxp"]])
                        else:
                            P.op("dve", lambda e, oc=oc, po=po: e.tensor_tensor(out=oc, in0=oc, in1=po, op=ALU.add), reads=[BK[4 * d + 2], OTt[c]], writes=[OTt[c]])
                        P.op("dve", lambda e, c=c, d=d, pd=pd: e.scalar_tensor_tensor(out=S32[d], in0=S32[d], scalar=egl[:, c, dh[d]:dh[d] + 1], in1=pd,
                                                                                       op0=ALU.mult, op1=ALU.add),
                             reads=[St, SM["egl"], BK[4 * d + 3]], writes=[St])
                        P.op("act", lambda e, d=d: e.copy(out=Sbf[d], in_=S32[d]), reads=[St], writes=[Sbt])

                if stop_after == "chain":
                    dbg_dump("OT", OT, OTt, [128, 2048], F32)
                    raise _Stop()
                P.op("act", lambda e: e.activation(out=sqb, in_=OT, func=AF.Square), reads=OTt, writes=[DR["sqb"]] + ATTt)
                for tt in range(4):
                    P.op("pe", lambda e, tt=tt: e.matmul(bank(tt), lhsT=ones_bf, rhs=sqb[:, tt * 512:(tt + 1) * 512], start=True, stop=True),
                         reads=[CBt, DR["sqb"]], writes=[BK[tt]])
                P.op("act", lambda e: e.activation(out=F0, in_=bA, func=AF.Ln, bias=EPS, scale=1.0 / 128.0), reads=bAt, writes=[W["F0"]])
                P.op("act", lambda e: e.activation(out=F0, in_=F0, func=AF.Exp, scale=-0.5), reads=[W["F0"]], writes=[W["F0"]])
                wv, wt = load_w(win[l, :, :, C_Z + hd * 128:C_Z + (hd + 1) * 128], [128, 8, 128])
                bB = bank(4, 4)
                proj(bB, BK[4:8], wv, wt, hfn, Ht, 8)
                P.op("act", lambda e: e.activation(out=F1, in_=bB, func=AF.Tanh, scale=0.5), reads=BK[4:8], writes=[W["F1"]])
                P.op("dve", lambda e: e.scalar_tensor_tensor(out=F2, in0=F1, scalar=1.0, in1=bB, op0=ALU.add, op1=ALU.mult),
                     reads=[W["F1"]] + BK[4:8], writes=[W["F2"]])
                P.op("dve", lambda e: e.scalar_tensor_tensor(out=F1, in0=OT, scalar=dnnorm[:, l:l + 1], in1=F0, op0=ALU.mult, op1=ALU.mult),
                     reads=OTt + [VEC, W["F0"]], writes=[W["F1"]])
                P.op("dve", lambda e, hd=hd: e.scalar_tensor_tensor(out=RYb[:, hd, :], in0=F1, scalar=0.5, in1=F2, op0=ALU.mult, op1=ALU.mult),
                     reads=[W["F1"], W["F2"]], writes=[RYt[hd]])
            if s == 0 and l == 0:
                dbg_dump("yd", RYb[:, 0:8, :], RYt[0:8], [128, 8, SEQ], BF16)
            if stop_after == "dn":
                raise _Stop()

            MIXB = [RYtail[:, 0:4096].rearrange("p (c t) -> p c t", c=8), RYtail[:, 4096:8192].rearrange("p (c t) -> p c t", c=8)]
            MIXt = [T(), T()]
            P.alias(MIXt, list(DR.values()) + T2Tt + ATTt)
            thg = [QGbuf[:, 0:1024].bitcast(F32), QGbuf[:, 1024:2048].bitcast(F32)]
            mst = [QGbuf[:, 2048:2560], QGbuf[:, 2560:3072]]
            thg_t = [T(), T()]
            mst_t = [T(), T()]
            P.alias(thg_t + mst_t, QGt)
            allwork = list(W.values()) + list(DN.values()) + OTt
            P.alias(Xt, allwork)
            bi = [0]

            def nb():
                b = bi[0]
                bi[0] = (b + 1) % 8
                return bank(b), BK[b]
            for tt in range(4):
                mb, mbt = MIXB[tt % 2], MIXt[tt % 2]
                for n in range(8):
                    wv, wt = load_w(wdn[l, :, :, n * 128:(n + 1) * 128], [128, 8, 128])
                    wv2, wt2 = load_w(win[l, :, :, C_GD + n * 128:C_GD + (n + 1) * 128], [128, 8, 128])
                    py, pyt = nb()
                    pg, pgt = nb()
                    for k in range(8):
                        P.op("pe", lambda e, k=k, py=py, wv=wv: e.matmul(py, lhsT=wv[:, k, :], rhs=RYb[:, k, tt * 512:(tt + 1) * 512], start=(k == 0), stop=(k == 7)),
                             reads=[wt, RYt[k]], writes=[pyt])
                    for k in range(8):
                        P.op("pe", lambda e, k=k, pg=pg, wv2=wv2: e.matmul(pg, lhsT=wv2[:, k, :], rhs=RH[:, k, tt * 512:(tt + 1) * 512], start=(k == 0), stop=(k == 7)),
                             reads=[wt2, Ht[k]], writes=[pgt])
                    j = n % 2
                    P.op("act", lambda e, j=j, pg=pg: e.activation(out=thg[j], in_=pg, func=AF.Tanh, scale=0.5), reads=[pgt], writes=[thg_t[j]])
                    P.dma("sp", mst[j], mixspill[n, :, tt * 512:(tt + 1) * 512], reads=[MSt[n]], writes=[mst_t[j]])
                    P.op("dve", lambda e, j=j, py=py: e.scalar_tensor_tensor(out=thg[j], in0=thg[j], scalar=1.0, in1=py, op0=ALU.add, op1=ALU.mult),
                         reads=[thg_t[j], pyt], writes=[thg_t[j]])
                    P.op("dve", lambda e, j=j, n=n, mb=mb: e.tensor_tensor(out=mb[:, n, :], in0=thg[j], in1=mst[j], op=ALU.add),
                         reads=[thg_t[j], mst_t[j]], writes=[mbt])
                for n in range(8):
                    wv, wt = load_w(wout[l, :, :, n * 128:(n + 1) * 128], [128, 8, 128])
                    po_, pot = nb()
                    for k in range(8):
                        P.op("pe", lambda e, k=k, po_=po_, wv=wv, mb=mb: e.matmul(po_, lhsT=wv[:, k, :], rhs=mb[:, k, :], start=(k == 0), stop=(k == 7)),
                             reads=[wt, mbt], writes=[pot])
                    xs = Xv[:, n, tt * 512:(tt + 1) * 512]
                    P.dma("sp", xs, xspill[:, n, tt * 512:(tt + 1) * 512], reads=[XSt[n]], writes=[Xt[n]])
                    P.op("dve", lambda e, xs=xs, po_=po_: e.scalar_tensor_tensor(out=xs, in0=po_, scalar=0.5, in1=xs, op0=ALU.mult, op1=ALU.add),
                         reads=[pot, Xt[n]], writes=[Xt[n]])
            P.alias(RYt, MIXt + RYt)
            P.alias(QGt, thg_t + mst_t)
            if s == 0 and l == 0:
                dbg_dump("xmix", Xv, Xt, [128, 8, SEQ], F32)
            if stop_after == "mix":
                raise _Stop()

            rstd, tmp_t = rms_stats(1.0 / D)
            for c in range(8):
                g = norms[:, (l * 2 + 1) * 8 + c:(l * 2 + 1) * 8 + c + 1]
                P.op("dve", lambda e, c=c, g=g: e.scalar_tensor_tensor(out=RH[:, c, :], in0=Xv[:, c, :], scalar=g, in1=rstd, op0=ALU.mult, op1=ALU.mult),
                     reads=[Xt[c], tmp_t[9], VEC], writes=[Ht[c]])
            ACTb = RY[:, 0:22528].rearrange("p (f t) -> p f t", f=22)
            ACt = [T() for _ in range(22)]
            P.alias(ACt, tmp_t)
            thf = [QGbuf[:, 0:2048].bitcast(F32), QGbuf[:, 2048:4096].bitcast(F32)]
            thf_t = [T(), T()]
            P.alias(thf_t, QGt)
            for half in range(2):
                t0 = half * 1024
                for f in range(22):
                    wv, wt = load_w(wgu[l, :, :, f * 128:(f + 1) * 128], [128, 8, 128])
                    wv2, wt2 = load_w(wgu[l, :, :, DFF + f * 128:DFF + (f + 1) * 128], [128, 8, 128])
                    pb, pt = grp()
                    for q_ in range(2):
                        for k in range(8):
                            P.op("pe", lambda e, k=k, q_=q_, pb=pb, wv=wv: e.matmul(pb[:, q_ * 512:(q_ + 1) * 512], lhsT=wv[:, k, :],
                                                                                     rhs=RH[:, k, t0 + q_ * 512:t0 + (q_ + 1) * 512], start=(k == 0), stop=(k == 7)),
                                 reads=[wt, Ht[k]], writes=[pt[q_]])
                    for q_ in range(2):
                        for k in range(8):
                            P.op("pe", lambda e, k=k, q_=q_, pb=pb, wv2=wv2: e.matmul(pb[:, 1024 + q_ * 512:1024 + (q_ + 1) * 512], lhsT=wv2[:, k, :],
                                                                                       rhs=RH[:, k, t0 + q_ * 512:t0 + (q_ + 1) * 512], start=(k == 0), stop=(k == 7)),
                                 reads=[wt2, Ht[k]], writes=[pt[2 + q_]])
                    j = f % 2
                    P.op("act", lambda e, j=j, pb=pb: e.activation(out=thf[j], in_=pb[:, 0:1024], func=AF.Tanh, scale=0.5), reads=pt[0:2], writes=[thf_t[j]])
                    P.op("dve", lambda e, j=j, pb=pb: e.scalar_tensor_tensor(out=thf[j], in0=thf[j], scalar=1.0, in1=pb[:, 0:1024], op0=ALU.add, op1=ALU.mult),
                         reads=[thf_t[j]] + pt[0:2], writes=[thf_t[j]])
                    P.op("dve", lambda e, j=j, pb=pb, f=f: e.scalar_tensor_tensor(out=ACTb[:, f, :], in0=thf[j], scalar=0.5, in1=pb[:, 1024:2048], op0=ALU.mult, op1=ALU.mult),
                         reads=[thf_t[j]] + pt[2:4], writes=[ACt[f]])
                for n in range(8):
                    wv, wt = load_w(wdown[l, :, 0:11, n * 128:(n + 1) * 128], [128, 11, 128])
                    wv2, wt2 = load_w(wdown[l, :, 11:22, n * 128:(n + 1) * 128], [128, 11, 128])
                    pb, pt = grp()
                    for q_ in range(2):
                        for f in range(22):
                            w_, wt_ = (wv, wt) if f < 11 else (wv2, wt2)
                            P.op("pe", lambda e, f=f, q_=q_, pb=pb, w_=w_: e.matmul(pb[:, q_ * 512:(q_ + 1) * 512], lhsT=w_[:, f % 11, :],
                                                                                     rhs=ACTb[:, f, q_ * 512:(q_ + 1) * 512], start=(f == 0), stop=(f == 21)),
                                 reads=[wt_, ACt[f]], writes=[pt[q_]])
                    xs = Xv[:, n, t0:t0 + 1024]
                    P.op("dve", lambda e, xs=xs, pb=pb: e.tensor_tensor(out=xs, in0=xs, in1=pb[:, 0:1024], op=ALU.add), reads=[Xt[n]] + pt[0:2], writes=[Xt[n]])
            P.alias(RYt, ACt)
            P.alias(QGt, thf_t)
            if s == 0:
                dbg_dump("xl%d" % l, Xv, Xt, [128, 8, SEQ], F32)
        if stop_after:
            raise _Stop()

        rstd, tmp_t = rms_stats(1.0 / D)
        ost = [RYf[:, 0:2048], RYf[:, 2048:4096]]
        ost_t = [T(), T()]
        P.alias(ost_t, tmp_t[0:8])
        for c in range(8):
            j = c % 2
            P.op("dve", lambda e, c=c, j=j: e.scalar_tensor_tensor(out=ost[j], in0=Xv[:, c, :], scalar=fnorm[:, c:c + 1], in1=rstd, op0=ALU.mult, op1=ALU.mult),
                 reads=[Xt[c], tmp_t[9], VEC], writes=[ost_t[j]])
            P.dma("sp", outT[s, :, c, :], ost[j], reads=[ost_t[j]], is_output=True)
        P.alias(RYt, tmp_t + ost_t)

    return


def prep_weights(w_in, rg_conv_w, rg_conv_b, rg_wa, rg_ba, rg_wx, rg_bx, rg_lambda, w_rnn_proj, dn_conv_w,
                 dn_a_log, dn_dt_bias, dn_norm, w_dn_proj, w_out, mix_norm, ffn_norm, w_gate_up, w_down, final_norm):
    L = NLAYER
    f = lambda a: np.ascontiguousarray(np.asarray(a, dtype=np.float32))
    m = {}
    m["win"] = f(np.asarray(w_in).reshape(L, 8, 128, NIN).transpose(0, 2, 1, 3))
    wa = np.asarray(rg_wa)
    wx = np.asarray(rg_wx)
    g = np.stack([wa[:, 0], wa[:, 1], wx[:, 0], wx[:, 1]], axis=3)
    m["gatew"] = f(g)
    cwv = np.asarray(rg_conv_w).reshape(L, 4, 12, 128).transpose(3, 0, 2, 1)
    pv = lambda a: np.asarray(a).reshape(L, 12, 128).transpose(2, 0, 1)[..., None]
    pv2 = lambda a, d: np.asarray(a)[:, d].reshape(L, 12, 128).transpose(2, 0, 1)[..., None]
    rgv = np.concatenate([cwv, pv(rg_conv_b), pv2(rg_ba, 0), pv2(rg_ba, 1), pv2(rg_bx, 0), pv2(rg_bx, 1),
                          pv2(rg_lambda, 0), pv2(rg_lambda, 1)], axis=3)
    m["rgv"] = f(rgv.reshape(128, L * 12, 11))
    m["wrnn"] = f(np.asarray(w_rnn_proj).reshape(L, 12, 128, D).transpose(0, 2, 1, 3))
    m["dncw"] = f(np.asarray(dn_conv_w).reshape(L, 4, 24, 128).transpose(3, 0, 2, 1).reshape(128, L * 24, 4))
    dv = np.stack([np.asarray(dn_a_log).reshape(L, 16), np.asarray(dn_dt_bias).reshape(L, 16)], axis=1)
    m["dnv"] = f(np.broadcast_to(dv[None], (128, L, 2, 16)))
    m["dnnorm"] = f(np.asarray(dn_norm).T)
    m["wdn"] = f(np.asarray(w_dn_proj).reshape(L, 8, 128, D).transpose(0, 2, 1, 3))
    m["wout"] = f(np.asarray(w_out).reshape(L, 8, 128, D).transpose(0, 2, 1, 3))
    nm = np.stack([np.asarray(mix_norm).reshape(L, 8, 128), np.asarray(ffn_norm).reshape(L, 8, 128)], axis=1)
    m["norms"] = f(nm.transpose(3, 0, 1, 2).reshape(128, L * 2 * 8))
    m["fnorm"] = f(np.asarray(final_norm).reshape(8, 128).T)
    m["wgu"] = f(np.asarray(w_gate_up).reshape(L, 8, 128, 2 * DFF).transpose(0, 2, 1, 3))
    m["wdown"] = f(np.asarray(w_down).reshape(L, 22, 128, D).transpose(0, 2, 1, 3))
    m["consts"] = make_consts()
    return m


def prep_x(xs):
    n = xs.shape[0]
    return np.ascontiguousarray(np.asarray(xs, dtype=np.float32).transpose(0, 2, 1).reshape(n, 8, 128, SEQ).transpose(0, 2, 1, 3))


def unprep_x(o):
    n = o.shape[0]
    return np.ascontiguousarray(o.transpose(0, 2, 1, 3).reshape(n, D, SEQ).transpose(0, 2, 1))


def kernel(x, mix_norm, w_in, rg_conv_w, rg_conv_b, rg_wa, rg_ba, rg_wx, rg_bx, rg_lambda,
           w_rnn_proj, dn_conv_w, dn_a_log, dn_dt_bias, dn_norm, w_dn_proj, w_out,
           ffn_norm, w_gate_up, w_down, final_norm):
    x = np.asarray(x)
    B = x.shape[0]
    nseq = B // NCORES
    wm = prep_weights(w_in, rg_conv_w, rg_conv_b, rg_wa, rg_ba, rg_wx, rg_bx, rg_lambda, w_rnn_proj, dn_conv_w,
                      dn_a_log, dn_dt_bias, dn_norm, w_dn_proj, w_out, mix_norm, ffn_norm, w_gate_up, w_down, final_norm)
    nc = bass.Bass("TRN2", target_bir_lowering=False)
    build(nc, NSEQ=nseq)
    in_maps = []
    for cidx in range(NCORES):
        mm = dict(wm)
        mm["xT"] = prep_x(x[cidx * nseq:(cidx + 1) * nseq])
        in_maps.append(mm)
    res = run_bass_kernel_spmd(nc, in_maps, core_ids=list(range(NCORES)))
    outs = [unprep_x(np.asarray(r["outT"])) for r in res.results]
    return np.concatenate(outs, axis=0).astype(np.float32)
```

```python
import numpy as np
import concourse.bass as bass
import concourse.mybir as mybir
from concourse.bass_utils import run_bass_kernel_spmd

F32 = mybir.dt.float32
F32R = mybir.dt.float32
BF16 = mybir.dt.bfloat16
ALU = mybir.AluOpType
AF = mybir.ActivationFunctionType

NCORES = 8
D = 1024
SEQ = 2048
NLAYER = 4
DRNN = 1536
DFF = 2816
NIN = 9248
C_RY = 1536
C_QKV = 3072
C_Z = 6144
C_BA = 7168
C_GR = 7200
C_GD = 8224
EPS = 1e-6
NEG = -30000.0


class T:
    __slots__ = ("ap", "w", "r", "ps")

    def __init__(self, ap=None, ps=False):
        self.ap = ap
        self.w = None
        self.r = {}
        self.ps = ps


class _Rec:
    def __init__(self):
        self.call = None

    def __getattr__(self, name):
        def f(*a, **k):
            self.call = (name, a, k)
            return self
        return f


class Eng:
    def __init__(self, name, sem):
        self.name = name
        self.sem = sem
        self.key = ("e", name)
        self.count = 0
        self.waited = {}
        self.ops = []


class Prog:
    NDMASEM = 24

    def __init__(self, nc):
        self.nc = nc
        self.sems = {}
        self.E = {}
        for n in ("pe", "dve", "act", "pool", "sp"):
            s = nc.alloc_semaphore(name="s_" + n)
            e = Eng(n, s)
            self.E[n] = e
            self.sems[e.key] = s
        self.dsem = []
        for i in range(self.NDMASEM):
            s = nc.alloc_semaphore(name="d%d" % i)
            self.dsem.append([s, 0])
            self.sems[("d", i)] = s
        self.dnext = 0
        self.out_tokens = []
        self.ninst = 0

    def _deps(self, reads, writes):
        need = {}
        for t in reads:
            if t.w is not None:
                k, v = t.w
                if need.get(k, 0) < v:
                    need[k] = v
        for t in writes:
            if t.w is not None:
                k, v = t.w
                if need.get(k, 0) < v:
                    need[k] = v
            for k, v in t.r.items():
                if need.get(k, 0) < v:
                    need[k] = v
        return need

    def _emit_waits(self, e, need, skip_self=False):
        for k, v in need.items():
            if skip_self and k == e.key:
                continue
            if e.waited.get(k, 0) < v:
                e.waited[k] = v
                e.ops.append(("w", k, v))

    def _mark(self, tok, reads, writes):
        k, v = tok
        for t in reads:
            if t.r.get(k, 0) < v:
                t.r[k] = v
        for t in writes:
            t.w = tok
            t.r = {}

    def op(self, eng, fn, reads=(), writes=()):
        e = self.E[eng]
        psr = [t for t in reads if t.ps]
        if psr:
            reads = [t for t in reads if not t.ps]
            writes = list(writes) + psr
        need = self._deps(reads, writes)
        self._emit_waits(e, need, skip_self=(eng == "pe"))
        e.count += 1
        tok = (e.key, e.count)
        rec = _Rec()
        fn(rec)
        e.ops.append(("i", rec.call))
        self._mark(tok, reads, writes)
        self.ninst += 1
        return tok

    def dma(self, q, out_ap, in_ap, reads=(), writes=(), is_output=False):
        e = self.E[q]
        i = self.dnext
        self.dnext = (self.dnext + 1) % self.NDMASEM
        ds = self.dsem[i]
        need = self._deps(reads, writes)
        k = ("d", i)
        if ds[1] > 0 and need.get(k, 0) < ds[1]:
            need[k] = ds[1]
        self._emit_waits(e, need)
        ds[1] += 16
        tok = (k, ds[1])
        e.ops.append(("d", out_ap, in_ap, k))
        self._mark(tok, reads, writes)
        if is_output:
            self.out_tokens.append(tok)
        self.ninst += 1
        return tok

    def alias(self, new_tiles, old_tiles):
        acc = {}
        for t in old_tiles:
            if t.w is not None:
                k, v = t.w
                if acc.get(k, 0) < v:
                    acc[k] = v
            for k, v in t.r.items():
                if acc.get(k, 0) < v:
                    acc[k] = v
        for t in new_tiles:
            t.w = None
            t.r = dict(acc)

    def finish(self):
        e = self.E["sp"]
        need = {}
        for k, v in self.out_tokens:
            if need.get(k, 0) < v:
                need[k] = v
        self._emit_waits(e, need)
        nc = self.nc
        sems = self.sems

        def run(e, eng):
            for o in e.ops:
                if o[0] == "w":
                    eng.wait_ge(sems[o[1]], o[2])
                elif o[0] == "i":
                    name, a, k = o[1]
                    getattr(eng, name)(*a, **k).then_inc(e.sem, 1)
                else:
                    eng.dma_start(out=o[1], in_=o[2]).then_inc(sems[o[3]], 16)

        with nc.Block() as block:
            @block.tensor
            def _(eng):
                run(self.E["pe"], eng)

            @block.vector
            def _(eng):
                run(self.E["dve"], eng)

            @block.scalar
            def _(eng):
                run(self.E["act"], eng)

            @block.gpsimd
            def _(eng):
                run(self.E["pool"], eng)

            @block.sync
            def _(eng):
                run(self.E["sp"], eng)


CONST_NAMES = ["ident", "ones", "Lf", "Lb", "SUf", "SUb", "nsf", "nsb", "inf", "inb"]


def make_consts():
    t = np.arange(128)[:, None]
    i = np.arange(128)[None, :]
    c = {
        "ident": (t == i),
        "ones": np.ones((128, 128)),
        "Lf": (t <= i),
        "Lb": (t >= i),
        "SUf": (t > i),
        "SUb": (t < i),
        "nsf": -1.0 * (i > t),
        "nsb": -1.0 * (i < t),
        "inf": (i >= t),
        "inb": (i <= t),
    }
    return np.ascontiguousarray(np.stack([np.asarray(c[n], dtype=np.float32) for n in CONST_NAMES], axis=1))


class _Stop(Exception):
    pass


def build(nc, NSEQ=4, NL=NLAYER, dbg=None, stop_after=None, heads=8, rg_blocks=12):
    P = Prog(nc)
    try:
        _build(P, nc, NSEQ, NL, dbg, stop_after, heads, rg_blocks)
    except _Stop:
        pass
    P.finish()
    return P, None


def _build(P, nc, NSEQ, NL, dbg, stop_after, heads, rg_blocks):
    dbg = dbg or []
    dbg_out = {}

    def din(name, shape, dt=F32):
        return nc.dram_tensor(name, shape, dt, kind="ExternalInput").ap()

    xT = din("xT", [NSEQ, 128, 8, SEQ])
    outT = nc.dram_tensor("outT", [NSEQ, 128, 8, SEQ], F32, kind="ExternalOutput").ap()
    win = din("win", [NLAYER, 128, 8, NIN])
    gatew = din("gatew", [NLAYER, 12, 128, 4, 128])
    rgv_d = din("rgv", [128, NLAYER * 12, 11])
    wrnn = din("wrnn", [NLAYER, 128, 12, D])
    dncw_d = din("dncw", [128, NLAYER * 24, 4])
    dnv_d = din("dnv", [128, NLAYER, 2, 16])
    dnnorm_d = din("dnnorm", [128, NLAYER])
    wdn = din("wdn", [NLAYER, 128, 8, D])
    wout = din("wout", [NLAYER, 128, 8, D])
    norms_d = din("norms", [128, NLAYER * 2 * 8])
    fnorm_d = din("fnorm", [128, 8])
    wgu = din("wgu", [NLAYER, 128, 8, 2 * DFF])
    wdown = din("wdown", [NLAYER, 128, 22, D])
    consts_d = din("consts", [128, len(CONST_NAMES), 128])
    xspill = nc.dram_tensor("xspill", [128, 8, SEQ], F32, kind="Internal").ap()
    mixspill = nc.dram_tensor("mixspill", [8, 128, SEQ], BF16, kind="Internal").ap()
    XSt = [T() for _ in range(8)]
    MSt = [T() for _ in range(8)]

    def sb(name, shape, dt=F32):
        return nc.alloc_sbuf_tensor("s_" + name, shape, dt).ap()

    def dbg_dump(name, ap, tiles, shape, dt=F32):
        if name not in dbg:
            return
        o = nc.dram_tensor("dbg_" + name, shape, dt, kind="ExternalOutput").ap()
        P.dma("sp", o, ap, reads=tiles, is_output=True)
        dbg_out[name] = o

    RX = sb("RX", [128, 16384], F32)
    RH = sb("RH", [128, 8, SEQ], BF16)
    RY = sb("RY", [128, 24576], BF16)
    WP = [sb("WP%d" % i, [128, 2048], BF16) for i in range(3)]
    WPt = [T() for _ in range(3)]
    GW = [sb("GW%d" % i, [128, 4, 128], BF16) for i in range(2)]
    GWt = [T() for _ in range(2)]
    wp_i = [0]
    gw_i = [0]
    CM = sb("CM", [128, len(CONST_NAMES), 128], F32)
    CMt = T()
    cm = {n: CM[:, i, :] for i, n in enumerate(CONST_NAMES)}
    ones_bf = sb("ones_bf", [128, 128], BF16)
    ident_bf = sb("ident_bf", [128, 128], BF16)
    CBt = T()
    mskA = sb("mskA", [128, 2, 128], F32)
    mskB = sb("mskB", [128, 2, 128], F32)
    rgv = sb("rgv", [128, NLAYER * 12, 11], F32)
    rgd = sb("rgd", [128, NLAYER * 12, 10], F32)
    rgtmp = sb("rgtmp", [128, NLAYER * 12, 2], F32)
    dncw = sb("dncw", [128, NLAYER * 24, 4], F32)
    dnv = sb("dnv", [128, NLAYER, 2, 16], F32)
    nega = sb("nega", [128, NLAYER, 16], F32)
    dnnorm = sb("dnnorm", [128, NLAYER], F32)
    norms = sb("norms", [128, NLAYER * 2 * 8], F32)
    fnorm = sb("fnorm", [128, 8], F32)
    VEC = T()
    QGbuf = sb("QGbuf", [128, 4096], BF16)
    SMbuf = sb("SMbuf", [128, 512 + 8 * 256], F32)
    YQ = [[sb("YQ%d_%d" % (sl, j), [128, 2, 256], F32) for j in range(2)] for sl in range(2)]
    YTb = [[sb("YT%d_%d" % (sl, j), [128, 2, 128], F32) for j in range(2)] for sl in range(2)]
    KQA = [sb("KQA%d" % sl, [128, 2, 128], BF16) for sl in range(2)]
    KQB = [sb("KQB%d" % sl, [128, 2, 128], BF16) for sl in range(2)]
    YQt = [[T() for _ in range(2)] for _ in range(4)]
    YTt = [[T() for _ in range(2)] for _ in range(4)]
    KQAt = [T() for _ in range(4)]
    KQBt = [T() for _ in range(4)]
    Dgt = [T() for _ in range(4)]
    D2t = [T() for _ in range(4)]

    PS = nc.alloc_psum_tensor("PS", [128, 4096], F32).ap()
    BK = [T(ps=True) for _ in range(8)]

    def bank(b, n=1):
        return PS[:, b * 512:(b + n) * 512]

    P.dma("sp", CM, consts_d, writes=[CMt])
    for dst, src in ((rgv, rgv_d), (dncw, dncw_d), (dnv, dnv_d), (dnnorm, dnnorm_d), (norms, norms_d), (fnorm, fnorm_d)):
        P.dma("sp", dst, src, writes=[VEC])
    P.op("dve", lambda e: e.tensor_copy(out=ones_bf, in_=cm["ones"]), reads=[CMt], writes=[CBt])
    P.op("dve", lambda e: e.tensor_copy(out=ident_bf, in_=cm["ident"]), reads=[CMt], writes=[CBt])
    P.op("dve", lambda e: e.tensor_copy(out=mskA[:, 0, :], in_=cm["nsf"]), reads=[CMt], writes=[CBt])
    P.op("dve", lambda e: e.tensor_copy(out=mskA[:, 1, :], in_=cm["inf"]), reads=[CMt], writes=[CBt])
    P.op("dve", lambda e: e.tensor_copy(out=mskB[:, 0, :], in_=cm["nsb"]), reads=[CMt], writes=[CBt])
    P.op("dve", lambda e: e.tensor_copy(out=mskB[:, 1, :], in_=cm["inb"]), reads=[CMt], writes=[CBt])
    P.op("act", lambda e: e.mul(out=rgd[:, :, 0:4], in_=rgv[:, :, 5:9], mul=0.5), reads=[VEC], writes=[VEC])
    P.op("act", lambda e: e.activation(out=rgtmp, in_=rgv[:, :, 9:11], func=AF.Exp, scale=-1.0), reads=[VEC], writes=[VEC])
    P.op("act", lambda e: e.activation(out=rgtmp, in_=rgtmp, func=AF.Ln, bias=1.0), reads=[VEC], writes=[VEC])
    P.op("act", lambda e: e.mul(out=rgd[:, :, 4:6], in_=rgtmp, mul=-4.0), reads=[VEC], writes=[VEC])
    P.op("act", lambda e: e.mul(out=rgd[:, :, 6:8], in_=rgtmp, mul=-8.0), reads=[VEC], writes=[VEC])
    P.op("act", lambda e: e.mul(out=rgd[:, :, 8:10], in_=rgtmp, mul=4.0), reads=[VEC], writes=[VEC])
    P.op("act", lambda e: e.activation(out=nega, in_=dnv[:, :, 0, :], func=AF.Exp), reads=[VEC], writes=[VEC])
    P.op("dve", lambda e: e.tensor_scalar(out=nega, in0=nega, scalar1=-1.0, scalar2=None, op0=ALU.mult), reads=[VEC], writes=[VEC])

    def load_w(src_ap, shape):
        i = wp_i[0]
        wp_i[0] = (i + 1) % len(WP)
        n = int(np.prod(shape[1:]))
        view = WP[i][:, 0:n].rearrange("p (a b) -> p a b", a=shape[1])
        P.dma("pool", view, src_ap, writes=[WPt[i]])
        return view, WPt[i]

    def proj(pb, ptiles, wv, wt, rhs_fn, rhs_tiles, nk, ncols=SEQ):
        nt = ncols // 512
        for tt in range(nt):
            for k in range(nk):
                P.op("pe", lambda e, tt=tt, k=k: e.matmul(pb[:, tt * 512:(tt + 1) * 512], lhsT=wv[:, k, :], rhs=rhs_fn(k, tt),
                                                            start=(k == 0), stop=(k == nk - 1)),
                     reads=[wt] + rhs_tiles, writes=[ptiles[tt]])

    QGt = [T(), T()]
    SM = {nm: T() for nm in ["bt", "lnb", "g", "gc", "colE", "colC", "colD", "egl", "tmp"]}
    Xv = RX.rearrange("p (c t) -> p c t", c=8)
    Xt = [T() for _ in range(8)]
    Ht = [T() for _ in range(8)]
    RYt = [T() for _ in range(12)]
    RYb = RY.rearrange("p (c t) -> p c t", c=12)
    RYf = RY.bitcast(F32)
    hfn = lambda k, tt: RH[:, k, tt * 512:(tt + 1) * 512]
    pgrp = [0]

    def grp():
        g_ = pgrp[0]
        pgrp[0] ^= 1
        return bank(4 * g_, 4), BK[4 * g_:4 * g_ + 4]

    def rms_stats(scale):
        tmp_t = [T() for _ in range(10)]
        P.alias(tmp_t, RYt)
        lnv = RYf[:, 8192:10240]
        rstd = RYf[:, 10240:12288]
        for c in range(8):
            P.op("act", lambda e, c=c: e.activation(out=RYb[:, c, :], in_=Xv[:, c, :], func=AF.Square), reads=[Xt[c]], writes=[tmp_t[c]])
        for tt in range(4):
            for c in range(8):
                P.op("pe", lambda e, c=c, tt=tt: e.matmul(bank(tt), lhsT=ones_bf, rhs=RYb[:, c, tt * 512:(tt + 1) * 512],
                                                            start=(c == 0), stop=(c == 7)),
                     reads=[CBt, tmp_t[c]], writes=[BK[tt]])
        P.op("act", lambda e: e.activation(out=lnv, in_=bank(0, 4), func=AF.Ln, bias=EPS, scale=scale), reads=BK[0:4], writes=[tmp_t[8]])
        P.op("act", lambda e: e.activation(out=rstd, in_=lnv, func=AF.Exp, scale=-0.5), reads=[tmp_t[8]], writes=[tmp_t[9]])
        return rstd, tmp_t

    rxp = RX[:, 0:2052]
    F0 = RX[:, 2052:4100]
    F1 = RX[:, 4100:6148]
    F2 = RX[:, 6148:8196]
    HF = RX[:, 8196:10244]
    HB = RX[:, 10244:12292]
    ub = RX[:, 12292:13316].bitcast(BF16)
    thi = RX[:, 13316:14340].bitcast(BF16)
    sqv = RX[:, 14340:15364].bitcast(BF16)
    QK = RX[:, 8196:10244].bitcast(BF16).rearrange("p (c two d) -> p c two d", c=16, two=2)
    QKf = RX[:, 8196:10244].bitcast(BF16).rearrange("p (c n) -> p c n", c=16)
    ktok = RX[:, 10244:11268].bitcast(BF16).rearrange("p (c d) -> p c d", c=16)
    vtok = RX[:, 11268:12292].bitcast(BF16).rearrange("p (c d) -> p c d", c=16)
    KDB = RX[:, 12292:14340].bitcast(BF16).rearrange("p (two c d) -> p two c d", two=2, c=16)
    tb = 14340
    S32 = [RX[:, tb:tb + 128], RX[:, tb + 128:tb + 256]]
    Sbf = [RX[:, tb + 256:tb + 320].bitcast(BF16), RX[:, tb + 320:tb + 384].bitcast(BF16)]
    rbf = [RX[:, tb + 384:tb + 448].bitcast(BF16), RX[:, tb + 448:tb + 512].bitcast(BF16)]
    xbf = [RX[:, tb + 512:tb + 576].bitcast(BF16), RX[:, tb + 576:tb + 640].bitcast(BF16)]
    NS = 2
    Dg = [RX[:, tb + 640:tb + 896].rearrange("p (a b) -> p a b", a=2), RX[:, tb + 896:tb + 1152].rearrange("p (a b) -> p a b", a=2)]
    D2 = [RX[:, tb + 1152:tb + 1280].bitcast(BF16).rearrange("p (a b) -> p a b", a=2),
          RX[:, tb + 1280:tb + 1408].bitcast(BF16).rearrange("p (a b) -> p a b", a=2)]
    for b0 in (2052, 4228):
        YQ.append([RX[:, b0:b0 + 512].rearrange("p (a b) -> p a b", a=2), RX[:, b0 + 512:b0 + 1024].rearrange("p (a b) -> p a b", a=2)])
        YTb.append([RX[:, b0 + 1024:b0 + 1280].rearrange("p (a b) -> p a b", a=2), RX[:, b0 + 1280:b0 + 1536].rearrange("p (a b) -> p a b", a=2)])
        Dg.append(RX[:, b0 + 1536:b0 + 1792].rearrange("p (a b) -> p a b", a=2))
        KQA.append(RX[:, b0 + 1792:b0 + 1920].bitcast(BF16).rearrange("p (a b) -> p a b", a=2))
        KQB.append(RX[:, b0 + 1920:b0 + 2048].bitcast(BF16).rearrange("p (a b) -> p a b", a=2))
        D2.append(RX[:, b0 + 2048:b0 + 2176].bitcast(BF16).rearrange("p (a b) -> p a b", a=2))
    YQr = [[a_.bitcast(F32R) for a_ in sl_] for sl_ in YQ[0:NS]]
    YTr = [[a_.bitcast(F32R) for a_ in sl_] for sl_ in YTb[0:NS]]
    Dgq = [RX[:, tb + 1408:tb + 1536], RX[:, tb + 1536:tb + 1664]]
    QG = QGbuf.rearrange("p (two c d) -> p two c d", two=2, c=16)
    RYtail = RY[:, 16384:24576]
    qsb, ksb, vsb, sqb = (RYtail[:, 0:2048], RYtail[:, 2048:4096], RYtail[:, 4096:6144], RYtail[:, 6144:8192])
    T2T = RYtail[:, 0:4096].rearrange("p (c two d) -> p c two d", c=16, two=2)
    ATT = RYtail[:, 4096:8192].rearrange("p (c two d) -> p c two d", c=16, two=2)
    OT = RX[:, 0:2048]
    bt = SMbuf[:, 0:512].rearrange("p (c k) -> p c k", c=16)
    smv = lambda i: SMbuf[:, 512 + 256 * i:512 + 256 * (i + 1)].rearrange("p (c k) -> p c k", c=16)
    lnb, gg, gc, colE, colC, colD, egl, stmp = [smv(i) for i in range(8)]

    for s in range(NSEQ):
        for c in range(8):
            P.dma("sp", Xv[:, c, :], xT[s, :, c, :], writes=[Xt[c]])

        for l in range(NL):
            rstd, tmp_t = rms_stats(1.0 / D)
            for c in range(8):
                g = norms[:, (l * 2 + 0) * 8 + c:(l * 2 + 0) * 8 + c + 1]
                P.op("dve", lambda e, c=c, g=g: e.scalar_tensor_tensor(out=RH[:, c, :], in0=Xv[:, c, :], scalar=g, in1=rstd,
                                                                         op0=ALU.mult, op1=ALU.mult),
                     reads=[Xt[c], tmp_t[9], VEC], writes=[Ht[c]])
            P.alias(RYt, tmp_t)
            if s == 0 and l == 0:
                dbg_dump("h", RH, Ht, [128, 8, SEQ], BF16)
            for c in range(8):
                P.dma("sp", xspill[:, c, :], Xv[:, c, :], reads=[Xt[c]], writes=[XSt[c]])
            W = {k_: T() for k_ in ["rxp", "F0", "F1", "F2", "HF", "HB", "ub", "thi", "sq"]}
            P.alias(list(W.values()), Xt)
            W["F1b"] = T()
            W["F2b"] = T()
            P.alias([W["F1b"]], QGt)
            P.alias([W["F2b"]], list(SM.values()))
            F1d = [F1, QGbuf.bitcast(F32)]
            F2d = [F2, SMbuf[:, 0:2048]]
            F1k = ["F1", "F1b"]
            F2k = ["F2", "F2b"]
            P.op("pool", lambda e: e.memset(rxp[:, 0:2], 0.0), writes=[W["rxp"]])
            P.op("pool", lambda e: e.memset(rxp[:, 2050:2052], 0.0), writes=[W["rxp"]])

            for n in range(rg_blocks):
                ln_ = l * 12 + n
                wv, wt = load_w(win[l, :, :, n * 128:(n + 1) * 128], [128, 8, 128])
                gi = gw_i[0]
                gw_i[0] ^= 1
                P.dma("pool", GW[gi], gatew[l, n], writes=[GWt[gi]])
                pb, pt = grp()
                proj(pb, pt, wv, wt, hfn, Ht, 8)
                P.op("act", lambda e, pb=pb: e.copy(out=rxp[:, 2:2050], in_=pb), reads=pt, writes=[W["rxp"]])
                cw = lambda j, ln_=ln_: rgv[:, ln_, j:j + 1]
                P.op("dve", lambda e, cw=cw: e.tensor_scalar(out=F0, in0=rxp[:, 0:2048], scalar1=cw(0), scalar2=cw(4), op0=ALU.mult, op1=ALU.add),
                     reads=[W["rxp"], VEC], writes=[W["F0"]])
                P.op("dve", lambda e, cw=cw: e.scalar_tensor_tensor(out=F1, in0=rxp[:, 1:2049], scalar=cw(1), in1=F0, op0=ALU.mult, op1=ALU.add),
                     reads=[W["rxp"], VEC, W["F0"]], writes=[W["F1"]])
                P.op("dve", lambda e, cw=cw: e.scalar_tensor_tensor(out=F0, in0=rxp[:, 2:2050], scalar=cw(2), in1=F1, op0=ALU.mult, op1=ALU.add),
                     reads=[W["rxp"], VEC, W["F1"]], writes=[W["F0"]])
                P.op("dve", lambda e, cw=cw: e.scalar_tensor_tensor(out=ub, in0=rxp[:, 3:2051], scalar=cw(3), in1=F0, op0=ALU.mult, op1=ALU.add),
                     reads=[W["rxp"], VEC, W["F0"]], writes=[W["ub"]])
                ufn = lambda k, tt: ub[:, tt * 512:(tt + 1) * 512]
                gv = GW[gi]
                for d in range(2):
                    pb, pt = grp()
                    proj(pb, pt, gv[:, d:d + 1, :], GWt[gi], ufn, [W["ub"]], 1)
                    P.op("act", lambda e, pb=pb, d=d, ln_=ln_: e.activation(out=F0, in_=pb, func=AF.Tanh, scale=0.5, bias=rgd[:, ln_, d:d + 1]),
                         reads=pt + [VEC], writes=[W["F0"]])
                    pb, pt = grp()
                    proj(pb, pt, gv[:, 2 + d:3 + d, :], GWt[gi], ufn, [W["ub"]], 1)
                    P.op("act", lambda e, pb=pb, d=d, ln_=ln_: e.activation(out=thi, in_=pb, func=AF.Tanh, scale=0.5, bias=rgd[:, ln_, 2 + d:3 + d]),
                         reads=pt + [VEC], writes=[W["thi"]])
                    F1x, F2x, k1, k2 = F1d[d], F2d[d], F1k[d], F2k[d]
                    P.op("act", lambda e, d=d, ln_=ln_, F1x=F1x: e.activation(out=F1x, in_=F0, func=AF.Exp, scale=rgd[:, ln_, 4 + d:5 + d], bias=rgd[:, ln_, 4 + d:5 + d]),
                         reads=[W["F0"], VEC], writes=[W[k1]])
                    P.op("act", lambda e, d=d, ln_=ln_, F2x=F2x: e.activation(out=F2x, in_=F0, func=AF.Tanh, scale=rgd[:, ln_, 8 + d:9 + d], bias=rgd[:, ln_, 8 + d:9 + d]),
                         reads=[W["F0"], VEC], writes=[W[k2]])
                    P.op("act", lambda e, d=d, ln_=ln_: e.activation(out=F0, in_=F0, func=AF.Exp, scale=rgd[:, ln_, 6 + d:7 + d], bias=rgd[:, ln_, 6 + d:7 + d]),
                         reads=[W["F0"], VEC], writes=[W["F0"]])
                    P.op("dve", lambda e, F2x=F2x: e.scalar_tensor_tensor(out=F0, in0=F0, scalar=1.0, in1=F2x, op0=ALU.add, op1=ALU.mult),
                         reads=[W["F0"], W[k2]], writes=[W["F0"]])
                    P.op("act", lambda e: e.activation(out=F0, in_=F0, func=AF.Ln), reads=[W["F0"]], writes=[W["F0"]])
                    P.op("act", lambda e: e.activation(out=sqv, in_=F0, func=AF.Exp, scale=0.5), reads=[W["F0"]], writes=[W["sq"]])
                    P.op("dve", lambda e, F2x=F2x: e.scalar_tensor_tensor(out=F2x, in0=thi, scalar=1.0, in1=ub, op0=ALU.add, op1=ALU.mult),
                         reads=[W["thi"], W["ub"]], writes=[W[k2]])
                    P.op("dve", lambda e, F2x=F2x: e.scalar_tensor_tensor(out=F2x, in0=F2x, scalar=0.5, in1=sqv, op0=ALU.mult, op1=ALU.mult),
                         reads=[W[k2], W["sq"]], writes=[W[k2]])
                    if d == 0:
                        P.op("dve", lambda e, F1x=F1x, F2x=F2x: e.tensor_tensor_scan(out=HF, data0=F1x, data1=F2x, initial=0.0, op0=ALU.mult, op1=ALU.add),
                             reads=[W[k1], W[k2]], writes=[W["HF"]])
                    else:
                        P.op("dve", lambda e, F1x=F1x, F2x=F2x: e.tensor_tensor_scan(out=HB[:, ::-1], data0=F1x[:, ::-1], data1=F2x[:, ::-1], initial=0.0,
                                                                     op0=ALU.mult, op1=ALU.add),
                             reads=[W[k1], W[k2]], writes=[W["HB"]])
                wv, wt = load_w(win[l, :, :, C_RY + n * 128:C_RY + (n + 1) * 128], [128, 8, 128])
                pb, pt = grp()
                proj(pb, pt, wv, wt, hfn, Ht, 8)
                P.op("act", lambda e, pb=pb: e.activation(out=F0, in_=pb, func=AF.Square), reads=pt, writes=[W["F0"]])
                P.op("dve", lambda e: e.tensor_scalar(out=F0, in0=F0, scalar1=0.044715, scalar2=1.0, op0=ALU.mult, op1=ALU.add),
                     reads=[W["F0"]], writes=[W["F0"]])
                P.op("dve", lambda e, pb=pb: e.tensor_tensor(out=F0, in0=F0, in1=pb, op=ALU.mult), reads=[W["F0"]] + pt, writes=[W["F0"]])
                P.op("act", lambda e: e.activation(out=F1, in_=F0, func=AF.Tanh, scale=0.7978845608028654), reads=[W["F0"]], writes=[W["F1"]])
                P.op("dve", lambda e: e.tensor_tensor(out=F2, in0=HF, in1=HB, op=ALU.add), reads=[W["HF"], W["HB"]], writes=[W["F2"]])
                P.op("dve", lambda e, pb=pb: e.scalar_tensor_tensor(out=F0, in0=F1, scalar=1.0, in1=pb, op0=ALU.add, op1=ALU.mult),
                     reads=[W["F1"]] + pt, writes=[W["F0"]])
                P.op("dve", lambda e, n=n: e.scalar_tensor_tensor(out=RYb[:, n, :], in0=F0, scalar=0.5, in1=F2, op0=ALU.mult, op1=ALU.mult),
                     reads=[W["F0"], W["F2"]], writes=[RYt[n]])
            if s == 0 and l == 0:
                dbg_dump("yr", RYb, RYt, [128, 12, SEQ], BF16)
            if stop_after == "rg":
                raise _Stop()

            P.alias(QGt, [W["F1b"]])
            P.alias(list(SM.values()), [W["F2b"]])
            stg = [RX[:, 8196:9220].bitcast(BF16), RX[:, 9220:10244].bitcast(BF16)]
            stg_t = [T(), T()]
            P.alias(stg_t, [W["HF"]])
            yfn = lambda k, tt: RYb[:, k, tt * 512:(tt + 1) * 512]
            for n in range(8):
                wv, wt = load_w(wrnn[l, :, :, n * 128:(n + 1) * 128], [128, 12, 128])
                pb, pt = grp()
                proj(pb, pt, wv, wt, yfn, RYt, 12)
                wv2, wt2 = load_w(win[l, :, :, C_GR + n * 128:C_GR + (n + 1) * 128], [128, 8, 128])
                pb2, pt2 = grp()
                proj(pb2, pt2, wv2, wt2, hfn, Ht, 8)
                P.op("act", lambda e, pb2=pb2: e.activation(out=F0, in_=pb2, func=AF.Tanh, scale=0.5), reads=pt2, writes=[W["F0"]])
                si = n % 2
                P.op("dve", lambda e, pb=pb, si=si: e.scalar_tensor_tensor(out=stg[si], in0=F0, scalar=1.0, in1=pb, op0=ALU.add, op1=ALU.mult),
                     reads=[W["F0"]] + pt, writes=[stg_t[si]])
                P.dma("sp", mixspill[n], stg[si], reads=[stg_t[si]], writes=[MSt[n]])

            DN = {nm: T() for nm in ["QK", "ktok", "vtok", "KDB", "S0", "S1", "Sb0", "Sb1", "r0", "r1", "x0", "x1",
                                     "Dgq0", "Dgq1"]}
            P.alias(list(DN.values()) + Dgt[0:2] + D2t[0:2], [W["HF"], W["HB"], W["ub"], W["thi"], W["sq"]] + stg_t)
            DR = {nm: T() for nm in ["qs", "ks", "vs", "sqb"]}
            P.alias(list(DR.values()), RYt[8:12])
            T2Tt = [T() for _ in range(16)]
            ATTt = [T() for _ in range(16)]
            OTt = [T() for _ in range(16)]
            wv, wt = load_w(win[l, :, :, C_BA:C_BA + 32], [128, 8, 32])
            pbt = bank(0).rearrange("p (c k) -> p c k", c=16)
            for c in range(16):
                for k in range(8):
                    P.op("pe", lambda e, c=c, k=k: e.matmul(pbt[:, c, :], lhsT=RH[:, k, c * 128:(c + 1) * 128], rhs=wv[:, k, :],
                                                              start=(k == 0), stop=(k == 7)),
                         reads=[Ht[k], wt], writes=[BK[0]])
            P.op("act", lambda e: e.copy(out=bt, in_=pbt), reads=[BK[0]], writes=[SM["bt"]])
            P.op("act", lambda e: e.activation(out=lnb, in_=bt[:, :, 0:16], func=AF.Exp, scale=-1.0), reads=[SM["bt"]], writes=[SM["lnb"]])
            P.op("act", lambda e: e.activation(out=lnb, in_=lnb, func=AF.Ln, bias=1.0), reads=[SM["lnb"]], writes=[SM["lnb"]])
            P.op("dve", lambda e: e.tensor_scalar(out=lnb, in0=lnb, scalar1=-1.0, scalar2=None, op0=ALU.mult), reads=[SM["lnb"]], writes=[SM["lnb"]])
            dtb = dnv[:, l, 1, :].unsqueeze(1).to_broadcast([128, 16, 16])
            ngb = nega[:, l, :].unsqueeze(1).to_broadcast([128, 16, 16])
            P.op("dve", lambda e: e.tensor_tensor(out=gg, in0=bt[:, :, 16:32], in1=dtb, op=ALU.add), reads=[SM["bt"], VEC], writes=[SM["g"]])
            P.op("act", lambda e: e.activation(out=gg, in_=gg, func=AF.Exp), reads=[SM["g"]], writes=[SM["g"]])
            P.op("act", lambda e: e.activation(out=gg, in_=gg, func=AF.Ln, bias=1.0), reads=[SM["g"]], writes=[SM["g"]])
            P.op("dve", lambda e: e.tensor_tensor(out=gg, in0=gg, in1=ngb, op=ALU.mult), reads=[SM["g"], VEC], writes=[SM["g"]])
            pgc = bank(1)[:, 0:256].rearrange("p (c k) -> p c k", c=16)
            pgl = bank(1)[:, 256:512].rearrange("p (c k) -> p c k", c=16)
            P.op("pe", lambda e: e.matmul(pgc[:, :, 0:8], lhsT=cm["Lf"], rhs=gg[:, :, 0:8], start=True, stop=True), reads=[CMt, SM["g"]], writes=[BK[1]])
            P.op("pe", lambda e: e.matmul(pgc[:, :, 8:16], lhsT=cm["Lb"], rhs=gg[:, :, 8:16], start=True, stop=True), reads=[CMt, SM["g"]], writes=[BK[1]])
            P.op("pe", lambda e: e.matmul(pgl, lhsT=cm["ones"], rhs=gg, start=True, stop=True), reads=[CMt, SM["g"]], writes=[BK[1]])
            P.op("act", lambda e: e.copy(out=gc, in_=pgc), reads=[BK[1]], writes=[SM["gc"]])
            P.op("act", lambda e: e.activation(out=colE, in_=pgc, func=AF.Exp), reads=[BK[1]], writes=[SM["colE"]])
            P.op("act", lambda e: e.activation(out=egl, in_=pgl, func=AF.Exp), reads=[BK[1]], writes=[SM["egl"]])
            P.op("dve", lambda e: e.tensor_scalar(out=colC, in0=colE, scalar1=-1.0, scalar2=None, op0=ALU.mult), reads=[SM["colE"]], writes=[SM["colC"]])
            P.op("dve", lambda e: e.tensor_tensor(out=stmp, in0=pgl, in1=gc, op=ALU.subtract), reads=[BK[1], SM["gc"]], writes=[SM["tmp"]])
            P.op("dve", lambda e: e.tensor_tensor(out=stmp, in0=stmp, in1=lnb, op=ALU.add), reads=[SM["tmp"], SM["lnb"]], writes=[SM["tmp"]])
            P.op("act", lambda e: e.activation(out=colD, in_=stmp, func=AF.Exp), reads=[SM["tmp"]], writes=[SM["colD"]])
            if s == 0 and l == 0:
                dbg_dump("sm", SMbuf, list(SM.values()), [128, 512 + 8 * 256], F32)
            if stop_after == "sm":
                raise _Stop()

            p1v = [bank(sl).rearrange("p (g n) -> p g n", g=2) for sl in range(NS)]
            p2f = [bank(4 + sl)[:, 0:256].rearrange("p (g n) -> p g n", g=2) for sl in range(NS)]
            bA = bank(0, 4)
            bAt = BK[0:4]

            for hd in range(heads):
                dh = [hd, 8 + hd]
                for blk, dst, dkey in ((hd, qsb, "qs"), (8 + hd, ksb, "ks"), (16 + hd, vsb, "vs")):
                    wv, wt = load_w(win[l, :, :, C_QKV + blk * 128:C_QKV + (blk + 1) * 128], [128, 8, 128])
                    proj(bA, bAt, wv, wt, hfn, Ht, 8)
                    P.op("act", lambda e: e.copy(out=rxp[:, 2:2050], in_=bA), reads=bAt, writes=[W["rxp"]] + OTt)
                    cw = lambda j, r_=l * 24 + blk: dncw[:, r_, j:j + 1]
                    P.op("dve", lambda e, cw=cw: e.tensor_scalar(out=F0, in0=rxp[:, 0:2048], scalar1=cw(0), scalar2=None, op0=ALU.mult),
                         reads=[W["rxp"], VEC], writes=[W["F0"]])
                    P.op("dve", lambda e, cw=cw: e.scalar_tensor_tensor(out=F1, in0=rxp[:, 1:2049], scalar=cw(1), in1=F0, op0=ALU.mult, op1=ALU.add),
                         reads=[W["rxp"], VEC, W["F0"]], writes=[W["F1"]])
                    P.op("dve", lambda e, cw=cw: e.scalar_tensor_tensor(out=F0, in0=rxp[:, 2:2050], scalar=cw(2), in1=F1, op0=ALU.mult, op1=ALU.add),
                         reads=[W["rxp"], VEC, W["F1"]], writes=[W["F0"]])
                    P.op("dve", lambda e, cw=cw: e.scalar_tensor_tensor(out=F1, in0=rxp[:, 3:2051], scalar=cw(3), in1=F0, op0=ALU.mult, op1=ALU.add),
                         reads=[W["rxp"], VEC, W["F0"]], writes=[W["F1"]])
                    P.op("act", lambda e: e.activation(out=F2, in_=F1, func=AF.Tanh, scale=0.5), reads=[W["F1"]], writes=[W["F2"]])
                    P.op("dve", lambda e, dst=dst: e.scalar_tensor_tensor(out=dst, in0=F2, scalar=1.0, in1=F1, op0=ALU.add, op1=ALU.mult),
                         reads=[W["F2"], W["F1"]], writes=[DR[dkey]] + T2Tt + ATTt)
                for src, skey, slot, scl in ((ksb, "ks", 0, 1.0), (qsb, "qs", 1, 128.0 ** -0.5)):
                    P.op("act", lambda e, src=src: e.activation(out=sqb, in_=src, func=AF.Square), reads=[DR[skey]], writes=[DR["sqb"]])
                    for tt in range(4):
                        P.op("pe", lambda e, tt=tt: e.matmul(bank(tt), lhsT=ones_bf, rhs=sqb[:, tt * 512:(tt + 1) * 512], start=True, stop=True),
                             reads=[CBt, DR["sqb"]], writes=[BK[tt]])
                    P.op("act", lambda e: e.activation(out=F0, in_=bA, func=AF.Ln, bias=4.0 * EPS), reads=bAt, writes=[W["F0"]])
                    P.op("act", lambda e: e.activation(out=F0, in_=F0, func=AF.Exp, scale=-0.5), reads=[W["F0"]], writes=[W["F0"]])
                    P.op("dve", lambda e, src=src, slot=slot, scl=scl: e.scalar_tensor_tensor(
                        out=QK[:, :, slot, :], in0=src.rearrange("p (c d) -> p c d", c=16), scalar=scl,
                        in1=F0.rearrange("p (c d) -> p c d", c=16), op0=ALU.mult, op1=ALU.mult),
                         reads=[DR[skey], W["F0"]], writes=[DN["QK"]])
                ptr = bank(0, 2).bitcast(BF16).rearrange("p (c d) -> p c d", c=16)
                ptr2 = bank(2, 2).bitcast(BF16).rearrange("p (c d) -> p c d", c=16)
                for c in range(16):
                    P.op("pe", lambda e, c=c: e.transpose(ptr[:, c, :], QK[:, c, 0, :], ident_bf), reads=[DN["QK"], CBt], writes=[BK[0], BK[1]])
                P.op("act", lambda e: e.copy(out=ktok, in_=ptr), reads=[BK[0], BK[1]], writes=[DN["ktok"]])
                for c in range(16):
                    P.op("pe", lambda e, c=c: e.transpose(ptr2[:, c, :], vsb[:, c * 128:(c + 1) * 128], ident_bf), reads=[DR["vs"], CBt], writes=[BK[2], BK[3]])
                P.op("act", lambda e: e.mul(out=vtok, in_=ptr2, mul=0.5), reads=[BK[2], BK[3]], writes=[DN["vtok"]])
                for d in range(2):
                    cdb = colD[:, :, dh[d]:dh[d] + 1].to_broadcast([128, 16, 128])
                    P.op("dve", lambda e, d=d, cdb=cdb: e.tensor_tensor(out=KDB[:, d], in0=ktok, in1=cdb, op=ALU.mult),
                         reads=[DN["ktok"], SM["colD"]], writes=[DN["KDB"]])
                    for c4 in range(4):
                        prb = bank(3).rearrange("p (c d) -> p c d", c=4)
                        for cc in range(4):
                            c = c4 * 4 + cc
                            j = c % 2
                            P.op("pool", lambda e, c=c, d=d, j=j: e.tensor_scalar(out=Dgq[j], in0=cm["ident"], scalar1=colE[:, c, dh[d]:dh[d] + 1],
                                                                                   scalar2=None, op0=ALU.mult),
                                 reads=[CMt, SM["colE"]], writes=[DN["Dgq%d" % j]])
                            P.op("pe", lambda e, cc=cc, j=j: e.matmul(prb[:, cc, :], lhsT=cm["ones"], rhs=Dgq[j], start=True, stop=True),
                                 reads=[CMt, DN["Dgq%d" % j]], writes=[BK[3]])
                        P.op("dve", lambda e, d=d, c4=c4: e.tensor_tensor(out=QG[:, d, c4 * 4:(c4 + 1) * 4, :], in0=QK[:, c4 * 4:(c4 + 1) * 4, 1, :],
                                                                            in1=prb, op=ALU.mult),
                             reads=[DN["QK"], BK[3]], writes=[QGt[d]])

                if stop_after == "prep":
                    dbg_dump("QK", RX[:, 8196:10244].bitcast(BF16), [DN["QK"]], [128, 4096], BF16)
                    dbg_dump("ktok", RX[:, 10244:11268].bitcast(BF16), [DN["ktok"]], [128, 2048], BF16)
                    dbg_dump("vtok", RX[:, 11268:12292].bitcast(BF16), [DN["vtok"]], [128, 2048], BF16)
                    dbg_dump("KDB", RX[:, 12292:14340].bitcast(BF16), [DN["KDB"]], [128, 4096], BF16)
                    dbg_dump("QG", QGbuf, QGt, [128, 4096], BF16)
                    raise _Stop()
                def unit_setup(sl, c):
                    P.op("pe", lambda e: e.matmul(p1v[sl][:, 0, :], lhsT=QK[:, c, 0, :], rhs=QKf[:, c, :], start=True, stop=True),
                         reads=[DN["QK"]], writes=[BK[sl]])
                    kq = p1v[sl][:, 0, :].rearrange("p (a b) -> p a b", a=2)
                    P.op("dve", lambda e: e.tensor_tensor(out=KQA[sl], in0=kq, in1=mskA, op=ALU.mult), reads=[BK[sl], CBt], writes=[KQAt[sl]])
                    P.op("dve", lambda e: e.tensor_tensor(out=KQB[sl], in0=kq, in1=mskB, op=ALU.mult), reads=[BK[sl], CBt], writes=[KQBt[sl]])
                    for d in range(2):
                        Lm = cm["Lf"] if d == 0 else cm["Lb"]
                        Sm = cm["SUf"] if d == 0 else cm["SUb"]
                        kq_ = KQA[sl] if d == 0 else KQB[sl]
                        kqt = KQAt[sl] if d == 0 else KQBt[sl]
                        pe_ = p1v[sl][:, 1, d * 128:(d + 1) * 128]
                        P.op("pool", lambda e, d=d, Lm=Lm: e.tensor_scalar(out=Dg[sl][:, d, :], in0=Lm, scalar1=gg[:, c, dh[d]:dh[d] + 1], scalar2=None, op0=ALU.mult),
                             reads=[CMt, SM["g"]], writes=[Dgt[sl]])
                        P.op("pe", lambda e, d=d, Sm=Sm, pe_=pe_: e.matmul(pe_, lhsT=Sm, rhs=Dg[sl][:, d, :], start=True, stop=True),
                             reads=[CMt, Dgt[sl]], writes=[BK[sl]])
                        P.op("act", lambda e, d=d, pe_=pe_: e.activation(out=D2[sl][:, d, :], in_=pe_, func=AF.Exp, bias=lnb[:, c, dh[d]:dh[d] + 1]),
                             reads=[BK[sl], SM["lnb"]], writes=[D2t[sl]])
                        P.op("dve", lambda e, d=d, kq_=kq_: e.tensor_tensor(out=YQr[sl][0][:, d, 0:128], in0=kq_[:, 0, :], in1=D2[sl][:, d, :], op=ALU.mult),
                             reads=[kqt, D2t[sl]], writes=[YQt[sl][0]])
                        P.op("dve", lambda e, d=d, kq_=kq_: e.tensor_tensor(out=ATT[:, c, d, :], in0=kq_[:, 1, :], in1=D2[sl][:, d, :], op=ALU.mult),
                             reads=[kqt, D2t[sl]], writes=[ATTt[c], DR["vs"], DR["sqb"]])
                        P.op("pe", lambda e, d=d: e.transpose(p2f[sl][:, d, :], YQ[sl][0][:, d, 0:128], cm["ident"]),
                             reads=[YQt[sl][0], CMt], writes=[BK[4 + sl]])
                    P.op("act", lambda e: e.copy(out=YTr[sl][0], in_=p2f[sl]), reads=[BK[4 + sl]], writes=[YTt[sl][0]])

                def unit_level_mm(sl, c, k):
                    ci = k % 2
                    yq, yt = YQr[sl][ci], YTr[sl][ci]
                    rd = [YQt[sl][ci], YTt[sl][ci]]
                    for g_ in range(2):
                        if k == 0:
                            P.op("pe", lambda e, g_=g_: e.matmul(p1v[sl][:, g_, 0:128], lhsT=yt[:, g_, :], rhs=yq[:, g_, 0:128], start=True, stop=True),
                                 reads=rd, writes=[BK[sl]])
                        elif k < 6:
                            P.op("pe", lambda e, g_=g_: e.matmul(p1v[sl][:, g_, :], lhsT=yt[:, g_, :], rhs=yq[:, g_, :], start=True, stop=True),
                                 reads=rd, writes=[BK[sl]])
                        else:
                            P.op("pe", lambda e, g_=g_: e.matmul(p1v[sl][:, g_, 128:256], lhsT=yt[:, g_, :], rhs=yq[:, g_, 128:256], start=True, stop=True),
                                 reads=rd, writes=[BK[sl]])
                    if k < 6:
                        for g_ in range(2):
                            P.op("pe", lambda e, g_=g_: e.matmul(p2f[sl][:, g_, :], lhsT=yq[:, g_, 0:128], rhs=yt[:, g_, :], start=True, stop=True),
                                 reads=rd, writes=[BK[4 + sl]])

                def unit_level_ev(sl, c, k):
                    ci = k % 2
                    ni = 1 - ci
                    yq = YQ[sl][ci]
                    if k < 6:
                        P.op("act", lambda e: e.copy(out=YQr[sl][ni][:, :, 0:128], in_=p1v[sl][:, :, 0:128]), reads=[BK[sl]], writes=[YQt[sl][ni]])
                        P.op("act", lambda e: e.copy(out=YTr[sl][ni], in_=p2f[sl]), reads=[BK[4 + sl]], writes=[YTt[sl][ni]])
                        if k == 0:
                            idb = cm["ident"].unsqueeze(1).to_broadcast([128, 2, 128])
                            P.op("dve", lambda e: e.tensor_tensor(out=YQr[sl][ni][:, :, 128:256], in0=yq[:, :, 0:128], in1=idb, op=ALU.add),
                                 reads=[YQt[sl][ci], CMt], writes=[YQt[sl][ni]])
                        else:
                            P.op("dve", lambda e: e.tensor_tensor(out=YQr[sl][ni][:, :, 128:256], in0=yq[:, :, 128:256], in1=p1v[sl][:, :, 128:256], op=ALU.add),
                                 reads=[YQt[sl][ci], BK[sl]], writes=[YQt[sl][ni]])
                    else:
                        P.op("dve", lambda e: e.tensor_tensor(out=T2T[:, c, :, :], in0=yq[:, :, 128:256], in1=p1v[sl][:, :, 128:256], op=ALU.add),
                             reads=[YQt[sl][ci], BK[sl]], writes=[T2Tt[c], DR["qs"], DR["ks"]])

                for d in range(2):
                    P.op("pool", lambda e, d=d: e.memset(S32[d], 0.0), writes=[DN["S%d" % d]])
                    P.op("pool", lambda e, d=d: e.memset(Sbf[d], 0.0), writes=[DN["Sb%d" % d]])

                def chain_ops(step, d):
                    c = step if d == 0 else 15 - step
                    bA_ = 2 + 4 * d
                    bB_ = 3 + 4 * d
                    pk = bank(bA_)[:, 0:128]
                    px = bank(bA_)[:, 128:256]
                    po = bank(bB_)[:, 0:128]
                    pd = bank(bB_)[:, 128:256]
                    tA, tB = BK[bA_], BK[bB_]
                    St, Sbt, rt, xt_ = DN["S%d" % d], DN["Sb%d" % d], DN["r%d" % d], DN["x%d" % d]
                    eg = egl[:, c, dh[d]:dh[d] + 1]
                    oc = OT[:, c * 128:(c + 1) * 128]
                    ops = [
                        lambda: P.op("pe", lambda e: e.matmul(pk, lhsT=QK[:, c, 0, :], rhs=Sbf[d], start=True, stop=True), reads=[DN["QK"], Sbt], writes=[tA]),
                        lambda: P.op("dve", lambda e: e.scalar_tensor_tensor(out=rbf[d], in0=pk, scalar=colC[:, c, dh[d]:dh[d] + 1], in1=vtok[:, c, :], op0=ALU.mult, op1=ALU.add),
                                     reads=[tA, SM["colC"], DN["vtok"]], writes=[rt]),
                        lambda: P.op("pe", lambda e: e.matmul(px, lhsT=T2T[:, c, d, :], rhs=rbf[d], start=True, stop=True), reads=[T2Tt[c], rt], writes=[tA]),
                        lambda: P.op("act", lambda e: e.copy(out=xbf[d], in_=px), reads=[tA], writes=[xt_]),
                        lambda: P.op("pe", lambda e: e.matmul(pd, lhsT=KDB[:, d, c, :], rhs=xbf[d], start=True, stop=True), reads=[DN["KDB"], xt_], writes=[tB]),
                        lambda: P.op("pe", lambda e: e.matmul(po, lhsT=Sbf[d], rhs=QG[:, d, c, :], start=True, stop=False), reads=[Sbt, QGt[d]], writes=[tB]),
                        lambda: P.op("pe", lambda e: e.matmul(po, lhsT=xbf[d], rhs=ATT[:, c, d, :], start=False, stop=True), reads=[xt_, ATTt[c]], writes=[tB]),
                        lambda: P.op("dve", lambda e: e.scalar_tensor_tensor(out=Sbf[d], in0=S32[d], scalar=eg, in1=pd, op0=ALU.mult, op1=ALU.add),
                                     reads=[St, SM["egl"], tB], writes=[Sbt]),
                        lambda: P.op("dve", lambda e: e.scalar_tensor_tensor(out=S32[d], in0=S32[d], scalar=eg, in1=pd, op0=ALU.mult, op1=ALU.add),
                                     reads=[St, SM["egl"], tB], writes=[St]),
                    ]
                    if step < 8:
                        ops.append(lambda: P.op("act", lambda e: e.copy(out=oc, in_=po), reads=[tB], writes=[OTt[c], W["rxp"]]))
                    else:
                        ops.append(lambda: P.op("dve", lambda e: e.tensor_tensor(out=oc, in0=oc, in1=po, op=ALU.add), reads=[tB, OTt[c]], writes=[OTt[c]]))
                    return ops

                pending = []

                def emit_some(n):
                    for _ in range(n):
                        if pending:
                            pending.pop(0)()

                def sched(step):
                    a_, b_ = chain_ops(step, 0), chain_ops(step, 1)
                    for i_ in range(len(a_)):
                        pending.append(a_[i_])
                        pending.append(b_[i_])

                for p in range(8):
                    cs = [p, 15 - p]
                    for sl in range(2):
                        unit_setup(sl, cs[sl])
                    for k in range(7):
                        for sl in range(2):
                            unit_level_mm(sl, cs[sl], k)
                        emit_some(2)
                        for sl in range(2):
                            unit_level_ev(sl, cs[sl], k)
                        emit_some(2)
                    emit_some(len(pending))
                    sched(p)
                emit_some(len(pending))
                for step in range(8, 16):
                    sched(step)
                    emit_some(len(pending))

                if stop_after == "chain":
                    dbg_dump("OT", OT, OTt, [128, 2048], F32)
                    raise _Stop()
                P.op("act", lambda e: e.activation(out=sqb, in_=OT, func=AF.Square), reads=OTt, writes=[DR["sqb"]] + ATTt)
                for tt in range(4):
                    P.op("pe", lambda e, tt=tt: e.matmul(bank(tt), lhsT=ones_bf, rhs=sqb[:, tt * 512:(tt + 1) * 512], start=True, stop=True),
                         reads=[CBt, DR["sqb"]], writes=[BK[tt]])
                P.op("act", lambda e: e.activation(out=F0, in_=bA, func=AF.Ln, bias=EPS, scale=1.0 / 128.0), reads=bAt, writes=[W["F0"]])
                P.op("act", lambda e: e.activation(out=F0, in_=F0, func=AF.Exp, scale=-0.5), reads=[W["F0"]], writes=[W["F0"]])
                wv, wt = load_w(win[l, :, :, C_Z + hd * 128:C_Z + (hd + 1) * 128], [128, 8, 128])
                bB = bank(4, 4)
                proj(bB, BK[4:8], wv, wt, hfn, Ht, 8)
                P.op("act", lambda e: e.activation(out=F1, in_=bB, func=AF.Tanh, scale=0.5), reads=BK[4:8], writes=[W["F1"]])
                P.op("dve", lambda e: e.scalar_tensor_tensor(out=F2, in0=F1, scalar=1.0, in1=bB, op0=ALU.add, op1=ALU.mult),
                     reads=[W["F1"]] + BK[4:8], writes=[W["F2"]])
                P.op("dve", lambda e: e.scalar_tensor_tensor(out=F1, in0=OT, scalar=dnnorm[:, l:l + 1], in1=F0, op0=ALU.mult, op1=ALU.mult),
                     reads=OTt + [VEC, W["F0"]], writes=[W["F1"]])
                P.op("dve", lambda e, hd=hd: e.scalar_tensor_tensor(out=RYb[:, hd, :], in0=F1, scalar=0.5, in1=F2, op0=ALU.mult, op1=ALU.mult),
                     reads=[W["F1"], W["F2"]], writes=[RYt[hd]])
            if s == 0 and l == 0:
                dbg_dump("yd", RYb[:, 0:8, :], RYt[0:8], [128, 8, SEQ], BF16)
            if stop_after == "dn":
                raise _Stop()

            MIXB = [RYtail[:, 0:4096].rearrange("p (c t) -> p c t", c=8), RYtail[:, 4096:8192].rearrange("p (c t) -> p c t", c=8)]
            MIXt = [T(), T()]
            P.alias(MIXt, list(DR.values()) + T2Tt + ATTt)
            thg = [QGbuf[:, 0:1024].bitcast(F32), QGbuf[:, 1024:2048].bitcast(F32)]
            mst = [QGbuf[:, 2048:2560], QGbuf[:, 2560:3072]]
            thg_t = [T(), T()]
            mst_t = [T(), T()]
            P.alias(thg_t + mst_t, QGt)
            allwork = list(W.values()) + list(DN.values()) + OTt
            P.alias(Xt, allwork)
            bi = [0]

            def nb():
                b = bi[0]
                bi[0] = (b + 1) % 8
                return bank(b), BK[b]
            for tt in range(4):
                mb, mbt = MIXB[tt % 2], MIXt[tt % 2]
                for n in range(8):
                    wv, wt = load_w(wdn[l, :, :, n * 128:(n + 1) * 128], [128, 8, 128])
                    wv2, wt2 = load_w(win[l, :, :, C_GD + n * 128:C_GD + (n + 1) * 128], [128, 8, 128])
                    py, pyt = nb()
                    pg, pgt = nb()
                    for k in range(8):
                        P.op("pe", lambda e, k=k, py=py, wv=wv: e.matmul(py, lhsT=wv[:, k, :], rhs=RYb[:, k, tt * 512:(tt + 1) * 512], start=(k == 0), stop=(k == 7)),
                             reads=[wt, RYt[k]], writes=[pyt])
                    for k in range(8):
                        P.op("pe", lambda e, k=k, pg=pg, wv2=wv2: e.matmul(pg, lhsT=wv2[:, k, :], rhs=RH[:, k, tt * 512:(tt + 1) * 512], start=(k == 0), stop=(k == 7)),
                             reads=[wt2, Ht[k]], writes=[pgt])
                    j = n % 2
                    P.op("act", lambda e, j=j, pg=pg: e.activation(out=thg[j], in_=pg, func=AF.Tanh, scale=0.5), reads=[pgt], writes=[thg_t[j]])
                    P.dma("sp", mst[j], mixspill[n, :, tt * 512:(tt + 1) * 512], reads=[MSt[n]], writes=[mst_t[j]])
                    P.op("dve", lambda e, j=j, py=py: e.scalar_tensor_tensor(out=thg[j], in0=thg[j], scalar=1.0, in1=py, op0=ALU.add, op1=ALU.mult),
                         reads=[thg_t[j], pyt], writes=[thg_t[j]])
                    P.op("dve", lambda e, j=j, n=n, mb=mb: e.tensor_tensor(out=mb[:, n, :], in0=thg[j], in1=mst[j], op=ALU.add),
                         reads=[thg_t[j], mst_t[j]], writes=[mbt])
                for n in range(8):
                    wv, wt = load_w(wout[l, :, :, n * 128:(n + 1) * 128], [128, 8, 128])
                    po_, pot = nb()
                    for k in range(8):
                        P.op("pe", lambda e, k=k, po_=po_, wv=wv, mb=mb: e.matmul(po_, lhsT=wv[:, k, :], rhs=mb[:, k, :], start=(k == 0), stop=(k == 7)),
                             reads=[wt, mbt], writes=[pot])
                    xs = Xv[:, n, tt * 512:(tt + 1) * 512]
                    P.dma("sp", xs, xspill[:, n, tt * 512:(tt + 1) * 512], reads=[XSt[n]], writes=[Xt[n]])
                    P.op("dve", lambda e, xs=xs, po_=po_: e.scalar_tensor_tensor(out=xs, in0=po_, scalar=0.5, in1=xs, op0=ALU.mult, op1=ALU.add),
                         reads=[pot, Xt[n]], writes=[Xt[n]])
            P.alias(RYt, MIXt + RYt)
            P.alias(QGt, thg_t + mst_t)
            if s == 0 and l == 0:
                dbg_dump("xmix", Xv, Xt, [128, 8, SEQ], F32)
            if stop_after == "mix":
                raise _Stop()

            rstd, tmp_t = rms_stats(1.0 / D)
            for c in range(8):
                g = norms[:, (l * 2 + 1) * 8 + c:(l * 2 + 1) * 8 + c + 1]
                P.op("dve", lambda e, c=c, g=g: e.scalar_tensor_tensor(out=RH[:, c, :], in0=Xv[:, c, :], scalar=g, in1=rstd, op0=ALU.mult, op1=ALU.mult),
                     reads=[Xt[c], tmp_t[9], VEC], writes=[Ht[c]])
            ACTb = RY[:, 0:22528].rearrange("p (f t) -> p f t", f=22)
            ACt = [T() for _ in range(22)]
            P.alias(ACt, tmp_t)
            thf = [QGbuf[:, 0:2048].bitcast(F32), QGbuf[:, 2048:4096].bitcast(F32)]
            thf_t = [T(), T()]
            P.alias(thf_t, QGt)
            for half in range(2):
                t0 = half * 1024
                for f in range(22):
                    wv, wt = load_w(wgu[l, :, :, f * 128:(f + 1) * 128], [128, 8, 128])
                    wv2, wt2 = load_w(wgu[l, :, :, DFF + f * 128:DFF + (f + 1) * 128], [128, 8, 128])
                    pb, pt = grp()
                    for q_ in range(2):
                        for k in range(8):
                            P.op("pe", lambda e, k=k, q_=q_, pb=pb, wv=wv: e.matmul(pb[:, q_ * 512:(q_ + 1) * 512], lhsT=wv[:, k, :],
                                                                                     rhs=RH[:, k, t0 + q_ * 512:t0 + (q_ + 1) * 512], start=(k == 0), stop=(k == 7)),
                                 reads=[wt, Ht[k]], writes=[pt[q_]])
                    for q_ in range(2):
                        for k in range(8):
                            P.op("pe", lambda e, k=k, q_=q_, pb=pb, wv2=wv2: e.matmul(pb[:, 1024 + q_ * 512:1024 + (q_ + 1) * 512], lhsT=wv2[:, k, :],
                                                                                       rhs=RH[:, k, t0 + q_ * 512:t0 + (q_ + 1) * 512], start=(k == 0), stop=(k == 7)),
                                 reads=[wt2, Ht[k]], writes=[pt[2 + q_]])
                    j = f % 2
                    P.op("act", lambda e, j=j, pb=pb: e.activation(out=thf[j], in_=pb[:, 0:1024], func=AF.Tanh, scale=0.5), reads=pt[0:2], writes=[thf_t[j]])
                    P.op("dve", lambda e, j=j, pb=pb: e.scalar_tensor_tensor(out=thf[j], in0=thf[j], scalar=1.0, in1=pb[:, 0:1024], op0=ALU.add, op1=ALU.mult),
                         reads=[thf_t[j]] + pt[0:2], writes=[thf_t[j]])
                    P.op("dve", lambda e, j=j, pb=pb, f=f: e.scalar_tensor_tensor(out=ACTb[:, f, :], in0=thf[j], scalar=0.5, in1=pb[:, 1024:2048], op0=ALU.mult, op1=ALU.mult),
                         reads=[thf_t[j]] + pt[2:4], writes=[ACt[f]])
                for n in range(8):
                    wv, wt = load_w(wdown[l, :, 0:11, n * 128:(n + 1) * 128], [128, 11, 128])
                    wv2, wt2 = load_w(wdown[l, :, 11:22, n * 128:(n + 1) * 128], [128, 11, 128])
                    pb, pt = grp()
                    for q_ in range(2):
                        for f in range(22):
                            w_, wt_ = (wv, wt) if f < 11 else (wv2, wt2)
                            P.op("pe", lambda e, f=f, q_=q_, pb=pb, w_=w_: e.matmul(pb[:, q_ * 512:(q_ + 1) * 512], lhsT=w_[:, f % 11, :],
                                                                                     rhs=ACTb[:, f, q_ * 512:(q_ + 1) * 512], start=(f == 0), stop=(f == 21)),
                                 reads=[wt_, ACt[f]], writes=[pt[q_]])
                    xs = Xv[:, n, t0:t0 + 1024]
                    P.op("dve", lambda e, xs=xs, pb=pb: e.tensor_tensor(out=xs, in0=xs, in1=pb[:, 0:1024], op=ALU.add), reads=[Xt[n]] + pt[0:2], writes=[Xt[n]])
            P.alias(RYt, ACt)
            P.alias(QGt, thf_t)
            if s == 0:
                dbg_dump("xl%d" % l, Xv, Xt, [128, 8, SEQ], F32)
        if stop_after:
            raise _Stop()

        rstd, tmp_t = rms_stats(1.0 / D)
        ost = [RYf[:, 0:2048], RYf[:, 2048:4096]]
        ost_t = [T(), T()]
        P.alias(ost_t, tmp_t[0:8])
        for c in range(8):
            j = c % 2
            P.op("dve", lambda e, c=c, j=j: e.scalar_tensor_tensor(out=ost[j], in0=Xv[:, c, :], scalar=fnorm[:, c:c + 1], in1=rstd, op0=ALU.mult, op1=ALU.mult),
                 reads=[Xt[c], tmp_t[9], VEC], writes=[ost_t[j]])
            P.dma("sp", outT[s, :, c, :], ost[j], reads=[ost_t[j]], is_output=True)
        P.alias(RYt, tmp_t + ost_t)

    return


def prep_weights(w_in, rg_conv_w, rg_conv_b, rg_wa, rg_ba, rg_wx, rg_bx, rg_lambda, w_rnn_proj, dn_conv_w,
                 dn_a_log, dn_dt_bias, dn_norm, w_dn_proj, w_out, mix_norm, ffn_norm, w_gate_up, w_down, final_norm):
    L = NLAYER
    f = lambda a: np.ascontiguousarray(np.asarray(a, dtype=np.float32))
    m = {}
    m["win"] = f(np.asarray(w_in).reshape(L, 8, 128, NIN).transpose(0, 2, 1, 3))
    wa = np.asarray(rg_wa)
    wx = np.asarray(rg_wx)
    g = np.stack([wa[:, 0], wa[:, 1], wx[:, 0], wx[:, 1]], axis=3)
    m["gatew"] = f(g)
    cwv = np.asarray(rg_conv_w).reshape(L, 4, 12, 128).transpose(3, 0, 2, 1)
    pv = lambda a: np.asarray(a).reshape(L, 12, 128).transpose(2, 0, 1)[..., None]
    pv2 = lambda a, d: np.asarray(a)[:, d].reshape(L, 12, 128).transpose(2, 0, 1)[..., None]
    rgv = np.concatenate([cwv, pv(rg_conv_b), pv2(rg_ba, 0), pv2(rg_ba, 1), pv2(rg_bx, 0), pv2(rg_bx, 1),
                          pv2(rg_lambda, 0), pv2(rg_lambda, 1)], axis=3)
    m["rgv"] = f(rgv.reshape(128, L * 12, 11))
    m["wrnn"] = f(np.asarray(w_rnn_proj).reshape(L, 12, 128, D).transpose(0, 2, 1, 3))
    m["dncw"] = f(np.asarray(dn_conv_w).reshape(L, 4, 24, 128).transpose(3, 0, 2, 1).reshape(128, L * 24, 4))
    dv = np.stack([np.asarray(dn_a_log).reshape(L, 16), np.asarray(dn_dt_bias).reshape(L, 16)], axis=1)
    m["dnv"] = f(np.broadcast_to(dv[None], (128, L, 2, 16)))
    m["dnnorm"] = f(np.asarray(dn_norm).T)
    m["wdn"] = f(np.asarray(w_dn_proj).reshape(L, 8, 128, D).transpose(0, 2, 1, 3))
    m["wout"] = f(np.asarray(w_out).reshape(L, 8, 128, D).transpose(0, 2, 1, 3))
    nm = np.stack([np.asarray(mix_norm).reshape(L, 8, 128), np.asarray(ffn_norm).reshape(L, 8, 128)], axis=1)
    m["norms"] = f(nm.transpose(3, 0, 1, 2).reshape(128, L * 2 * 8))
    m["fnorm"] = f(np.asarray(final_norm).reshape(8, 128).T)
    m["wgu"] = f(np.asarray(w_gate_up).reshape(L, 8, 128, 2 * DFF).transpose(0, 2, 1, 3))
    m["wdown"] = f(np.asarray(w_down).reshape(L, 22, 128, D).transpose(0, 2, 1, 3))
    m["consts"] = make_consts()
    return m


def prep_x(xs):
    n = xs.shape[0]
    return np.ascontiguousarray(np.asarray(xs, dtype=np.float32).transpose(0, 2, 1).reshape(n, 8, 128, SEQ).transpose(0, 2, 1, 3))


def unprep_x(o):
    n = o.shape[0]
    return np.ascontiguousarray(o.transpose(0, 2, 1, 3).reshape(n, D, SEQ).transpose(0, 2, 1))


def kernel(x, mix_norm, w_in, rg_conv_w, rg_conv_b, rg_wa, rg_ba, rg_wx, rg_bx, rg_lambda,
           w_rnn_proj, dn_conv_w, dn_a_log, dn_dt_bias, dn_norm, w_dn_proj, w_out,
           ffn_norm, w_gate_up, w_down, final_norm):
    x = np.asarray(x)
    B = x.shape[0]
    nseq = B // NCORES
    wm = prep_weights(w_in, rg_conv_w, rg_conv_b, rg_wa, rg_ba, rg_wx, rg_bx, rg_lambda, w_rnn_proj, dn_conv_w,
                      dn_a_log, dn_dt_bias, dn_norm, w_dn_proj, w_out, mix_norm, ffn_norm, w_gate_up, w_down, final_norm)
    nc = bass.Bass("TRN2", target_bir_lowering=False)
    build(nc, NSEQ=nseq)
    in_maps = []
    for cidx in range(NCORES):
        mm = dict(wm)
        mm["xT"] = prep_x(x[cidx * nseq:(cidx + 1) * nseq])
        in_maps.append(mm)
    res = run_bass_kernel_spmd(nc, in_maps, core_ids=list(range(NCORES)))
    outs = [unprep_x(np.asarray(r["outT"])) for r in res.results]
    return np.concatenate(outs, axis=0).astype(np.float32)
```

```python
import numpy as np
import concourse.bass as bass
import concourse.mybir as mybir
from concourse.bass_utils import run_bass_kernel_spmd

F32 = mybir.dt.float32
F32R = mybir.dt.float32
BF16 = mybir.dt.bfloat16
ALU = mybir.AluOpType
AF = mybir.ActivationFunctionType

NCORES = 8
D = 1024
SEQ = 2048
NLAYER = 4
DRNN = 1536
DFF = 2816
NIN = 9248
C_RY = 1536
C_QKV = 3072
C_Z = 6144
C_BA = 7168
C_GR = 7200
C_GD = 8224
EPS = 1e-6
NEG = -30000.0


class T:
    __slots__ = ("ap", "w", "r", "ps")

    def __init__(self, ap=None, ps=False):
        self.ap = ap
        self.w = None
        self.r = {}
        self.ps = ps


class _Rec:
    def __init__(self):
        self.call = None

    def __getattr__(self, name):
        def f(*a, **k):
            self.call = (name, a, k)
            return self
        return f


class Eng:
    def __init__(self, name, sem):
        self.name = name
        self.sem = sem
        self.key = ("e", name)
        self.count = 0
        self.waited = {}
        self.ops = []


class Prog:
    NDMASEM = 24

    def __init__(self, nc):
        self.nc = nc
        self.sems = {}
        self.E = {}
        for n in ("pe", "dve", "act", "pool", "sp"):
            s = nc.alloc_semaphore(name="s_" + n)
            e = Eng(n, s)
            self.E[n] = e
            self.sems[e.key] = s
        self.dsem = []
        for i in range(self.NDMASEM):
            s = nc.alloc_semaphore(name="d%d" % i)
            self.dsem.append([s, 0])
            self.sems[("d", i)] = s
        self.dnext = 0
        self.out_tokens = []
        self.ninst = 0

    def _deps(self, reads, writes):
        need = {}
        for t in reads:
            if t.w is not None:
                k, v = t.w
                if need.get(k, 0) < v:
                    need[k] = v
        for t in writes:
            if t.w is not None:
                k, v = t.w
                if need.get(k, 0) < v:
                    need[k] = v
            for k, v in t.r.items():
                if need.get(k, 0) < v:
                    need[k] = v
        return need

    def _emit_waits(self, e, need, skip_self=False):
        for k, v in need.items():
            if skip_self and k == e.key:
                continue
            if e.waited.get(k, 0) < v:
                e.waited[k] = v
                e.ops.append(("w", k, v))

    def _mark(self, tok, reads, writes):
        k, v = tok
        for t in reads:
            if t.r.get(k, 0) < v:
                t.r[k] = v
        for t in writes:
            t.w = tok
            t.r = {}

    def op(self, eng, fn, reads=(), writes=()):
        e = self.E[eng]
        psr = [t for t in reads if t.ps]
        if psr:
            reads = [t for t in reads if not t.ps]
            writes = list(writes) + psr
        need = self._deps(reads, writes)
        self._emit_waits(e, need, skip_self=(eng == "pe"))
        e.count += 1
        tok = (e.key, e.count)
        rec = _Rec()
        fn(rec)
        e.ops.append(("i", rec.call))
        self._mark(tok, reads, writes)
        self.ninst += 1
        return tok

    def dma(self, q, out_ap, in_ap, reads=(), writes=(), is_output=False):
        e = self.E[q]
        i = self.dnext
        self.dnext = (self.dnext + 1) % self.NDMASEM
        ds = self.dsem[i]
        need = self._deps(reads, writes)
        k = ("d", i)
        if ds[1] > 0 and need.get(k, 0) < ds[1]:
            need[k] = ds[1]
        self._emit_waits(e, need)
        ds[1] += 16
        tok = (k, ds[1])
        e.ops.append(("d", out_ap, in_ap, k))
        self._mark(tok, reads, writes)
        if is_output:
            self.out_tokens.append(tok)
        self.ninst += 1
        return tok

    def alias(self, new_tiles, old_tiles):
        acc = {}
        for t in old_tiles:
            if t.w is not None:
                k, v = t.w
                if acc.get(k, 0) < v:
                    acc[k] = v
            for k, v in t.r.items():
                if acc.get(k, 0) < v:
                    acc[k] = v
        for t in new_tiles:
            t.w = None
            t.r = dict(acc)

    def finish(self):
        e = self.E["sp"]
        need = {}
        for k, v in self.out_tokens:
            if need.get(k, 0) < v:
                need[k] = v
        self._emit_waits(e, need)
        nc = self.nc
        sems = self.sems

        def run(e, eng):
            for o in e.ops:
                if o[0] == "w":
                    eng.wait_ge(sems[o[1]], o[2])
                elif o[0] == "i":
                    name, a, k = o[1]
                    getattr(eng, name)(*a, **k).then_inc(e.sem, 1)
                else:
                    eng.dma_start(out=o[1], in_=o[2]).then_inc(sems[o[3]], 16)

        with nc.Block() as block:
            @block.tensor
            def _(eng):
                run(self.E["pe"], eng)

            @block.vector
            def _(eng):
                run(self.E["dve"], eng)

            @block.scalar
            def _(eng):
                run(self.E["act"], eng)

            @block.gpsimd
            def _(eng):
                run(self.E["pool"], eng)

            @block.sync
            def _(eng):
                run(self.E["sp"], eng)


CONST_NAMES = ["ident", "ones", "Lf", "Lb", "SUf", "SUb", "nsf", "nsb", "inf", "inb"]


def make_consts():
    t = np.arange(128)[:, None]
    i = np.arange(128)[None, :]
    c = {
        "ident": (t == i),
        "ones": np.ones((128, 128)),
        "Lf": (t <= i),
        "Lb": (t >= i),
        "SUf": (t > i),
        "SUb": (t < i),
        "nsf": -1.0 * (i > t),
        "nsb": -1.0 * (i < t),
        "inf": (i >= t),
        "inb": (i <= t),
    }
    return np.ascontiguousarray(np.stack([np.asarray(c[n], dtype=np.float32) for n in CONST_NAMES], axis=1))


class _Stop(Exception):
    pass


def build(nc, NSEQ=4, NL=NLAYER, dbg=None, stop_after=None, heads=8, rg_blocks=12):
    P = Prog(nc)
    try:
        _build(P, nc, NSEQ, NL, dbg, stop_after, heads, rg_blocks)
    except _Stop:
        pass
    P.finish()
    return P, None


def _build(P, nc, NSEQ, NL, dbg, stop_after, heads, rg_blocks):
    dbg = dbg or []
    dbg_out = {}

    def din(name, shape, dt=F32):
        return nc.dram_tensor(name, shape, dt, kind="ExternalInput").ap()

    xT = din("xT", [NSEQ, 128, 8, SEQ])
    outT = nc.dram_tensor("outT", [NSEQ, 128, 8, SEQ], F32, kind="ExternalOutput").ap()
    win = din("win", [NLAYER, 128, 8, NIN])
    gatew = din("gatew", [NLAYER, 12, 128, 4, 128])
    rgv_d = din("rgv", [128, NLAYER * 12, 11])
    wrnn = din("wrnn", [NLAYER, 128, 12, D])
    dncw_d = din("dncw", [128, NLAYER * 24, 4])
    dnv_d = din("dnv", [128, NLAYER, 2, 16])
    dnnorm_d = din("dnnorm", [128, NLAYER])
    wdn = din("wdn", [NLAYER, 128, 8, D])
    wout = din("wout", [NLAYER, 128, 8, D])
    norms_d = din("norms", [128, NLAYER * 2 * 8])
    fnorm_d = din("fnorm", [128, 8])
    wgu = din("wgu", [NLAYER, 128, 8, 2 * DFF])
    wdown = din("wdown", [NLAYER, 128, 22, D])
    consts_d = din("consts", [128, len(CONST_NAMES), 128])
    xspill = nc.dram_tensor("xspill", [128, 8, SEQ], F32, kind="Internal").ap()
    mixspill = nc.dram_tensor("mixspill", [8, 128, SEQ], BF16, kind="Internal").ap()
    XSt = [T() for _ in range(8)]
    MSt = [T() for _ in range(8)]

    def sb(name, shape, dt=F32):
        return nc.alloc_sbuf_tensor("s_" + name, shape, dt).ap()

    def dbg_dump(name, ap, tiles, shape, dt=F32):
        if name not in dbg:
            return
        o = nc.dram_tensor("dbg_" + name, shape, dt, kind="ExternalOutput").ap()
        P.dma("sp", o, ap, reads=tiles, is_output=True)
        dbg_out[name] = o

    RX = sb("RX", [128, 16384], F32)
    RH = sb("RH", [128, 8, SEQ], BF16)
    RY = sb("RY", [128, 24576], BF16)
    WP = [sb("WP%d" % i, [128, 2048], BF16) for i in range(3)]
    WPt = [T() for _ in range(3)]
    GW = [sb("GW%d" % i, [128, 4, 128], BF16) for i in range(2)]
    GWt = [T() for _ in range(2)]
    wp_i = [0]
    gw_i = [0]
    CM = sb("CM", [128, len(CONST_NAMES), 128], F32)
    CMt = T()
    cm = {n: CM[:, i, :] for i, n in enumerate(CONST_NAMES)}
    ones_bf = sb("ones_bf", [128, 128], BF16)
    ident_bf = sb("ident_bf", [128, 128], BF16)
    CBt = T()
    mskA = sb("mskA", [128, 2, 128], F32)
    mskB = sb("mskB", [128, 2, 128], F32)
    rgv = sb("rgv", [128, NLAYER * 12, 11], F32)
    rgd = sb("rgd", [128, NLAYER * 12, 10], F32)
    rgtmp = sb("rgtmp", [128, NLAYER * 12, 2], F32)
    dncw = sb("dncw", [128, NLAYER * 24, 4], F32)
    dnv = sb("dnv", [128, NLAYER, 2, 16], F32)
    nega = sb("nega", [128, NLAYER, 16], F32)
    dnnorm = sb("dnnorm", [128, NLAYER], F32)
    norms = sb("norms", [128, NLAYER * 2 * 8], F32)
    fnorm = sb("fnorm", [128, 8], F32)
    VEC = T()
    QGbuf = sb("QGbuf", [128, 4096], BF16)
    SMbuf = sb("SMbuf", [128, 512 + 8 * 256], F32)
    YQ = [[sb("YQ%d_%d" % (sl, j), [128, 2, 256], F32) for j in range(2)] for sl in range(2)]
    YTb = [[sb("YT%d_%d" % (sl, j), [128, 2, 128], F32) for j in range(2)] for sl in range(2)]
    KQA = [sb("KQA%d" % sl, [128, 2, 128], BF16) for sl in range(2)]
    KQB = [sb("KQB%d" % sl, [128, 2, 128], BF16) for sl in range(2)]
    YQt = [[T() for _ in range(2)] for _ in range(4)]
    YTt = [[T() for _ in range(2)] for _ in range(4)]
    KQAt = [T() for _ in range(4)]
    KQBt = [T() for _ in range(4)]
    Dgt = [T() for _ in range(4)]
    D2t = [T() for _ in range(4)]

    PS = nc.alloc_psum_tensor("PS", [128, 4096], F32).ap()
    BK = [T(ps=True) for _ in range(8)]

    def bank(b, n=1):
        return PS[:, b * 512:(b + n) * 512]

    P.dma("sp", CM, consts_d, writes=[CMt])
    for dst, src in ((rgv, rgv_d), (dncw, dncw_d), (dnv, dnv_d), (dnnorm, dnnorm_d), (norms, norms_d), (fnorm, fnorm_d)):
        P.dma("sp", dst, src, writes=[VEC])
    P.op("dve", lambda e: e.tensor_copy(out=ones_bf, in_=cm["ones"]), reads=[CMt], writes=[CBt])
    P.op("dve", lambda e: e.tensor_copy(out=ident_bf, in_=cm["ident"]), reads=[CMt], writes=[CBt])
    P.op("dve", lambda e: e.tensor_copy(out=mskA[:, 0, :], in_=cm["nsf"]), reads=[CMt], writes=[CBt])
    P.op("dve", lambda e: e.tensor_copy(out=mskA[:, 1, :], in_=cm["inf"]), reads=[CMt], writes=[CBt])
    P.op("dve", lambda e: e.tensor_copy(out=mskB[:, 0, :], in_=cm["nsb"]), reads=[CMt], writes=[CBt])
    P.op("dve", lambda e: e.tensor_copy(out=mskB[:, 1, :], in_=cm["inb"]), reads=[CMt], writes=[CBt])
    P.op("act", lambda e: e.mul(out=rgd[:, :, 0:4], in_=rgv[:, :, 5:9], mul=0.5), reads=[VEC], writes=[VEC])
    P.op("act", lambda e: e.activation(out=rgtmp, in_=rgv[:, :, 9:11], func=AF.Exp, scale=-1.0), reads=[VEC], writes=[VEC])
    P.op("act", lambda e: e.activation(out=rgtmp, in_=rgtmp, func=AF.Ln, bias=1.0), reads=[VEC], writes=[VEC])
    P.op("act", lambda e: e.mul(out=rgd[:, :, 4:6], in_=rgtmp, mul=-4.0), reads=[VEC], writes=[VEC])
    P.op("act", lambda e: e.mul(out=rgd[:, :, 6:8], in_=rgtmp, mul=-8.0), reads=[VEC], writes=[VEC])
    P.op("act", lambda e: e.mul(out=rgd[:, :, 8:10], in_=rgtmp, mul=4.0), reads=[VEC], writes=[VEC])
    P.op("act", lambda e: e.activation(out=nega, in_=dnv[:, :, 0, :], func=AF.Exp), reads=[VEC], writes=[VEC])
    P.op("dve", lambda e: e.tensor_scalar(out=nega, in0=nega, scalar1=-1.0, scalar2=None, op0=ALU.mult), reads=[VEC], writes=[VEC])

    def load_w(src_ap, shape):
        i = wp_i[0]
        wp_i[0] = (i + 1) % len(WP)
        n = int(np.prod(shape[1:]))
        view = WP[i][:, 0:n].rearrange("p (a b) -> p a b", a=shape[1])
        P.dma("pool", view, src_ap, writes=[WPt[i]])
        return view, WPt[i]

    def proj(pb, ptiles, wv, wt, rhs_fn, rhs_tiles, nk, ncols=SEQ):
        nt = ncols // 512
        for k in range(nk):
            for tt in range(nt):
                P.op("pe", lambda e, tt=tt, k=k: e.matmul(pb[:, tt * 512:(tt + 1) * 512], lhsT=wv[:, k, :], rhs=rhs_fn(k, tt),
                                                            start=(k == 0), stop=(k == nk - 1)),
                     reads=[wt] + rhs_tiles, writes=[ptiles[tt]])

    QGt = [T(), T()]
    SM = {nm: T() for nm in ["bt", "lnb", "g", "gc", "colE", "colC", "colD", "egl", "tmp"]}
    Xv = RX.rearrange("p (c t) -> p c t", c=8)
    Xt = [T() for _ in range(8)]
    Ht = [T() for _ in range(8)]
    RYt = [T() for _ in range(12)]
    RYb = RY.rearrange("p (c t) -> p c t", c=12)
    RYf = RY.bitcast(F32)
    hfn = lambda k, tt: RH[:, k, tt * 512:(tt + 1) * 512]
    pgrp = [0]

    def grp():
        g_ = pgrp[0]
        pgrp[0] ^= 1
        return bank(4 * g_, 4), BK[4 * g_:4 * g_ + 4]

    def rms_stats(scale):
        tmp_t = [T() for _ in range(10)]
        P.alias(tmp_t, RYt)
        lnv = RYf[:, 8192:10240]
        rstd = RYf[:, 10240:12288]
        for c in range(8):
            P.op("act", lambda e, c=c: e.activation(out=RYb[:, c, :], in_=Xv[:, c, :], func=AF.Square), reads=[Xt[c]], writes=[tmp_t[c]])
        for tt in range(4):
            for c in range(8):
                P.op("pe", lambda e, c=c, tt=tt: e.matmul(bank(tt), lhsT=ones_bf, rhs=RYb[:, c, tt * 512:(tt + 1) * 512],
                                                            start=(c == 0), stop=(c == 7)),
                     reads=[CBt, tmp_t[c]], writes=[BK[tt]])
        P.op("act", lambda e: e.activation(out=lnv, in_=bank(0, 4), func=AF.Ln, bias=EPS, scale=scale), reads=BK[0:4], writes=[tmp_t[8]])
        P.op("act", lambda e: e.activation(out=rstd, in_=lnv, func=AF.Exp, scale=-0.5), reads=[tmp_t[8]], writes=[tmp_t[9]])
        return rstd, tmp_t

    rxp = RX[:, 0:2052]
    F0 = RX[:, 2052:4100]
    F1 = RX[:, 4100:6148]
    F2 = RX[:, 6148:8196]
    HF = RX[:, 8196:10244]
    HB = RX[:, 10244:12292]
    ub = RX[:, 12292:13316].bitcast(BF16)
    thi = RX[:, 13316:14340].bitcast(BF16)
    sqv = RX[:, 14340:15364].bitcast(BF16)
    QK = RX[:, 8196:10244].bitcast(BF16).rearrange("p (c two d) -> p c two d", c=16, two=2)
    QKf = RX[:, 8196:10244].bitcast(BF16).rearrange("p (c n) -> p c n", c=16)
    ktok = RX[:, 10244:11268].bitcast(BF16).rearrange("p (c d) -> p c d", c=16)
    vtok = RX[:, 11268:12292].bitcast(BF16).rearrange("p (c d) -> p c d", c=16)
    KDB = RX[:, 12292:14340].bitcast(BF16).rearrange("p (two c d) -> p two c d", two=2, c=16)
    tb = 14340
    S32 = [RX[:, tb:tb + 128], RX[:, tb + 128:tb + 256]]
    Sbf = [RX[:, tb + 256:tb + 320].bitcast(BF16), RX[:, tb + 320:tb + 384].bitcast(BF16)]
    rbf = [RX[:, tb + 384:tb + 448].bitcast(BF16), RX[:, tb + 448:tb + 512].bitcast(BF16)]
    xbf = [RX[:, tb + 512:tb + 576].bitcast(BF16), RX[:, tb + 576:tb + 640].bitcast(BF16)]
    NS = 2
    Dg = [RX[:, tb + 640:tb + 896].rearrange("p (a b) -> p a b", a=2), RX[:, tb + 896:tb + 1152].rearrange("p (a b) -> p a b", a=2)]
    D2 = [RX[:, tb + 1152:tb + 1280].bitcast(BF16).rearrange("p (a b) -> p a b", a=2),
          RX[:, tb + 1280:tb + 1408].bitcast(BF16).rearrange("p (a b) -> p a b", a=2)]
    for b0 in (2052, 4228):
        YQ.append([RX[:, b0:b0 + 512].rearrange("p (a b) -> p a b", a=2), RX[:, b0 + 512:b0 + 1024].rearrange("p (a b) -> p a b", a=2)])
        YTb.append([RX[:, b0 + 1024:b0 + 1280].rearrange("p (a b) -> p a b", a=2), RX[:, b0 + 1280:b0 + 1536].rearrange("p (a b) -> p a b", a=2)])
        Dg.append(RX[:, b0 + 1536:b0 + 1792].rearrange("p (a b) -> p a b", a=2))
        KQA.append(RX[:, b0 + 1792:b0 + 1920].bitcast(BF16).rearrange("p (a b) -> p a b", a=2))
        KQB.append(RX[:, b0 + 1920:b0 + 2048].bitcast(BF16).rearrange("p (a b) -> p a b", a=2))
        D2.append(RX[:, b0 + 2048:b0 + 2176].bitcast(BF16).rearrange("p (a b) -> p a b", a=2))
    YQr = [[a_.bitcast(F32R) for a_ in sl_] for sl_ in YQ[0:NS]]
    YTr = [[a_.bitcast(F32R) for a_ in sl_] for sl_ in YTb[0:NS]]
    Dgq = [RX[:, tb + 1408:tb + 1536], RX[:, tb + 1536:tb + 1664]]
    QG = QGbuf.rearrange("p (two c d) -> p two c d", two=2, c=16)
    RYtail = RY[:, 16384:24576]
    qsb, ksb, vsb, sqb = (RYtail[:, 0:2048], RYtail[:, 2048:4096], RYtail[:, 4096:6144], RYtail[:, 6144:8192])
    T2T = RYtail[:, 0:4096].rearrange("p (c two d) -> p c two d", c=16, two=2)
    ATT = RYtail[:, 4096:8192].rearrange("p (c two d) -> p c two d", c=16, two=2)
    OT = RX[:, 0:2048]
    bt = SMbuf[:, 0:512].rearrange("p (c k) -> p c k", c=16)
    smv = lambda i: SMbuf[:, 512 + 256 * i:512 + 256 * (i + 1)].rearrange("p (c k) -> p c k", c=16)
    lnb, gg, gc, colE, colC, colD, egl, stmp = [smv(i) for i in range(8)]

    for s in range(NSEQ):
        for c in range(8):
            P.dma("sp", Xv[:, c, :], xT[s, :, c, :], writes=[Xt[c]])

        for l in range(NL):
            rstd, tmp_t = rms_stats(1.0 / D)
            for c in range(8):
                g = norms[:, (l * 2 + 0) * 8 + c:(l * 2 + 0) * 8 + c + 1]
                P.op("dve", lambda e, c=c, g=g: e.scalar_tensor_tensor(out=RH[:, c, :], in0=Xv[:, c, :], scalar=g, in1=rstd,
                                                                         op0=ALU.mult, op1=ALU.mult),
                     reads=[Xt[c], tmp_t[9], VEC], writes=[Ht[c]])
            P.alias(RYt, tmp_t)
            if s == 0 and l == 0:
                dbg_dump("h", RH, Ht, [128, 8, SEQ], BF16)
            for c in range(8):
                P.dma("sp", xspill[:, c, :], Xv[:, c, :], reads=[Xt[c]], writes=[XSt[c]])
            W = {k_: T() for k_ in ["rxp", "F0", "F1", "F2", "HF", "HB", "ub", "thi", "sq"]}
            P.alias(list(W.values()), Xt)
            W["F1b"] = T()
            W["F2b"] = T()
            P.alias([W["F1b"]], QGt)
            P.alias([W["F2b"]], list(SM.values()))
            F1d = [F1, QGbuf.bitcast(F32)]
            F2d = [F2, SMbuf[:, 0:2048]]
            F1k = ["F1", "F1b"]
            F2k = ["F2", "F2b"]
            P.op("pool", lambda e: e.memset(rxp[:, 0:2], 0.0), writes=[W["rxp"]])
            P.op("pool", lambda e: e.memset(rxp[:, 2050:2052], 0.0), writes=[W["rxp"]])

            for n in range(rg_blocks):
                ln_ = l * 12 + n
                wv, wt = load_w(win[l, :, :, n * 128:(n + 1) * 128], [128, 8, 128])
                gi = gw_i[0]
                gw_i[0] ^= 1
                P.dma("pool", GW[gi], gatew[l, n], writes=[GWt[gi]])
                pb, pt = grp()
                proj(pb, pt, wv, wt, hfn, Ht, 8)
                P.op("act", lambda e, pb=pb: e.copy(out=rxp[:, 2:2050], in_=pb), reads=pt, writes=[W["rxp"]])
                cw = lambda j, ln_=ln_: rgv[:, ln_, j:j + 1]
                P.op("dve", lambda e, cw=cw: e.tensor_scalar(out=F0, in0=rxp[:, 0:2048], scalar1=cw(0), scalar2=cw(4), op0=ALU.mult, op1=ALU.add),
                     reads=[W["rxp"], VEC], writes=[W["F0"]])
                P.op("dve", lambda e, cw=cw: e.scalar_tensor_tensor(out=F1, in0=rxp[:, 1:2049], scalar=cw(1), in1=F0, op0=ALU.mult, op1=ALU.add),
                     reads=[W["rxp"], VEC, W["F0"]], writes=[W["F1"]])
                P.op("dve", lambda e, cw=cw: e.scalar_tensor_tensor(out=F0, in0=rxp[:, 2:2050], scalar=cw(2), in1=F1, op0=ALU.mult, op1=ALU.add),
                     reads=[W["rxp"], VEC, W["F1"]], writes=[W["F0"]])
                P.op("dve", lambda e, cw=cw: e.scalar_tensor_tensor(out=ub, in0=rxp[:, 3:2051], scalar=cw(3), in1=F0, op0=ALU.mult, op1=ALU.add),
                     reads=[W["rxp"], VEC, W["F0"]], writes=[W["ub"]])
                ufn = lambda k, tt: ub[:, tt * 512:(tt + 1) * 512]
                gv = GW[gi]
                for d in range(2):
                    pb, pt = grp()
                    proj(pb, pt, gv[:, d:d + 1, :], GWt[gi], ufn, [W["ub"]], 1)
                    P.op("act", lambda e, pb=pb, d=d, ln_=ln_: e.activation(out=F0, in_=pb, func=AF.Tanh, scale=0.5, bias=rgd[:, ln_, d:d + 1]),
                         reads=pt + [VEC], writes=[W["F0"]])
                    pb, pt = grp()
                    proj(pb, pt, gv[:, 2 + d:3 + d, :], GWt[gi], ufn, [W["ub"]], 1)
                    P.op("act", lambda e, pb=pb, d=d, ln_=ln_: e.activation(out=thi, in_=pb, func=AF.Tanh, scale=0.5, bias=rgd[:, ln_, 2 + d:3 + d]),
                         reads=pt + [VEC], writes=[W["thi"]])
                    F1x, F2x, k1, k2 = F1d[d], F2d[d], F1k[d], F2k[d]
                    P.op("act", lambda e, d=d, ln_=ln_, F1x=F1x: e.activation(out=F1x, in_=F0, func=AF.Exp, scale=rgd[:, ln_, 4 + d:5 + d], bias=rgd[:, ln_, 4 + d:5 + d]),
                         reads=[W["F0"], VEC], writes=[W[k1]])
                    P.op("act", lambda e, d=d, ln_=ln_, F2x=F2x: e.activation(out=F2x, in_=F0, func=AF.Tanh, scale=rgd[:, ln_, 8 + d:9 + d], bias=rgd[:, ln_, 8 + d:9 + d]),
                         reads=[W["F0"], VEC], writes=[W[k2]])
                    P.op("act", lambda e, d=d, ln_=ln_: e.activation(out=F0, in_=F0, func=AF.Exp, scale=rgd[:, ln_, 6 + d:7 + d], bias=rgd[:, ln_, 6 + d:7 + d]),
                         reads=[W["F0"], VEC], writes=[W["F0"]])
                    P.op("dve", lambda e, F2x=F2x: e.scalar_tensor_tensor(out=F0, in0=F0, scalar=1.0, in1=F2x, op0=ALU.add, op1=ALU.mult),
                         reads=[W["F0"], W[k2]], writes=[W["F0"]])
                    P.op("act", lambda e: e.activation(out=F0, in_=F0, func=AF.Ln), reads=[W["F0"]], writes=[W["F0"]])
                    P.op("act", lambda e: e.activation(out=sqv, in_=F0, func=AF.Exp, scale=0.5), reads=[W["F0"]], writes=[W["sq"]])
                    P.op("dve", lambda e, F2x=F2x: e.scalar_tensor_tensor(out=F2x, in0=thi, scalar=1.0, in1=ub, op0=ALU.add, op1=ALU.mult),
                         reads=[W["thi"], W["ub"]], writes=[W[k2]])
                    P.op("dve", lambda e, F2x=F2x: e.scalar_tensor_tensor(out=F2x, in0=F2x, scalar=0.5, in1=sqv, op0=ALU.mult, op1=ALU.mult),
                         reads=[W[k2], W["sq"]], writes=[W[k2]])
                    if d == 0:
                        P.op("dve", lambda e, F1x=F1x, F2x=F2x: e.tensor_tensor_scan(out=HF, data0=F1x, data1=F2x, initial=0.0, op0=ALU.mult, op1=ALU.add),
                             reads=[W[k1], W[k2]], writes=[W["HF"]])
                    else:
                        P.op("dve", lambda e, F1x=F1x, F2x=F2x: e.tensor_tensor_scan(out=HB[:, ::-1], data0=F1x[:, ::-1], data1=F2x[:, ::-1], initial=0.0,
                                                                     op0=ALU.mult, op1=ALU.add),
                             reads=[W[k1], W[k2]], writes=[W["HB"]])
                wv, wt = load_w(win[l, :, :, C_RY + n * 128:C_RY + (n + 1) * 128], [128, 8, 128])
                pb, pt = grp()
                proj(pb, pt, wv, wt, hfn, Ht, 8)
                P.op("act", lambda e, pb=pb: e.activation(out=F0, in_=pb, func=AF.Square), reads=pt, writes=[W["F0"]])
                P.op("dve", lambda e: e.tensor_scalar(out=F0, in0=F0, scalar1=0.044715, scalar2=1.0, op0=ALU.mult, op1=ALU.add),
                     reads=[W["F0"]], writes=[W["F0"]])
                P.op("dve", lambda e, pb=pb: e.tensor_tensor(out=F0, in0=F0, in1=pb, op=ALU.mult), reads=[W["F0"]] + pt, writes=[W["F0"]])
                P.op("act", lambda e: e.activation(out=F1, in_=F0, func=AF.Tanh, scale=0.7978845608028654), reads=[W["F0"]], writes=[W["F1"]])
                P.op("dve", lambda e: e.tensor_tensor(out=F2, in0=HF, in1=HB, op=ALU.add), reads=[W["HF"], W["HB"]], writes=[W["F2"]])
                P.op("dve", lambda e, pb=pb: e.scalar_tensor_tensor(out=F0, in0=F1, scalar=1.0, in1=pb, op0=ALU.add, op1=ALU.mult),
                     reads=[W["F1"]] + pt, writes=[W["F0"]])
                P.op("dve", lambda e, n=n: e.scalar_tensor_tensor(out=RYb[:, n, :], in0=F0, scalar=0.5, in1=F2, op0=ALU.mult, op1=ALU.mult),
                     reads=[W["F0"], W["F2"]], writes=[RYt[n]])
            if s == 0 and l == 0:
                dbg_dump("yr", RYb, RYt, [128, 12, SEQ], BF16)
            if stop_after == "rg":
                raise _Stop()

            P.alias(QGt, [W["F1b"]])
            P.alias(list(SM.values()), [W["F2b"]])
            stg = [RX[:, 8196:9220].bitcast(BF16), RX[:, 9220:10244].bitcast(BF16)]
            stg_t = [T(), T()]
            P.alias(stg_t, [W["HF"]])
            yfn = lambda k, tt: RYb[:, k, tt * 512:(tt + 1) * 512]
            for n in range(8):
                wv, wt = load_w(wrnn[l, :, :, n * 128:(n + 1) * 128], [128, 12, 128])
                pb, pt = grp()
                proj(pb, pt, wv, wt, yfn, RYt, 12)
                wv2, wt2 = load_w(win[l, :, :, C_GR + n * 128:C_GR + (n + 1) * 128], [128, 8, 128])
                pb2, pt2 = grp()
                proj(pb2, pt2, wv2, wt2, hfn, Ht, 8)
                P.op("act", lambda e, pb2=pb2: e.activation(out=F0, in_=pb2, func=AF.Tanh, scale=0.5), reads=pt2, writes=[W["F0"]])
                si = n % 2
                P.op("dve", lambda e, pb=pb, si=si: e.scalar_tensor_tensor(out=stg[si], in0=F0, scalar=1.0, in1=pb, op0=ALU.add, op1=ALU.mult),
                     reads=[W["F0"]] + pt, writes=[stg_t[si]])
                P.dma("sp", mixspill[n], stg[si], reads=[stg_t[si]], writes=[MSt[n]])

            DN = {nm: T() for nm in ["QK", "ktok", "vtok", "KDB", "S0", "S1", "Sb0", "Sb1", "r0", "r1", "x0", "x1",
                                     "Dgq0", "Dgq1"]}
            P.alias(list(DN.values()) + Dgt[0:2] + D2t[0:2], [W["HF"], W["HB"], W["ub"], W["thi"], W["sq"]] + stg_t)
            DR = {nm: T() for nm in ["qs", "ks", "vs", "sqb"]}
            P.alias(list(DR.values()), RYt[8:12])
            T2Tt = [T() for _ in range(16)]
            ATTt = [T() for _ in range(16)]
            OTt = [T() for _ in range(16)]
            wv, wt = load_w(win[l, :, :, C_BA:C_BA + 32], [128, 8, 32])
            pbt = bank(0).rearrange("p (c k) -> p c k", c=16)
            for c in range(16):
                for k in range(8):
                    P.op("pe", lambda e, c=c, k=k: e.matmul(pbt[:, c, :], lhsT=RH[:, k, c * 128:(c + 1) * 128], rhs=wv[:, k, :],
                                                              start=(k == 0), stop=(k == 7)),
                         reads=[Ht[k], wt], writes=[BK[0]])
            P.op("act", lambda e: e.copy(out=bt, in_=pbt), reads=[BK[0]], writes=[SM["bt"]])
            P.op("act", lambda e: e.activation(out=lnb, in_=bt[:, :, 0:16], func=AF.Exp, scale=-1.0), reads=[SM["bt"]], writes=[SM["lnb"]])
            P.op("act", lambda e: e.activation(out=lnb, in_=lnb, func=AF.Ln, bias=1.0), reads=[SM["lnb"]], writes=[SM["lnb"]])
            P.op("dve", lambda e: e.tensor_scalar(out=lnb, in0=lnb, scalar1=-1.0, scalar2=None, op0=ALU.mult), reads=[SM["lnb"]], writes=[SM["lnb"]])
            dtb = dnv[:, l, 1, :].unsqueeze(1).to_broadcast([128, 16, 16])
            ngb = nega[:, l, :].unsqueeze(1).to_broadcast([128, 16, 16])
            P.op("dve", lambda e: e.tensor_tensor(out=gg, in0=bt[:, :, 16:32], in1=dtb, op=ALU.add), reads=[SM["bt"], VEC], writes=[SM["g"]])
            P.op("act", lambda e: e.activation(out=gg, in_=gg, func=AF.Exp), reads=[SM["g"]], writes=[SM["g"]])
            P.op("act", lambda e: e.activation(out=gg, in_=gg, func=AF.Ln, bias=1.0), reads=[SM["g"]], writes=[SM["g"]])
            P.op("dve", lambda e: e.tensor_tensor(out=gg, in0=gg, in1=ngb, op=ALU.mult), reads=[SM["g"], VEC], writes=[SM["g"]])
            pgc = bank(1)[:, 0:256].rearrange("p (c k) -> p c k", c=16)
            pgl = bank(1)[:, 256:512].rearrange("p (c k) -> p c k", c=16)
            P.op("pe", lambda e: e.matmul(pgc[:, :, 0:8], lhsT=cm["Lf"], rhs=gg[:, :, 0:8], start=True, stop=True), reads=[CMt, SM["g"]], writes=[BK[1]])
            P.op("pe", lambda e: e.matmul(pgc[:, :, 8:16], lhsT=cm["Lb"], rhs=gg[:, :, 8:16], start=True, stop=True), reads=[CMt, SM["g"]], writes=[BK[1]])
            P.op("pe", lambda e: e.matmul(pgl, lhsT=cm["ones"], rhs=gg, start=True, stop=True), reads=[CMt, SM["g"]], writes=[BK[1]])
            P.op("act", lambda e: e.copy(out=gc, in_=pgc), reads=[BK[1]], writes=[SM["gc"]])
            P.op("act", lambda e: e.activation(out=colE, in_=pgc, func=AF.Exp), reads=[BK[1]], writes=[SM["colE"]])
            P.op("act", lambda e: e.activation(out=egl, in_=pgl, func=AF.Exp), reads=[BK[1]], writes=[SM["egl"]])
            P.op("dve", lambda e: e.tensor_scalar(out=colC, in0=colE, scalar1=-1.0, scalar2=None, op0=ALU.mult), reads=[SM["colE"]], writes=[SM["colC"]])
            P.op("dve", lambda e: e.tensor_tensor(out=stmp, in0=pgl, in1=gc, op=ALU.subtract), reads=[BK[1], SM["gc"]], writes=[SM["tmp"]])
            P.op("dve", lambda e: e.tensor_tensor(out=stmp, in0=stmp, in1=lnb, op=ALU.add), reads=[SM["tmp"], SM["lnb"]], writes=[SM["tmp"]])
            P.op("act", lambda e: e.activation(out=colD, in_=stmp, func=AF.Exp), reads=[SM["tmp"]], writes=[SM["colD"]])
            if s == 0 and l == 0:
                dbg_dump("sm", SMbuf, list(SM.values()), [128, 512 + 8 * 256], F32)
            if stop_after == "sm":
                raise _Stop()

            p1v = [bank(sl).rearrange("p (g n) -> p g n", g=2) for sl in range(NS)]
            p2f = [bank(4 + sl)[:, 0:256].rearrange("p (g n) -> p g n", g=2) for sl in range(NS)]
            bA = bank(0, 4)
            bAt = BK[0:4]

            for hd in range(heads):
                dh = [hd, 8 + hd]
                for blk, dst, dkey in ((hd, qsb, "qs"), (8 + hd, ksb, "ks"), (16 + hd, vsb, "vs")):
                    wv, wt = load_w(win[l, :, :, C_QKV + blk * 128:C_QKV + (blk + 1) * 128], [128, 8, 128])
                    proj(bA, bAt, wv, wt, hfn, Ht, 8)
                    P.op("act", lambda e: e.copy(out=rxp[:, 2:2050], in_=bA), reads=bAt, writes=[W["rxp"]] + OTt)
                    cw = lambda j, r_=l * 24 + blk: dncw[:, r_, j:j + 1]
                    P.op("dve", lambda e, cw=cw: e.tensor_scalar(out=F0, in0=rxp[:, 0:2048], scalar1=cw(0), scalar2=None, op0=ALU.mult),
                         reads=[W["rxp"], VEC], writes=[W["F0"]])
                    P.op("dve", lambda e, cw=cw: e.scalar_tensor_tensor(out=F1, in0=rxp[:, 1:2049], scalar=cw(1), in1=F0, op0=ALU.mult, op1=ALU.add),
                         reads=[W["rxp"], VEC, W["F0"]], writes=[W["F1"]])
                    P.op("dve", lambda e, cw=cw: e.scalar_tensor_tensor(out=F0, in0=rxp[:, 2:2050], scalar=cw(2), in1=F1, op0=ALU.mult, op1=ALU.add),
                         reads=[W["rxp"], VEC, W["F1"]], writes=[W["F0"]])
                    P.op("dve", lambda e, cw=cw: e.scalar_tensor_tensor(out=F1, in0=rxp[:, 3:2051], scalar=cw(3), in1=F0, op0=ALU.mult, op1=ALU.add),
                         reads=[W["rxp"], VEC, W["F0"]], writes=[W["F1"]])
                    P.op("act", lambda e: e.activation(out=F2, in_=F1, func=AF.Tanh, scale=0.5), reads=[W["F1"]], writes=[W["F2"]])
                    P.op("dve", lambda e, dst=dst: e.scalar_tensor_tensor(out=dst, in0=F2, scalar=1.0, in1=F1, op0=ALU.add, op1=ALU.mult),
                         reads=[W["F2"], W["F1"]], writes=[DR[dkey]] + T2Tt + ATTt)
                for src, skey, slot, scl in ((ksb, "ks", 0, 1.0), (qsb, "qs", 1, 128.0 ** -0.5)):
                    P.op("act", lambda e, src=src: e.activation(out=sqb, in_=src, func=AF.Square), reads=[DR[skey]], writes=[DR["sqb"]])
                    for tt in range(4):
                        P.op("pe", lambda e, tt=tt: e.matmul(bank(tt), lhsT=ones_bf, rhs=sqb[:, tt * 512:(tt + 1) * 512], start=True, stop=True),
                             reads=[CBt, DR["sqb"]], writes=[BK[tt]])
                    P.op("act", lambda e: e.activation(out=F0, in_=bA, func=AF.Ln, bias=4.0 * EPS), reads=bAt, writes=[W["F0"]])
                    P.op("act", lambda e: e.activation(out=F0, in_=F0, func=AF.Exp, scale=-0.5), reads=[W["F0"]], writes=[W["F0"]])
                    P.op("dve", lambda e, src=src, slot=slot, scl=scl: e.scalar_tensor_tensor(
                        out=QK[:, :, slot, :], in0=src.rearrange("p (c d) -> p c d", c=16), scalar=scl,
                        in1=F0.rearrange("p (c d) -> p c d", c=16), op0=ALU.mult, op1=ALU.mult),
                         reads=[DR[skey], W["F0"]], writes=[DN["QK"]])
                ptr = bank(0, 2).bitcast(BF16).rearrange("p (c d) -> p c d", c=16)
                ptr2 = bank(2, 2).bitcast(BF16).rearrange("p (c d) -> p c d", c=16)
                for c in range(16):
                    P.op("pe", lambda e, c=c: e.transpose(ptr[:, c, :], QK[:, c, 0, :], ident_bf), reads=[DN["QK"], CBt], writes=[BK[0], BK[1]])
                P.op("act", lambda e: e.copy(out=ktok, in_=ptr), reads=[BK[0], BK[1]], writes=[DN["ktok"]])
                for c in range(16):
                    P.op("pe", lambda e, c=c: e.transpose(ptr2[:, c, :], vsb[:, c * 128:(c + 1) * 128], ident_bf), reads=[DR["vs"], CBt], writes=[BK[2], BK[3]])
                P.op("act", lambda e: e.mul(out=vtok, in_=ptr2, mul=0.5), reads=[BK[2], BK[3]], writes=[DN["vtok"]])
                for d in range(2):
                    cdb = colD[:, :, dh[d]:dh[d] + 1].to_broadcast([128, 16, 128])
                    P.op("dve", lambda e, d=d, cdb=cdb: e.tensor_tensor(out=KDB[:, d], in0=ktok, in1=cdb, op=ALU.mult),
                         reads=[DN["ktok"], SM["colD"]], writes=[DN["KDB"]])
                    for c4 in range(4):
                        prb = bank(3).rearrange("p (c d) -> p c d", c=4)
                        for cc in range(4):
                            c = c4 * 4 + cc
                            j = c % 2
                            P.op("act", lambda e, c=c, d=d, j=j: e.mul(out=Dgq[j], in_=cm["ident"], mul=colE[:, c, dh[d]:dh[d] + 1]),
                                 reads=[CMt, SM["colE"]], writes=[DN["Dgq%d" % j]])
                            P.op("pe", lambda e, cc=cc, j=j: e.matmul(prb[:, cc, :], lhsT=cm["ones"], rhs=Dgq[j], start=True, stop=True),
                                 reads=[CMt, DN["Dgq%d" % j]], writes=[BK[3]])
                        P.op("dve", lambda e, d=d, c4=c4: e.tensor_tensor(out=QG[:, d, c4 * 4:(c4 + 1) * 4, :], in0=QK[:, c4 * 4:(c4 + 1) * 4, 1, :],
                                                                            in1=prb, op=ALU.mult),
                             reads=[DN["QK"], BK[3]], writes=[QGt[d]])

                if stop_after == "prep":
                    dbg_dump("QK", RX[:, 8196:10244].bitcast(BF16), [DN["QK"]], [128, 4096], BF16)
                    dbg_dump("ktok", RX[:, 10244:11268].bitcast(BF16), [DN["ktok"]], [128, 2048], BF16)
                    dbg_dump("vtok", RX[:, 11268:12292].bitcast(BF16), [DN["vtok"]], [128, 2048], BF16)
                    dbg_dump("KDB", RX[:, 12292:14340].bitcast(BF16), [DN["KDB"]], [128, 4096], BF16)
                    dbg_dump("QG", QGbuf, QGt, [128, 4096], BF16)
                    raise _Stop()
                def unit_setup(sl, c):
                    P.op("pe", lambda e: e.matmul(p1v[sl][:, 0, :], lhsT=QK[:, c, 0, :], rhs=QKf[:, c, :], start=True, stop=True),
                         reads=[DN["QK"]], writes=[BK[sl]])
                    kq = p1v[sl][:, 0, :].rearrange("p (a b) -> p a b", a=2)
                    P.op("dve", lambda e: e.tensor_tensor(out=KQA[sl], in0=kq, in1=mskA, op=ALU.mult), reads=[BK[sl], CBt], writes=[KQAt[sl]])
                    P.op("dve", lambda e: e.tensor_tensor(out=KQB[sl], in0=kq, in1=mskB, op=ALU.mult), reads=[BK[sl], CBt], writes=[KQBt[sl]])
                    for d in range(2):
                        Lm = cm["Lf"] if d == 0 else cm["Lb"]
                        Sm = cm["SUf"] if d == 0 else cm["SUb"]
                        kq_ = KQA[sl] if d == 0 else KQB[sl]
                        kqt = KQAt[sl] if d == 0 else KQBt[sl]
                        pe_ = p1v[sl][:, 1, d * 128:(d + 1) * 128]
                        P.op("dve", lambda e, d=d, Lm=Lm: e.tensor_scalar(out=Dg[sl][:, d, :], in0=Lm, scalar1=gg[:, c, dh[d]:dh[d] + 1], scalar2=None, op0=ALU.mult),
                             reads=[CMt, SM["g"]], writes=[Dgt[sl]])
                        P.op("pe", lambda e, d=d, Sm=Sm, pe_=pe_: e.matmul(pe_, lhsT=Sm, rhs=Dg[sl][:, d, :], start=True, stop=True),
                             reads=[CMt, Dgt[sl]], writes=[BK[sl]])
                        P.op("act", lambda e, d=d, pe_=pe_: e.activation(out=D2[sl][:, d, :], in_=pe_, func=AF.Exp, bias=lnb[:, c, dh[d]:dh[d] + 1]),
                             reads=[BK[sl], SM["lnb"]], writes=[D2t[sl]])
                        P.op("dve", lambda e, d=d, kq_=kq_: e.tensor_tensor(out=YQr[sl][0][:, d, 0:128], in0=kq_[:, 0, :], in1=D2[sl][:, d, :], op=ALU.mult),
                             reads=[kqt, D2t[sl]], writes=[YQt[sl][0]])
                        P.op("dve", lambda e, d=d, kq_=kq_: e.tensor_tensor(out=ATT[:, c, d, :], in0=kq_[:, 1, :], in1=D2[sl][:, d, :], op=ALU.mult),
                             reads=[kqt, D2t[sl]], writes=[ATTt[c], DR["vs"], DR["sqb"]])
                        P.op("pe", lambda e, d=d: e.transpose(p2f[sl][:, d, :], YQ[sl][0][:, d, 0:128], cm["ident"]),
                             reads=[YQt[sl][0], CMt], writes=[BK[4 + sl]])
                    P.op("act", lambda e: e.copy(out=YTr[sl][0], in_=p2f[sl]), reads=[BK[4 + sl]], writes=[YTt[sl][0]])

                def unit_level_mm(sl, c, k):
                    ci = k % 2
                    yq, yt = YQr[sl][ci], YTr[sl][ci]
                    rd = [YQt[sl][ci], YTt[sl][ci]]
                    for g_ in range(2):
                        if k == 0:
                            P.op("pe", lambda e, g_=g_: e.matmul(p1v[sl][:, g_, 0:128], lhsT=yt[:, g_, :], rhs=yq[:, g_, 0:128], start=True, stop=True),
                                 reads=rd, writes=[BK[sl]])
                        elif k < 6:
                            P.op("pe", lambda e, g_=g_: e.matmul(p1v[sl][:, g_, :], lhsT=yt[:, g_, :], rhs=yq[:, g_, :], start=True, stop=True),
                                 reads=rd, writes=[BK[sl]])
                        else:
                            P.op("pe", lambda e, g_=g_: e.matmul(p1v[sl][:, g_, 128:256], lhsT=yt[:, g_, :], rhs=yq[:, g_, 128:256], start=True, stop=True),
                                 reads=rd, writes=[BK[sl]])
                    if k < 6:
                        for g_ in range(2):
                            P.op("pe", lambda e, g_=g_: e.matmul(p2f[sl][:, g_, :], lhsT=yq[:, g_, 0:128], rhs=yt[:, g_, :], start=True, stop=True),
                                 reads=rd, writes=[BK[4 + sl]])

                def unit_level_ev(sl, c, k):
                    ci = k % 2
                    ni = 1 - ci
                    yq = YQ[sl][ci]
                    if k < 6:
                        P.op("act", lambda e: e.copy(out=YQr[sl][ni][:, :, 0:128], in_=p1v[sl][:, :, 0:128]), reads=[BK[sl]], writes=[YQt[sl][ni]])
                        P.op("act", lambda e: e.copy(out=YTr[sl][ni], in_=p2f[sl]), reads=[BK[4 + sl]], writes=[YTt[sl][ni]])
                        if k == 0:
                            idb = cm["ident"].unsqueeze(1).to_broadcast([128, 2, 128])
                            P.op("dve", lambda e: e.tensor_tensor(out=YQr[sl][ni][:, :, 128:256], in0=yq[:, :, 0:128], in1=idb, op=ALU.add),
                                 reads=[YQt[sl][ci], CMt], writes=[YQt[sl][ni]])
                        else:
                            P.op("dve", lambda e: e.tensor_tensor(out=YQr[sl][ni][:, :, 128:256], in0=yq[:, :, 128:256], in1=p1v[sl][:, :, 128:256], op=ALU.add),
                                 reads=[YQt[sl][ci], BK[sl]], writes=[YQt[sl][ni]])
                    else:
                        P.op("dve", lambda e: e.tensor_tensor(out=T2T[:, c, :, :], in0=yq[:, :, 128:256], in1=p1v[sl][:, :, 128:256], op=ALU.add),
                             reads=[YQt[sl][ci], BK[sl]], writes=[T2Tt[c], DR["qs"], DR["ks"]])

                for d in range(2):
                    P.op("pool", lambda e, d=d: e.memset(S32[d], 0.0), writes=[DN["S%d" % d]])
                    P.op("pool", lambda e, d=d: e.memset(Sbf[d], 0.0), writes=[DN["Sb%d" % d]])

                def chain_ops(step, d):
                    c = step if d == 0 else 15 - step
                    bA_ = 2 + 4 * d
                    bB_ = 3 + 4 * d
                    pk = bank(bA_)[:, 0:128]
                    px = bank(bA_)[:, 128:256]
                    po = bank(bB_)[:, 0:128]
                    pd = bank(bB_)[:, 128:256]
                    tA, tB = BK[bA_], BK[bB_]
                    St, Sbt, rt, xt_ = DN["S%d" % d], DN["Sb%d" % d], DN["r%d" % d], DN["x%d" % d]
                    eg = egl[:, c, dh[d]:dh[d] + 1]
                    oc = OT[:, c * 128:(c + 1) * 128]
                    ops = [
                        lambda: P.op("pe", lambda e: e.matmul(pk, lhsT=QK[:, c, 0, :], rhs=Sbf[d], start=True, stop=True), reads=[DN["QK"], Sbt], writes=[tA]),
                        lambda: P.op("dve", lambda e: e.scalar_tensor_tensor(out=rbf[d], in0=pk, scalar=colC[:, c, dh[d]:dh[d] + 1], in1=vtok[:, c, :], op0=ALU.mult, op1=ALU.add),
                                     reads=[tA, SM["colC"], DN["vtok"]], writes=[rt]),
                        lambda: P.op("pe", lambda e: e.matmul(px, lhsT=T2T[:, c, d, :], rhs=rbf[d], start=True, stop=True), reads=[T2Tt[c], rt], writes=[tA]),
                        lambda: P.op("act", lambda e: e.copy(out=xbf[d], in_=px), reads=[tA], writes=[xt_]),
                        lambda: P.op("pe", lambda e: e.matmul(pd, lhsT=KDB[:, d, c, :], rhs=xbf[d], start=True, stop=True), reads=[DN["KDB"], xt_], writes=[tB]),
                        lambda: P.op("pe", lambda e: e.matmul(po, lhsT=Sbf[d], rhs=QG[:, d, c, :], start=True, stop=False), reads=[Sbt, QGt[d]], writes=[tB]),
                        lambda: P.op("pe", lambda e: e.matmul(po, lhsT=xbf[d], rhs=ATT[:, c, d, :], start=False, stop=True), reads=[xt_, ATTt[c]], writes=[tB]),
                        lambda: P.op("dve", lambda e: e.scalar_tensor_tensor(out=Sbf[d], in0=S32[d], scalar=eg, in1=pd, op0=ALU.mult, op1=ALU.add),
                                     reads=[St, SM["egl"], tB], writes=[Sbt]),
                        lambda: P.op("dve", lambda e: e.scalar_tensor_tensor(out=S32[d], in0=S32[d], scalar=eg, in1=pd, op0=ALU.mult, op1=ALU.add),
                                     reads=[St, SM["egl"], tB], writes=[St]),
                    ]
                    if step < 8:
                        ops.append(lambda: P.op("act", lambda e: e.copy(out=oc, in_=po), reads=[tB], writes=[OTt[c], W["rxp"]]))
                    else:
                        ops.append(lambda: P.op("dve", lambda e: e.tensor_tensor(out=oc, in0=oc, in1=po, op=ALU.add), reads=[tB, OTt[c]], writes=[OTt[c]]))
                    return ops

                pending = []

                def emit_some(n):
                    for _ in range(n):
                        if pending:
                            pending.pop(0)()

                def sched(step):
                    a_, b_ = chain_ops(step, 0), chain_ops(step, 1)
                    for i_ in range(len(a_)):
                        pending.append(a_[i_])
                        pending.append(b_[i_])

                for p in range(8):
                    cs = [p, 15 - p]
                    for sl in range(2):
                        unit_setup(sl, cs[sl])
                    for k in range(7):
                        for sl in range(2):
                            unit_level_mm(sl, cs[sl], k)
                        emit_some(2)
                        for sl in range(2):
                            unit_level_ev(sl, cs[sl], k)
                        emit_some(2)
                    emit_some(len(pending))
                    sched(p)
                emit_some(len(pending))
                for step in range(8, 16):
                    sched(step)
                    emit_some(len(pending))

                if stop_after == "chain":
                    dbg_dump("OT", OT, OTt, [128, 2048], F32)
                    raise _Stop()
                P.op("act", lambda e: e.activation(out=sqb, in_=OT, func=AF.Square), reads=OTt, writes=[DR["sqb"]] + ATTt)
                for tt in range(4):
                    P.op("pe", lambda e, tt=tt: e.matmul(bank(tt), lhsT=ones_bf, rhs=sqb[:, tt * 512:(tt + 1) * 512], start=True, stop=True),
                         reads=[CBt, DR["sqb"]], writes=[BK[tt]])
                P.op("act", lambda e: e.activation(out=F0, in_=bA, func=AF.Ln, bias=EPS, scale=1.0 / 128.0), reads=bAt, writes=[W["F0"]])
                P.op("act", lambda e: e.activation(out=F0, in_=F0, func=AF.Exp, scale=-0.5), reads=[W["F0"]], writes=[W["F0"]])
                wv, wt = load_w(win[l, :, :, C_Z + hd * 128:C_Z + (hd + 1) * 128], [128, 8, 128])
                bB = bank(4, 4)
                proj(bB, BK[4:8], wv, wt, hfn, Ht, 8)
                P.op("act", lambda e: e.activation(out=F1, in_=bB, func=AF.Tanh, scale=0.5), reads=BK[4:8], writes=[W["F1"]])
                P.op("dve", lambda e: e.scalar_tensor_tensor(out=F2, in0=F1, scalar=1.0, in1=bB, op0=ALU.add, op1=ALU.mult),
                     reads=[W["F1"]] + BK[4:8], writes=[W["F2"]])
                P.op("dve", lambda e: e.scalar_tensor_tensor(out=F1, in0=OT, scalar=dnnorm[:, l:l + 1], in1=F0, op0=ALU.mult, op1=ALU.mult),
                     reads=OTt + [VEC, W["F0"]], writes=[W["F1"]])
                P.op("dve", lambda e, hd=hd: e.scalar_tensor_tensor(out=RYb[:, hd, :], in0=F1, scalar=0.5, in1=F2, op0=ALU.mult, op1=ALU.mult),
                     reads=[W["F1"], W["F2"]], writes=[RYt[hd]])
            if s == 0 and l == 0:
                dbg_dump("yd", RYb[:, 0:8, :], RYt[0:8], [128, 8, SEQ], BF16)
            if stop_after == "dn":
                raise _Stop()

            MIXB = [RYtail[:, 0:4096].rearrange("p (c t) -> p c t", c=8), RYtail[:, 4096:8192].rearrange("p (c t) -> p c t", c=8)]
            MIXt = [T(), T()]
            P.alias(MIXt, list(DR.values()) + T2Tt + ATTt)
            thg = [QGbuf[:, 0:1024].bitcast(F32), QGbuf[:, 1024:2048].bitcast(F32)]
            mst = [QGbuf[:, 2048:2560], QGbuf[:, 2560:3072]]
            thg_t = [T(), T()]
            mst_t = [T(), T()]
            P.alias(thg_t + mst_t, QGt)
            allwork = list(W.values()) + list(DN.values()) + OTt
            P.alias(Xt, allwork)
            bi = [0]

            def nb():
                b = bi[0]
                bi[0] = (b + 1) % 8
                return bank(b), BK[b]
            for tt in range(4):
                mb, mbt = MIXB[tt % 2], MIXt[tt % 2]
                for n in range(8):
                    wv, wt = load_w(wdn[l, :, :, n * 128:(n + 1) * 128], [128, 8, 128])
                    wv2, wt2 = load_w(win[l, :, :, C_GD + n * 128:C_GD + (n + 1) * 128], [128, 8, 128])
                    py, pyt = nb()
                    pg, pgt = nb()
                    for k in range(8):
                        P.op("pe", lambda e, k=k, py=py, wv=wv: e.matmul(py, lhsT=wv[:, k, :], rhs=RYb[:, k, tt * 512:(tt + 1) * 512], start=(k == 0), stop=(k == 7)),
                             reads=[wt, RYt[k]], writes=[pyt])
                    for k in range(8):
                        P.op("pe", lambda e, k=k, pg=pg, wv2=wv2: e.matmul(pg, lhsT=wv2[:, k, :], rhs=RH[:, k, tt * 512:(tt + 1) * 512], start=(k == 0), stop=(k == 7)),
                             reads=[wt2, Ht[k]], writes=[pgt])
                    j = n % 2
                    P.op("act", lambda e, j=j, pg=pg: e.activation(out=thg[j], in_=pg, func=AF.Tanh, scale=0.5), reads=[pgt], writes=[thg_t[j]])
                    P.dma("sp", mst[j], mixspill[n, :, tt * 512:(tt + 1) * 512], reads=[MSt[n]], writes=[mst_t[j]])
                    P.op("dve", lambda e, j=j, py=py: e.scalar_tensor_tensor(out=thg[j], in0=thg[j], scalar=1.0, in1=py, op0=ALU.add, op1=ALU.mult),
                         reads=[thg_t[j], pyt], writes=[thg_t[j]])
                    P.op("dve", lambda e, j=j, n=n, mb=mb: e.tensor_tensor(out=mb[:, n, :], in0=thg[j], in1=mst[j], op=ALU.add),
                         reads=[thg_t[j], mst_t[j]], writes=[mbt])
                for n in range(8):
                    wv, wt = load_w(wout[l, :, :, n * 128:(n + 1) * 128], [128, 8, 128])
                    po_, pot = nb()
                    for k in range(8):
                        P.op("pe", lambda e, k=k, po_=po_, wv=wv, mb=mb: e.matmul(po_, lhsT=wv[:, k, :], rhs=mb[:, k, :], start=(k == 0), stop=(k == 7)),
                             reads=[wt, mbt], writes=[pot])
                    xs = Xv[:, n, tt * 512:(tt + 1) * 512]
                    P.dma("sp", xs, xspill[:, n, tt * 512:(tt + 1) * 512], reads=[XSt[n]], writes=[Xt[n]])
                    P.op("dve", lambda e, xs=xs, po_=po_: e.scalar_tensor_tensor(out=xs, in0=po_, scalar=0.5, in1=xs, op0=ALU.mult, op1=ALU.add),
                         reads=[pot, Xt[n]], writes=[Xt[n]])
            P.alias(RYt, MIXt + RYt)
            P.alias(QGt, thg_t + mst_t)
            if s == 0 and l == 0:
                dbg_dump("xmix", Xv, Xt, [128, 8, SEQ], F32)
            if stop_after == "mix":
                raise _Stop()

            rstd, tmp_t = rms_stats(1.0 / D)
            for c in range(8):
                g = norms[:, (l * 2 + 1) * 8 + c:(l * 2 + 1) * 8 + c + 1]
                P.op("dve", lambda e, c=c, g=g: e.scalar_tensor_tensor(out=RH[:, c, :], in0=Xv[:, c, :], scalar=g, in1=rstd, op0=ALU.mult, op1=ALU.mult),
                     reads=[Xt[c], tmp_t[9], VEC], writes=[Ht[c]])
            ACTb = RY[:, 0:22528].rearrange("p (f t) -> p f t", f=22)
            ACt = [T() for _ in range(22)]
            P.alias(ACt, tmp_t)
            thf = [QGbuf[:, 0:2048].bitcast(F32), QGbuf[:, 2048:4096].bitcast(F32)]
            thf_t = [T(), T()]
            P.alias(thf_t, QGt)
            for half in range(2):
                t0 = half * 1024
                for f in range(22):
                    wv, wt = load_w(wgu[l, :, :, f * 128:(f + 1) * 128], [128, 8, 128])
                    wv2, wt2 = load_w(wgu[l, :, :, DFF + f * 128:DFF + (f + 1) * 128], [128, 8, 128])
                    pb, pt = grp()
                    for k in range(8):
                        for q_ in range(2):
                            P.op("pe", lambda e, k=k, q_=q_, pb=pb, wv=wv: e.matmul(pb[:, q_ * 512:(q_ + 1) * 512], lhsT=wv[:, k, :],
                                                                                     rhs=RH[:, k, t0 + q_ * 512:t0 + (q_ + 1) * 512], start=(k == 0), stop=(k == 7)),
                                 reads=[wt, Ht[k]], writes=[pt[q_]])
                    for k in range(8):
                        for q_ in range(2):
                            P.op("pe", lambda e, k=k, q_=q_, pb=pb, wv2=wv2: e.matmul(pb[:, 1024 + q_ * 512:1024 + (q_ + 1) * 512], lhsT=wv2[:, k, :],
                                                                                       rhs=RH[:, k, t0 + q_ * 512:t0 + (q_ + 1) * 512], start=(k == 0), stop=(k == 7)),
                                 reads=[wt2, Ht[k]], writes=[pt[2 + q_]])
                    j = f % 2
                    P.op("act", lambda e, j=j, pb=pb: e.activation(out=thf[j], in_=pb[:, 0:1024], func=AF.Tanh, scale=0.5), reads=pt[0:2], writes=[thf_t[j]])
                    P.op("dve", lambda e, j=j, pb=pb: e.scalar_tensor_tensor(out=thf[j], in0=thf[j], scalar=1.0, in1=pb[:, 0:1024], op0=ALU.add, op1=ALU.mult),
                         reads=[thf_t[j]] + pt[0:2], writes=[thf_t[j]])
                    P.op("dve", lambda e, j=j, pb=pb, f=f: e.scalar_tensor_tensor(out=ACTb[:, f, :], in0=thf[j], scalar=0.5, in1=pb[:, 1024:2048], op0=ALU.mult, op1=ALU.mult),
                         reads=[thf_t[j]] + pt[2:4], writes=[ACt[f]])
                for n in range(8):
                    wv, wt = load_w(wdown[l, :, 0:11, n * 128:(n + 1) * 128], [128, 11, 128])
                    wv2, wt2 = load_w(wdown[l, :, 11:22, n * 128:(n + 1) * 128], [128, 11, 128])
                    pb, pt = grp()
                    for f in range(22):
                        for q_ in range(2):
                            w_, wt_ = (wv, wt) if f < 11 else (wv2, wt2)
                            P.op("pe", lambda e, f=f, q_=q_, pb=pb, w_=w_: e.matmul(pb[:, q_ * 512:(q_ + 1) * 512], lhsT=w_[:, f % 11, :],
                                                                                     rhs=ACTb[:, f, q_ * 512:(q_ + 1) * 512], start=(f == 0), stop=(f == 21)),
                                 reads=[wt_, ACt[f]], writes=[pt[q_]])
                    xs = Xv[:, n, t0:t0 + 1024]
                    P.op("dve", lambda e, xs=xs, pb=pb: e.tensor_tensor(out=xs, in0=xs, in1=pb[:, 0:1024], op=ALU.add), reads=[Xt[n]] + pt[0:2], writes=[Xt[n]])
            P.alias(RYt, ACt)
            P.alias(QGt, thf_t)
            if s == 0:
                dbg_dump("xl%d" % l, Xv, Xt, [128, 8, SEQ], F32)
        if stop_after:
            raise _Stop()

        rstd, tmp_t = rms_stats(1.0 / D)
        ost = [RYf[:, 0:2048], RYf[:, 2048:4096]]
        ost_t = [T(), T()]
        P.alias(ost_t, tmp_t[0:8])
        for c in range(8):
            j = c % 2
            P.op("dve", lambda e, c=c, j=j: e.scalar_tensor_tensor(out=ost[j], in0=Xv[:, c, :], scalar=fnorm[:, c:c + 1], in1=rstd, op0=ALU.mult, op1=ALU.mult),
                 reads=[Xt[c], tmp_t[9], VEC], writes=[ost_t[j]])
            P.dma("sp", outT[s, :, c, :], ost[j], reads=[ost_t[j]], is_output=True)
        P.alias(RYt, tmp_t + ost_t)

    return


def prep_weights(w_in, rg_conv_w, rg_conv_b, rg_wa, rg_ba, rg_wx, rg_bx, rg_lambda, w_rnn_proj, dn_conv_w,
                 dn_a_log, dn_dt_bias, dn_norm, w_dn_proj, w_out, mix_norm, ffn_norm, w_gate_up, w_down, final_norm):
    L = NLAYER
    f = lambda a: np.ascontiguousarray(np.asarray(a, dtype=np.float32))
    m = {}
    m["win"] = f(np.asarray(w_in).reshape(L, 8, 128, NIN).transpose(0, 2, 1, 3))
    wa = np.asarray(rg_wa)
    wx = np.asarray(rg_wx)
    g = np.stack([wa[:, 0], wa[:, 1], wx[:, 0], wx[:, 1]], axis=3)
    m["gatew"] = f(g)
    cwv = np.asarray(rg_conv_w).reshape(L, 4, 12, 128).transpose(3, 0, 2, 1)
    pv = lambda a: np.asarray(a).reshape(L, 12, 128).transpose(2, 0, 1)[..., None]
    pv2 = lambda a, d: np.asarray(a)[:, d].reshape(L, 12, 128).transpose(2, 0, 1)[..., None]
    rgv = np.concatenate([cwv, pv(rg_conv_b), pv2(rg_ba, 0), pv2(rg_ba, 1), pv2(rg_bx, 0), pv2(rg_bx, 1),
                          pv2(rg_lambda, 0), pv2(rg_lambda, 1)], axis=3)
    m["rgv"] = f(rgv.reshape(128, L * 12, 11))
    m["wrnn"] = f(np.asarray(w_rnn_proj).reshape(L, 12, 128, D).transpose(0, 2, 1, 3))
    m["dncw"] = f(np.asarray(dn_conv_w).reshape(L, 4, 24, 128).transpose(3, 0, 2, 1).reshape(128, L * 24, 4))
    dv = np.stack([np.asarray(dn_a_log).reshape(L, 16), np.asarray(dn_dt_bias).reshape(L, 16)], axis=1)
    m["dnv"] = f(np.broadcast_to(dv[None], (128, L, 2, 16)))
    m["dnnorm"] = f(np.asarray(dn_norm).T)
    m["wdn"] = f(np.asarray(w_dn_proj).reshape(L, 8, 128, D).transpose(0, 2, 1, 3))
    m["wout"] = f(np.asarray(w_out).reshape(L, 8, 128, D).transpose(0, 2, 1, 3))
    nm = np.stack([np.asarray(mix_norm).reshape(L, 8, 128), np.asarray(ffn_norm).reshape(L, 8, 128)], axis=1)
    m["norms"] = f(nm.transpose(3, 0, 1, 2).reshape(128, L * 2 * 8))
    m["fnorm"] = f(np.asarray(final_norm).reshape(8, 128).T)
    m["wgu"] = f(np.asarray(w_gate_up).reshape(L, 8, 128, 2 * DFF).transpose(0, 2, 1, 3))
    m["wdown"] = f(np.asarray(w_down).reshape(L, 22, 128, D).transpose(0, 2, 1, 3))
    m["consts"] = make_consts()
    return m


def prep_x(xs):
    n = xs.shape[0]
    return np.ascontiguousarray(np.asarray(xs, dtype=np.float32).transpose(0, 2, 1).reshape(n, 8, 128, SEQ).transpose(0, 2, 1, 3))


def unprep_x(o):
    n = o.shape[0]
    return np.ascontiguousarray(o.transpose(0, 2, 1, 3).reshape(n, D, SEQ).transpose(0, 2, 1))


def kernel(x, mix_norm, w_in, rg_conv_w, rg_conv_b, rg_wa, rg_ba, rg_wx, rg_bx, rg_lambda,
           w_rnn_proj, dn_conv_w, dn_a_log, dn_dt_bias, dn_norm, w_dn_proj, w_out,
           ffn_norm, w_gate_up, w_down, final_norm):
    x = np.asarray(x)
    B = x.shape[0]
    nseq = B // NCORES
    wm = prep_weights(w_in, rg_conv_w, rg_conv_b, rg_wa, rg_ba, rg_wx, rg_bx, rg_lambda, w_rnn_proj, dn_conv_w,
                      dn_a_log, dn_dt_bias, dn_norm, w_dn_proj, w_out, mix_norm, ffn_norm, w_gate_up, w_down, final_norm)
    nc = bass.Bass("TRN2", target_bir_lowering=False)
    build(nc, NSEQ=nseq)
    in_maps = []
    for cidx in range(NCORES):
        mm = dict(wm)
        mm["xT"] = prep_x(x[cidx * nseq:(cidx + 1) * nseq])
        in_maps.append(mm)
    res = run_bass_kernel_spmd(nc, in_maps, core_ids=list(range(NCORES)))
    outs = [unprep_x(np.asarray(r["outT"])) for r in res.results]
    return np.concatenate(outs, axis=0).astype(np.float32)
```

```python
import numpy as np
import concourse.bass as bass
import concourse.mybir as mybir
from concourse.bass_utils import run_bass_kernel_spmd

F32 = mybir.dt.float32
F32R = mybir.dt.float32
BF16 = mybir.dt.bfloat16
ALU = mybir.AluOpType
AF = mybir.ActivationFunctionType

NCORES = 8
D = 1024
SEQ = 2048
NLAYER = 4
DRNN = 1536
DFF = 2816
NIN = 9248
C_RY = 1536
C_QKV = 3072
C_Z = 6144
C_BA = 7168
C_GR = 7200
C_GD = 8224
EPS = 1e-6
NEG = -30000.0


class T:
    __slots__ = ("ap", "w", "r", "ps")

    def __init__(self, ap=None, ps=False):
        self.ap = ap
        self.w = None
        self.r = {}
        self.ps = ps


class _Rec:
    def __init__(self):
        self.call = None

    def __getattr__(self, name):
        def f(*a, **k):
            self.call = (name, a, k)
            return self
        return f


class Eng:
    def __init__(self, name, sem):
        self.name = name
        self.sem = sem
        self.key = ("e", name)
        self.count = 0
        self.waited = {}
        self.ops = []


class Prog:
    NDMASEM = 24

    def __init__(self, nc):
        self.nc = nc
        self.sems = {}
        self.E = {}
        for n in ("pe", "dve", "act", "pool", "sp"):
            s = nc.alloc_semaphore(name="s_" + n)
            e = Eng(n, s)
            self.E[n] = e
            self.sems[e.key] = s
        self.dsem = []
        for i in range(self.NDMASEM):
            s = nc.alloc_semaphore(name="d%d" % i)
            self.dsem.append([s, 0])
            self.sems[("d", i)] = s
        self.dnext = 0
        self.out_tokens = []
        self.ninst = 0

    def _deps(self, reads, writes):
        need = {}
        for t in reads:
            if t.w is not None:
                k, v = t.w
                if need.get(k, 0) < v:
                    need[k] = v
        for t in writes:
            if t.w is not None:
                k, v = t.w
                if need.get(k, 0) < v:
                    need[k] = v
            for k, v in t.r.items():
                if need.get(k, 0) < v:
                    need[k] = v
        return need

    def _emit_waits(self, e, need, skip_self=False):
        for k, v in need.items():
            if skip_self and k == e.key:
                continue
            if e.waited.get(k, 0) < v:
                e.waited[k] = v
                e.ops.append(("w", k, v))

    def _mark(self, tok, reads, writes):
        k, v = tok
        for t in reads:
            if t.r.get(k, 0) < v:
                t.r[k] = v
        for t in writes:
            t.w = tok
            t.r = {}

    def op(self, eng, fn, reads=(), writes=()):
        e = self.E[eng]
        psr = [t for t in reads if t.ps]
        if psr:
            reads = [t for t in reads if not t.ps]
            writes = list(writes) + psr
        need = self._deps(reads, writes)
        self._emit_waits(e, need, skip_self=(eng == "pe"))
        e.count += 1
        tok = (e.key, e.count)
        rec = _Rec()
        fn(rec)
        e.ops.append(("i", rec.call))
        self._mark(tok, reads, writes)
        self.ninst += 1
        return tok

    def dma(self, q, out_ap, in_ap, reads=(), writes=(), is_output=False):
        e = self.E[q]
        i = self.dnext
        self.dnext = (self.dnext + 1) % self.NDMASEM
        ds = self.dsem[i]
        need = self._deps(reads, writes)
        k = ("d", i)
        if ds[1] > 0 and need.get(k, 0) < ds[1]:
            need[k] = ds[1]
        self._emit_waits(e, need)
        ds[1] += 16
        tok = (k, ds[1])
        e.ops.append(("d", out_ap, in_ap, k))
        self._mark(tok, reads, writes)
        if is_output:
            self.out_tokens.append(tok)
        self.ninst += 1
        return tok

    def alias(self, new_tiles, old_tiles):
        acc = {}
        for t in old_tiles:
            if t.w is not None:
                k, v = t.w
                if acc.get(k, 0) < v:
                    acc[k] = v
            for k, v in t.r.items():
                if acc.get(k, 0) < v:
                    acc[k] = v
        for t in new_tiles:
            t.w = None
            t.r = dict(acc)

    def finish(self):
        e = self.E["sp"]
        need = {}
        for k, v in self.out_tokens:
            if need.get(k, 0) < v:
                need[k] = v
        self._emit_waits(e, need)
        nc = self.nc
        sems = self.sems

        def run(e, eng):
            for o in e.ops:
                if o[0] == "w":
                    eng.wait_ge(sems[o[1]], o[2])
                elif o[0] == "i":
                    name, a, k = o[1]
                    getattr(eng, name)(*a, **k).then_inc(e.sem, 1)
                else:
                    eng.dma_start(out=o[1], in_=o[2]).then_inc(sems[o[3]], 16)

        with nc.Block() as block:
            @block.tensor
            def _(eng):
                run(self.E["pe"], eng)

            @block.vector
            def _(eng):
                run(self.E["dve"], eng)

            @block.scalar
            def _(eng):
                run(self.E["act"], eng)

            @block.gpsimd
            def _(eng):
                run(self.E["pool"], eng)

            @block.sync
            def _(eng):
                run(self.E["sp"], eng)


CONST_NAMES = ["ident", "ones", "Lf", "Lb", "SUf", "SUb", "nsf", "nsb", "inf", "inb"]


def make_consts():
    t = np.arange(128)[:, None]
    i = np.arange(128)[None, :]
    c = {
        "ident": (t == i),
        "ones": np.ones((128, 128)),
        "Lf": (t <= i),
        "Lb": (t >= i),
        "SUf": (t > i),
        "SUb": (t < i),
        "nsf": -1.0 * (i > t),
        "nsb": -1.0 * (i < t),
        "inf": (i >= t),
        "inb": (i <= t),
    }
    return np.ascontiguousarray(np.stack([np.asarray(c[n], dtype=np.float32) for n in CONST_NAMES], axis=1))


class _Stop(Exception):
    pass


def build(nc, NSEQ=4, NL=NLAYER, dbg=None, stop_after=None, heads=8, rg_blocks=12):
    P = Prog(nc)
    try:
        _build(P, nc, NSEQ, NL, dbg, stop_after, heads, rg_blocks)
    except _Stop:
        pass
    P.finish()
    return P, None


def _build(P, nc, NSEQ, NL, dbg, stop_after, heads, rg_blocks):
    dbg = dbg or []
    dbg_out = {}

    def din(name, shape, dt=F32):
        return nc.dram_tensor(name, shape, dt, kind="ExternalInput").ap()

    xT = din("xT", [NSEQ, 128, 8, SEQ])
    outT = nc.dram_tensor("outT", [NSEQ, 128, 8, SEQ], F32, kind="ExternalOutput").ap()
    win = din("win", [NLAYER, 128, 8, NIN])
    gatew = din("gatew", [NLAYER, 12, 128, 4, 128])
    rgv_d = din("rgv", [128, NLAYER * 12, 11])
    wrnn = din("wrnn", [NLAYER, 128, 12, D])
    dncw_d = din("dncw", [128, NLAYER * 24, 4])
    dnv_d = din("dnv", [128, NLAYER, 2, 16])
    dnnorm_d = din("dnnorm", [128, NLAYER])
    wdn = din("wdn", [NLAYER, 128, 8, D])
    wout = din("wout", [NLAYER, 128, 8, D])
    norms_d = din("norms", [128, NLAYER * 2 * 8])
    fnorm_d = din("fnorm", [128, 8])
    wgu = din("wgu", [NLAYER, 128, 8, 2 * DFF])
    wdown = din("wdown", [NLAYER, 128, 22, D])
    consts_d = din("consts", [128, len(CONST_NAMES), 128])
    xspill = nc.dram_tensor("xspill", [128, 8, SEQ], F32, kind="Internal").ap()
    mixspill = nc.dram_tensor("mixspill", [8, 128, SEQ], BF16, kind="Internal").ap()
    XSt = [T() for _ in range(8)]
    MSt = [T() for _ in range(8)]

    def sb(name, shape, dt=F32):
        return nc.alloc_sbuf_tensor("s_" + name, shape, dt).ap()

    def dbg_dump(name, ap, tiles, shape, dt=F32):
        if name not in dbg:
            return
        o = nc.dram_tensor("dbg_" + name, shape, dt, kind="ExternalOutput").ap()
        P.dma("sp", o, ap, reads=tiles, is_output=True)
        dbg_out[name] = o

    RX = sb("RX", [128, 16384], F32)
    RH = sb("RH", [128, 8, SEQ], BF16)
    RY = sb("RY", [128, 24576], BF16)
    WP = [sb("WP%d" % i, [128, 2048], BF16) for i in range(3)]
    WPt = [T() for _ in range(3)]
    GW = [sb("GW%d" % i, [128, 4, 128], BF16) for i in range(2)]
    GWt = [T() for _ in range(2)]
    wp_i = [0]
    gw_i = [0]
    CM = sb("CM", [128, len(CONST_NAMES), 128], F32)
    CMt = T()
    cm = {n: CM[:, i, :] for i, n in enumerate(CONST_NAMES)}
    ones_bf = sb("ones_bf", [128, 128], BF16)
    ident_bf = sb("ident_bf", [128, 128], BF16)
    CBt = T()
    mskA = sb("mskA", [128, 2, 128], F32)
    mskB = sb("mskB", [128, 2, 128], F32)
    rgv = sb("rgv", [128, NLAYER * 12, 11], F32)
    rgd = sb("rgd", [128, NLAYER * 12, 10], F32)
    rgtmp = sb("rgtmp", [128, NLAYER * 12, 2], F32)
    dncw = sb("dncw", [128, NLAYER * 24, 4], F32)
    dnv = sb("dnv", [128, NLAYER, 2, 16], F32)
    nega = sb("nega", [128, NLAYER, 16], F32)
    dnnorm = sb("dnnorm", [128, NLAYER], F32)
    norms = sb("norms", [128, NLAYER * 2 * 8], F32)
    fnorm = sb("fnorm", [128, 8], F32)
    VEC = T()
    QGbuf = sb("QGbuf", [128, 4096], BF16)
    SMbuf = sb("SMbuf", [128, 512 + 8 * 256], F32)
    YQ = [[sb("YQ%d_%d" % (sl, j), [128, 2, 256], F32) for j in range(2)] for sl in range(2)]
    YTb = [[sb("YT%d_%d" % (sl, j), [128, 2, 128], F32) for j in range(2)] for sl in range(2)]
    KQA = [sb("KQA%d" % sl, [128, 2, 128], BF16) for sl in range(2)]
    KQB = [sb("KQB%d" % sl, [128, 2, 128], BF16) for sl in range(2)]
    YQt = [[T() for _ in range(2)] for _ in range(4)]
    YTt = [[T() for _ in range(2)] for _ in range(4)]
    KQAt = [T() for _ in range(4)]
    KQBt = [T() for _ in range(4)]
    Dgt = [T() for _ in range(4)]
    D2t = [T() for _ in range(4)]

    PS = nc.alloc_psum_tensor("PS", [128, 4096], F32).ap()
    BK = [T(ps=True) for _ in range(8)]

    def bank(b, n=1):
        return PS[:, b * 512:(b + n) * 512]

    P.dma("sp", CM, consts_d, writes=[CMt])
    for dst, src in ((rgv, rgv_d), (dncw, dncw_d), (dnv, dnv_d), (dnnorm, dnnorm_d), (norms, norms_d), (fnorm, fnorm_d)):
        P.dma("sp", dst, src, writes=[VEC])
    P.op("dve", lambda e: e.tensor_copy(out=ones_bf, in_=cm["ones"]), reads=[CMt], writes=[CBt])
    P.op("dve", lambda e: e.tensor_copy(out=ident_bf, in_=cm["ident"]), reads=[CMt], writes=[CBt])
    P.op("dve", lambda e: e.tensor_copy(out=mskA[:, 0, :], in_=cm["nsf"]), reads=[CMt], writes=[CBt])
    P.op("dve", lambda e: e.tensor_copy(out=mskA[:, 1, :], in_=cm["inf"]), reads=[CMt], writes=[CBt])
    P.op("dve", lambda e: e.tensor_copy(out=mskB[:, 0, :], in_=cm["nsb"]), reads=[CMt], writes=[CBt])
    P.op("dve", lambda e: e.tensor_copy(out=mskB[:, 1, :], in_=cm["inb"]), reads=[CMt], writes=[CBt])
    P.op("act", lambda e: e.mul(out=rgd[:, :, 0:4], in_=rgv[:, :, 5:9], mul=0.5), reads=[VEC], writes=[VEC])
    P.op("act", lambda e: e.activation(out=rgtmp, in_=rgv[:, :, 9:11], func=AF.Exp, scale=-1.0), reads=[VEC], writes=[VEC])
    P.op("act", lambda e: e.activation(out=rgtmp, in_=rgtmp, func=AF.Ln, bias=1.0), reads=[VEC], writes=[VEC])
    P.op("act", lambda e: e.mul(out=rgd[:, :, 4:6], in_=rgtmp, mul=-4.0), reads=[VEC], writes=[VEC])
    P.op("act", lambda e: e.mul(out=rgd[:, :, 6:8], in_=rgtmp, mul=-8.0), reads=[VEC], writes=[VEC])
    P.op("act", lambda e: e.mul(out=rgd[:, :, 8:10], in_=rgtmp, mul=4.0), reads=[VEC], writes=[VEC])
    P.op("act", lambda e: e.activation(out=nega, in_=dnv[:, :, 0, :], func=AF.Exp), reads=[VEC], writes=[VEC])
    P.op("dve", lambda e: e.tensor_scalar(out=nega, in0=nega, scalar1=-1.0, scalar2=None, op0=ALU.mult), reads=[VEC], writes=[VEC])

    def load_w(src_ap, shape):
        i = wp_i[0]
        wp_i[0] = (i + 1) % len(WP)
        n = int(np.prod(shape[1:]))
        view = WP[i][:, 0:n].rearrange("p (a b) -> p a b", a=shape[1])
        P.dma("pool", view, src_ap, writes=[WPt[i]])
        return view, WPt[i]

    def proj(pb, ptiles, wv, wt, rhs_fn, rhs_tiles, nk, ncols=SEQ):
        nt = ncols // 512
        for k in range(nk):
            for tt in range(nt):
                P.op("pe", lambda e, tt=tt, k=k: e.matmul(pb[:, tt * 512:(tt + 1) * 512], lhsT=wv[:, k, :], rhs=rhs_fn(k, tt),
                                                            start=(k == 0), stop=(k == nk - 1)),
                     reads=[wt] + rhs_tiles, writes=[ptiles[tt]])

    QGt = [T(), T()]
    SM = {nm: T() for nm in ["bt", "lnb", "g", "gc", "colE", "colC", "colD", "egl", "tmp"]}
    Xv = RX.rearrange("p (c t) -> p c t", c=8)
    Xt = [T() for _ in range(8)]
    Ht = [T() for _ in range(8)]
    RYt = [T() for _ in range(12)]
    RYb = RY.rearrange("p (c t) -> p c t", c=12)
    RYf = RY.bitcast(F32)
    hfn = lambda k, tt: RH[:, k, tt * 512:(tt + 1) * 512]
    pgrp = [0]

    def grp():
        g_ = pgrp[0]
        pgrp[0] ^= 1
        return bank(4 * g_, 4), BK[4 * g_:4 * g_ + 4]

    def rms_stats(scale):
        tmp_t = [T() for _ in range(10)]
        P.alias(tmp_t, RYt)
        lnv = RYf[:, 8192:10240]
        rstd = RYf[:, 10240:12288]
        for c in range(8):
            P.op("act", lambda e, c=c: e.activation(out=RYb[:, c, :], in_=Xv[:, c, :], func=AF.Square), reads=[Xt[c]], writes=[tmp_t[c]])
        for tt in range(4):
            for c in range(8):
                P.op("pe", lambda e, c=c, tt=tt: e.matmul(bank(tt), lhsT=ones_bf, rhs=RYb[:, c, tt * 512:(tt + 1) * 512],
                                                            start=(c == 0), stop=(c == 7)),
                     reads=[CBt, tmp_t[c]], writes=[BK[tt]])
        P.op("act", lambda e: e.activation(out=lnv, in_=bank(0, 4), func=AF.Ln, bias=EPS, scale=scale), reads=BK[0:4], writes=[tmp_t[8]])
        P.op("act", lambda e: e.activation(out=rstd, in_=lnv, func=AF.Exp, scale=-0.5), reads=[tmp_t[8]], writes=[tmp_t[9]])
        return rstd, tmp_t

    rxp = RX[:, 0:2052]
    F0 = RX[:, 2052:4100]
    F1 = RX[:, 4100:6148]
    F2 = RX[:, 6148:8196]
    HF = RX[:, 8196:10244]
    HB = RX[:, 10244:12292]
    ub = RX[:, 12292:13316].bitcast(BF16)
    thi = RX[:, 13316:14340].bitcast(BF16)
    sqv = RX[:, 14340:15364].bitcast(BF16)
    QK = RX[:, 8196:10244].bitcast(BF16).rearrange("p (c two d) -> p c two d", c=16, two=2)
    QKf = RX[:, 8196:10244].bitcast(BF16).rearrange("p (c n) -> p c n", c=16)
    ktok = RX[:, 10244:11268].bitcast(BF16).rearrange("p (c d) -> p c d", c=16)
    vtok = RX[:, 11268:12292].bitcast(BF16).rearrange("p (c d) -> p c d", c=16)
    KDB = RX[:, 12292:14340].bitcast(BF16).rearrange("p (two c d) -> p two c d", two=2, c=16)
    tb = 14340
    S32 = [RX[:, tb:tb + 128], RX[:, tb + 128:tb + 256]]
    Sbf = [RX[:, tb + 256:tb + 320].bitcast(BF16), RX[:, tb + 320:tb + 384].bitcast(BF16)]
    rbf = [RX[:, tb + 384:tb + 448].bitcast(BF16), RX[:, tb + 448:tb + 512].bitcast(BF16)]
    xbf = [RX[:, tb + 512:tb + 576].bitcast(BF16), RX[:, tb + 576:tb + 640].bitcast(BF16)]
    NS = 2
    Dg = [RX[:, tb + 640:tb + 896].rearrange("p (a b) -> p a b", a=2), RX[:, tb + 896:tb + 1152].rearrange("p (a b) -> p a b", a=2)]
    D2 = [RX[:, tb + 1152:tb + 1280].bitcast(BF16).rearrange("p (a b) -> p a b", a=2),
          RX[:, tb + 1280:tb + 1408].bitcast(BF16).rearrange("p (a b) -> p a b", a=2)]
    for b0 in (2052, 4228):
        YQ.append([RX[:, b0:b0 + 512].rearrange("p (a b) -> p a b", a=2), RX[:, b0 + 512:b0 + 1024].rearrange("p (a b) -> p a b", a=2)])
        YTb.append([RX[:, b0 + 1024:b0 + 1280].rearrange("p (a b) -> p a b", a=2), RX[:, b0 + 1280:b0 + 1536].rearrange("p (a b) -> p a b", a=2)])
        Dg.append(RX[:, b0 + 1536:b0 + 1792].rearrange("p (a b) -> p a b", a=2))
        KQA.append(RX[:, b0 + 1792:b0 + 1920].bitcast(BF16).rearrange("p (a b) -> p a b", a=2))
        KQB.append(RX[:, b0 + 1920:b0 + 2048].bitcast(BF16).rearrange("p (a b) -> p a b", a=2))
        D2.append(RX[:, b0 + 2048:b0 + 2176].bitcast(BF16).rearrange("p (a b) -> p a b", a=2))
    YQr = [[a_.bitcast(F32R) for a_ in sl_] for sl_ in YQ[0:NS]]
    YTr = [[a_.bitcast(F32R) for a_ in sl_] for sl_ in YTb[0:NS]]
    Dgq = [RX[:, tb + 1408:tb + 1536], RX[:, tb + 1536:tb + 1664]]
    QG = QGbuf.rearrange("p (two c d) -> p two c d", two=2, c=16)
    RYtail = RY[:, 16384:24576]
    qsb, ksb, vsb, sqb = (RYtail[:, 0:2048], RYtail[:, 2048:4096], RYtail[:, 4096:6144], RYtail[:, 6144:8192])
    T2T = RYtail[:, 0:4096].rearrange("p (c two d) -> p c two d", c=16, two=2)
    ATT = RYtail[:, 4096:8192].rearrange("p (c two d) -> p c two d", c=16, two=2)
    OT = RX[:, 0:2048]
    bt = SMbuf[:, 0:512].rearrange("p (c k) -> p c k", c=16)
    smv = lambda i: SMbuf[:, 512 + 256 * i:512 + 256 * (i + 1)].rearrange("p (c k) -> p c k", c=16)
    lnb, gg, gc, colE, colC, colD, egl, stmp = [smv(i) for i in range(8)]

    for s in range(NSEQ):
        for c in range(8):
            P.dma("sp", Xv[:, c, :], xT[s, :, c, :], writes=[Xt[c]])

        for l in range(NL):
            rstd, tmp_t = rms_stats(1.0 / D)
            for c in range(8):
                g = norms[:, (l * 2 + 0) * 8 + c:(l * 2 + 0) * 8 + c + 1]
                P.op("dve", lambda e, c=c, g=g: e.scalar_tensor_tensor(out=RH[:, c, :], in0=Xv[:, c, :], scalar=g, in1=rstd,
                                                                         op0=ALU.mult, op1=ALU.mult),
                     reads=[Xt[c], tmp_t[9], VEC], writes=[Ht[c]])
            P.alias(RYt, tmp_t)
            if s == 0 and l == 0:
                dbg_dump("h", RH, Ht, [128, 8, SEQ], BF16)
            for c in range(8):
                P.dma("sp", xspill[:, c, :], Xv[:, c, :], reads=[Xt[c]], writes=[XSt[c]])
            W = {k_: T() for k_ in ["rxp", "F0", "F1", "F2", "HF", "HB", "ub", "thi", "sq"]}
            P.alias(list(W.values()), Xt)
            W["F1b"] = T()
            W["F2b"] = T()
            P.alias([W["F1b"]], QGt)
            P.alias([W["F2b"]], list(SM.values()))
            F1d = [F1, QGbuf.bitcast(F32)]
            F2d = [F2, SMbuf[:, 0:2048]]
            F1k = ["F1", "F1b"]
            F2k = ["F2", "F2b"]
            P.op("pool", lambda e: e.memset(rxp[:, 0:2], 0.0), writes=[W["rxp"]])
            P.op("pool", lambda e: e.memset(rxp[:, 2050:2052], 0.0), writes=[W["rxp"]])

            for n in range(rg_blocks):
                ln_ = l * 12 + n
                wv, wt = load_w(win[l, :, :, n * 128:(n + 1) * 128], [128, 8, 128])
                gi = gw_i[0]
                gw_i[0] ^= 1
                P.dma("pool", GW[gi], gatew[l, n], writes=[GWt[gi]])
                pb, pt = grp()
                proj(pb, pt, wv, wt, hfn, Ht, 8)
                P.op("act", lambda e, pb=pb: e.copy(out=rxp[:, 2:2050], in_=pb), reads=pt, writes=[W["rxp"]])
                cw = lambda j, ln_=ln_: rgv[:, ln_, j:j + 1]
                P.op("dve", lambda e, cw=cw: e.tensor_scalar(out=F0, in0=rxp[:, 0:2048], scalar1=cw(0), scalar2=cw(4), op0=ALU.mult, op1=ALU.add),
                     reads=[W["rxp"], VEC], writes=[W["F0"]])
                P.op("dve", lambda e, cw=cw: e.scalar_tensor_tensor(out=F1, in0=rxp[:, 1:2049], scalar=cw(1), in1=F0, op0=ALU.mult, op1=ALU.add),
                     reads=[W["rxp"], VEC, W["F0"]], writes=[W["F1"]])
                P.op("dve", lambda e, cw=cw: e.scalar_tensor_tensor(out=F0, in0=rxp[:, 2:2050], scalar=cw(2), in1=F1, op0=ALU.mult, op1=ALU.add),
                     reads=[W["rxp"], VEC, W["F1"]], writes=[W["F0"]])
                P.op("dve", lambda e, cw=cw: e.scalar_tensor_tensor(out=ub, in0=rxp[:, 3:2051], scalar=cw(3), in1=F0, op0=ALU.mult, op1=ALU.add),
                     reads=[W["rxp"], VEC, W["F0"]], writes=[W["ub"]])
                ufn = lambda k, tt: ub[:, tt * 512:(tt + 1) * 512]
                gv = GW[gi]
                for d in range(2):
                    pb, pt = grp()
                    proj(pb, pt, gv[:, d:d + 1, :], GWt[gi], ufn, [W["ub"]], 1)
                    P.op("act", lambda e, pb=pb, d=d, ln_=ln_: e.activation(out=F0, in_=pb, func=AF.Tanh, scale=0.5, bias=rgd[:, ln_, d:d + 1]),
                         reads=pt + [VEC], writes=[W["F0"]])
                    pb, pt = grp()
                    proj(pb, pt, gv[:, 2 + d:3 + d, :], GWt[gi], ufn, [W["ub"]], 1)
                    P.op("act", lambda e, pb=pb, d=d, ln_=ln_: e.activation(out=thi, in_=pb, func=AF.Tanh, scale=0.5, bias=rgd[:, ln_, 2 + d:3 + d]),
                         reads=pt + [VEC], writes=[W["thi"]])
                    F1x, F2x, k1, k2 = F1d[d], F2d[d], F1k[d], F2k[d]
                    P.op("act", lambda e, d=d, ln_=ln_, F1x=F1x: e.activation(out=F1x, in_=F0, func=AF.Exp, scale=rgd[:, ln_, 4 + d:5 + d], bias=rgd[:, ln_, 4 + d:5 + d]),
                         reads=[W["F0"], VEC], writes=[W[k1]])
                    P.op("act", lambda e, d=d, ln_=ln_, F2x=F2x: e.activation(out=F2x, in_=F0, func=AF.Tanh, scale=rgd[:, ln_, 8 + d:9 + d], bias=rgd[:, ln_, 8 + d:9 + d]),
                         reads=[W["F0"], VEC], writes=[W[k2]])
                    P.op("act", lambda e, d=d, ln_=ln_: e.activation(out=F0, in_=F0, func=AF.Exp, scale=rgd[:, ln_, 6 + d:7 + d], bias=rgd[:, ln_, 6 + d:7 + d]),
                         reads=[W["F0"], VEC], writes=[W["F0"]])
                    P.op("dve", lambda e, F2x=F2x: e.scalar_tensor_tensor(out=F0, in0=F0, scalar=1.0, in1=F2x, op0=ALU.add, op1=ALU.mult),
                         reads=[W["F0"], W[k2]], writes=[W["F0"]])
                    P.op("act", lambda e: e.activation(out=F0, in_=F0, func=AF.Ln), reads=[W["F0"]], writes=[W["F0"]])
                    P.op("act", lambda e: e.activation(out=sqv, in_=F0, func=AF.Exp, scale=0.5), reads=[W["F0"]], writes=[W["sq"]])
                    P.op("dve", lambda e, F2x=F2x: e.scalar_tensor_tensor(out=F2x, in0=thi, scalar=1.0, in1=ub, op0=ALU.add, op1=ALU.mult),
                         reads=[W["thi"], W["ub"]], writes=[W[k2]])
                    P.op("dve", lambda e, F2x=F2x: e.scalar_tensor_tensor(out=F2x, in0=F2x, scalar=0.5, in1=sqv, op0=ALU.mult, op1=ALU.mult),
                         reads=[W[k2], W["sq"]], writes=[W[k2]])
                    if d == 0:
                        P.op("dve", lambda e, F1x=F1x, F2x=F2x: e.tensor_tensor_scan(out=HF, data0=F1x, data1=F2x, initial=0.0, op0=ALU.mult, op1=ALU.add),
                             reads=[W[k1], W[k2]], writes=[W["HF"]])
                    else:
                        P.op("dve", lambda e, F1x=F1x, F2x=F2x: e.tensor_tensor_scan(out=HB[:, ::-1], data0=F1x[:, ::-1], data1=F2x[:, ::-1], initial=0.0,
                                                                     op0=ALU.mult, op1=ALU.add),
                             reads=[W[k1], W[k2]], writes=[W["HB"]])
                wv, wt = load_w(win[l, :, :, C_RY + n * 128:C_RY + (n + 1) * 128], [128, 8, 128])
                pb, pt = grp()
                proj(pb, pt, wv, wt, hfn, Ht, 8)
                P.op("act", lambda e, pb=pb: e.activation(out=F0, in_=pb, func=AF.Square), reads=pt, writes=[W["F0"]])
                P.op("dve", lambda e: e.tensor_scalar(out=F0, in0=F0, scalar1=0.044715, scalar2=1.0, op0=ALU.mult, op1=ALU.add),
                     reads=[W["F0"]], writes=[W["F0"]])
                P.op("dve", lambda e, pb=pb: e.tensor_tensor(out=F0, in0=F0, in1=pb, op=ALU.mult), reads=[W["F0"]] + pt, writes=[W["F0"]])
                P.op("act", lambda e: e.activation(out=F1, in_=F0, func=AF.Tanh, scale=0.7978845608028654), reads=[W["F0"]], writes=[W["F1"]])
                P.op("dve", lambda e: e.tensor_tensor(out=F2, in0=HF, in1=HB, op=ALU.add), reads=[W["HF"], W["HB"]], writes=[W["F2"]])
                P.op("dve", lambda e, pb=pb: e.scalar_tensor_tensor(out=F0, in0=F1, scalar=1.0, in1=pb, op0=ALU.add, op1=ALU.mult),
                     reads=[W["F1"]] + pt, writes=[W["F0"]])
                P.op("dve", lambda e, n=n: e.scalar_tensor_tensor(out=RYb[:, n, :], in0=F0, scalar=0.5, in1=F2, op0=ALU.mult, op1=ALU.mult),
                     reads=[W["F0"], W["F2"]], writes=[RYt[n]])
            if s == 0 and l == 0:
                dbg_dump("yr", RYb, RYt, [128, 12, SEQ], BF16)
            if stop_after == "rg":
                raise _Stop()

            P.alias(QGt, [W["F1b"]])
            P.alias(list(SM.values()), [W["F2b"]])
            stg = [RX[:, 8196:9220].bitcast(BF16), RX[:, 9220:10244].bitcast(BF16)]
            stg_t = [T(), T()]
            P.alias(stg_t, [W["HF"]])
            yfn = lambda k, tt: RYb[:, k, tt * 512:(tt + 1) * 512]
            for n in range(8):
                wv, wt = load_w(wrnn[l, :, :, n * 128:(n + 1) * 128], [128, 12, 128])
                pb, pt = grp()
                proj(pb, pt, wv, wt, yfn, RYt, 12)
                wv2, wt2 = load_w(win[l, :, :, C_GR + n * 128:C_GR + (n + 1) * 128], [128, 8, 128])
                pb2, pt2 = grp()
                proj(pb2, pt2, wv2, wt2, hfn, Ht, 8)
                P.op("act", lambda e, pb2=pb2: e.activation(out=F0, in_=pb2, func=AF.Tanh, scale=0.5), reads=pt2, writes=[W["F0"]])
                si = n % 2
                P.op("dve", lambda e, pb=pb, si=si: e.scalar_tensor_tensor(out=stg[si], in0=F0, scalar=1.0, in1=pb, op0=ALU.add, op1=ALU.mult),
                     reads=[W["F0"]] + pt, writes=[stg_t[si]])
                P.dma("sp", mixspill[n], stg[si], reads=[stg_t[si]], writes=[MSt[n]])

            DN = {nm: T() for nm in ["QK", "ktok", "vtok", "KDB", "S0", "S1", "Sb0", "Sb1", "r0", "r1", "x0", "x1",
                                     "Dgq0", "Dgq1"]}
            P.alias(list(DN.values()) + Dgt[0:2] + D2t[0:2], [W["HF"], W["HB"], W["ub"], W["thi"], W["sq"]] + stg_t)
            DR = {nm: T() for nm in ["qs", "ks", "vs", "sqb"]}
            P.alias(list(DR.values()), RYt[8:12])
            T2Tt = [T() for _ in range(16)]
            ATTt = [T() for _ in range(16)]
            OTt = [T() for _ in range(16)]
            wv, wt = load_w(win[l, :, :, C_BA:C_BA + 32], [128, 8, 32])
            pbt = bank(0).rearrange("p (c k) -> p c k", c=16)
            for c in range(16):
                for k in range(8):
                    P.op("pe", lambda e, c=c, k=k: e.matmul(pbt[:, c, :], lhsT=RH[:, k, c * 128:(c + 1) * 128], rhs=wv[:, k, :],
                                                              start=(k == 0), stop=(k == 7)),
                         reads=[Ht[k], wt], writes=[BK[0]])
            P.op("act", lambda e: e.copy(out=bt, in_=pbt), reads=[BK[0]], writes=[SM["bt"]])
            P.op("act", lambda e: e.activation(out=lnb, in_=bt[:, :, 0:16], func=AF.Exp, scale=-1.0), reads=[SM["bt"]], writes=[SM["lnb"]])
            P.op("act", lambda e: e.activation(out=lnb, in_=lnb, func=AF.Ln, bias=1.0), reads=[SM["lnb"]], writes=[SM["lnb"]])
            P.op("dve", lambda e: e.tensor_scalar(out=lnb, in0=lnb, scalar1=-1.0, scalar2=None, op0=ALU.mult), reads=[SM["lnb"]], writes=[SM["lnb"]])
            dtb = dnv[:, l, 1, :].unsqueeze(1).to_broadcast([128, 16, 16])
            ngb = nega[:, l, :].unsqueeze(1).to_broadcast([128, 16, 16])
            P.op("dve", lambda e: e.tensor_tensor(out=gg, in0=bt[:, :, 16:32], in1=dtb, op=ALU.add), reads=[SM["bt"], VEC], writes=[SM["g"]])
            P.op("act", lambda e: e.activation(out=gg, in_=gg, func=AF.Exp), reads=[SM["g"]], writes=[SM["g"]])
            P.op("act", lambda e: e.activation(out=gg, in_=gg, func=AF.Ln, bias=1.0), reads=[SM["g"]], writes=[SM["g"]])
            P.op("dve", lambda e: e.tensor_tensor(out=gg, in0=gg, in1=ngb, op=ALU.mult), reads=[SM["g"], VEC], writes=[SM["g"]])
            pgc = bank(1)[:, 0:256].rearrange("p (c k) -> p c k", c=16)
            pgl = bank(1)[:, 256:512].rearrange("p (c k) -> p c k", c=16)
            P.op("pe", lambda e: e.matmul(pgc[:, :, 0:8], lhsT=cm["Lf"], rhs=gg[:, :, 0:8], start=True, stop=True), reads=[CMt, SM["g"]], writes=[BK[1]])
            P.op("pe", lambda e: e.matmul(pgc[:, :, 8:16], lhsT=cm["Lb"], rhs=gg[:, :, 8:16], start=True, stop=True), reads=[CMt, SM["g"]], writes=[BK[1]])
            P.op("pe", lambda e: e.matmul(pgl, lhsT=cm["ones"], rhs=gg, start=True, stop=True), reads=[CMt, SM["g"]], writes=[BK[1]])
            P.op("act", lambda e: e.copy(out=gc, in_=pgc), reads=[BK[1]], writes=[SM["gc"]])
            P.op("act", lambda e: e.activation(out=colE, in_=pgc, func=AF.Exp), reads=[BK[1]], writes=[SM["colE"]])
            P.op("act", lambda e: e.activation(out=egl, in_=pgl, func=AF.Exp), reads=[BK[1]], writes=[SM["egl"]])
            P.op("dve", lambda e: e.tensor_scalar(out=colC, in0=colE, scalar1=-1.0, scalar2=None, op0=ALU.mult), reads=[SM["colE"]], writes=[SM["colC"]])
            P.op("dve", lambda e: e.tensor_tensor(out=stmp, in0=pgl, in1=gc, op=ALU.subtract), reads=[BK[1], SM["gc"]], writes=[SM["tmp"]])
            P.op("dve", lambda e: e.tensor_tensor(out=stmp, in0=stmp, in1=lnb, op=ALU.add), reads=[SM["tmp"], SM["lnb"]], writes=[SM["tmp"]])
            P.op("act", lambda e: e.activation(out=colD, in_=stmp, func=AF.Exp), reads=[SM["tmp"]], writes=[SM["colD"]])
            if s == 0 and l == 0:
                dbg_dump("sm", SMbuf, list(SM.values()), [128, 512 + 8 * 256], F32)
            if stop_after == "sm":
                raise _Stop()

            p1v = [bank(sl).rearrange("p (g n) -> p g n", g=2) for sl in range(NS)]
            p2f = [bank(4 + sl)[:, 0:256].rearrange("p (g n) -> p g n", g=2) for sl in range(NS)]
            bA = bank(0, 4)
            bAt = BK[0:4]

            for hd in range(heads):
                dh = [hd, 8 + hd]
                for blk, dst, dkey in ((hd, qsb, "qs"), (8 + hd, ksb, "ks"), (16 + hd, vsb, "vs")):
                    wv, wt = load_w(win[l, :, :, C_QKV + blk * 128:C_QKV + (blk + 1) * 128], [128, 8, 128])
                    proj(bA, bAt, wv, wt, hfn, Ht, 8)
                    P.op("act", lambda e: e.copy(out=rxp[:, 2:2050], in_=bA), reads=bAt, writes=[W["rxp"]] + OTt)
                    cw = lambda j, r_=l * 24 + blk: dncw[:, r_, j:j + 1]
                    P.op("dve", lambda e, cw=cw: e.tensor_scalar(out=F0, in0=rxp[:, 0:2048], scalar1=cw(0), scalar2=None, op0=ALU.mult),
                         reads=[W["rxp"], VEC], writes=[W["F0"]])
                    P.op("dve", lambda e, cw=cw: e.scalar_tensor_tensor(out=F1, in0=rxp[:, 1:2049], scalar=cw(1), in1=F0, op0=ALU.mult, op1=ALU.add),
                         reads=[W["rxp"], VEC, W["F0"]], writes=[W["F1"]])
                    P.op("dve", lambda e, cw=cw: e.scalar_tensor_tensor(out=F0, in0=rxp[:, 2:2050], scalar=cw(2), in1=F1, op0=ALU.mult, op1=ALU.add),
                         reads=[W["rxp"], VEC, W["F1"]], writes=[W["F0"]])
                    P.op("dve", lambda e, cw=cw: e.scalar_tensor_tensor(out=F1, in0=rxp[:, 3:2051], scalar=cw(3), in1=F0, op0=ALU.mult, op1=ALU.add),
                         reads=[W["rxp"], VEC, W["F0"]], writes=[W["F1"]])
                    P.op("act", lambda e: e.activation(out=F2, in_=F1, func=AF.Tanh, scale=0.5), reads=[W["F1"]], writes=[W["F2"]])
                    P.op("dve", lambda e, dst=dst: e.scalar_tensor_tensor(out=dst, in0=F2, scalar=1.0, in1=F1, op0=ALU.add, op1=ALU.mult),
                         reads=[W["F2"], W["F1"]], writes=[DR[dkey]] + T2Tt + ATTt)
                for src, skey, slot, scl in ((ksb, "ks", 0, 1.0), (qsb, "qs", 1, 128.0 ** -0.5)):
                    P.op("act", lambda e, src=src: e.activation(out=sqb, in_=src, func=AF.Square), reads=[DR[skey]], writes=[DR["sqb"]])
                    for tt in range(4):
                        P.op("pe", lambda e, tt=tt: e.matmul(bank(tt), lhsT=ones_bf, rhs=sqb[:, tt * 512:(tt + 1) * 512], start=True, stop=True),
                             reads=[CBt, DR["sqb"]], writes=[BK[tt]])
                    P.op("act", lambda e: e.activation(out=F0, in_=bA, func=AF.Ln, bias=4.0 * EPS), reads=bAt, writes=[W["F0"]])
                    P.op("act", lambda e: e.activation(out=F0, in_=F0, func=AF.Exp, scale=-0.5), reads=[W["F0"]], writes=[W["F0"]])
                    P.op("dve", lambda e, src=src, slot=slot, scl=scl: e.scalar_tensor_tensor(
                        out=QK[:, :, slot, :], in0=src.rearrange("p (c d) -> p c d", c=16), scalar=scl,
                        in1=F0.rearrange("p (c d) -> p c d", c=16), op0=ALU.mult, op1=ALU.mult),
                         reads=[DR[skey], W["F0"]], writes=[DN["QK"]])
                ptr = bank(0, 2).bitcast(BF16).rearrange("p (c d) -> p c d", c=16)
                ptr2 = bank(2, 2).bitcast(BF16).rearrange("p (c d) -> p c d", c=16)
                for c in range(16):
                    P.op("pe", lambda e, c=c: e.transpose(ptr[:, c, :], QK[:, c, 0, :], ident_bf), reads=[DN["QK"], CBt], writes=[BK[0], BK[1]])
                P.op("act", lambda e: e.copy(out=ktok, in_=ptr), reads=[BK[0], BK[1]], writes=[DN["ktok"]])
                for c in range(16):
                    P.op("pe", lambda e, c=c: e.transpose(ptr2[:, c, :], vsb[:, c * 128:(c + 1) * 128], ident_bf), reads=[DR["vs"], CBt], writes=[BK[2], BK[3]])
                P.op("act", lambda e: e.mul(out=vtok, in_=ptr2, mul=0.5), reads=[BK[2], BK[3]], writes=[DN["vtok"]])
                for d in range(2):
                    cdb = colD[:, :, dh[d]:dh[d] + 1].to_broadcast([128, 16, 128])
                    P.op("dve", lambda e, d=d, cdb=cdb: e.tensor_tensor(out=KDB[:, d], in0=ktok, in1=cdb, op=ALU.mult),
                         reads=[DN["ktok"], SM["colD"]], writes=[DN["KDB"]])
                    for c4 in range(4):
                        prb = bank(3).rearrange("p (c d) -> p c d", c=4)
                        for cc in range(4):
                            c = c4 * 4 + cc
                            j = c % 2
                            P.op("act", lambda e, c=c, d=d, j=j: e.mul(out=Dgq[j], in_=cm["ident"], mul=colE[:, c, dh[d]:dh[d] + 1]),
                                 reads=[CMt, SM["colE"]], writes=[DN["Dgq%d" % j]])
                            P.op("pe", lambda e, cc=cc, j=j: e.matmul(prb[:, cc, :], lhsT=cm["ones"], rhs=Dgq[j], start=True, stop=True),
                                 reads=[CMt, DN["Dgq%d" % j]], writes=[BK[3]])
                        P.op("dve", lambda e, d=d, c4=c4: e.tensor_tensor(out=QG[:, d, c4 * 4:(c4 + 1) * 4, :], in0=QK[:, c4 * 4:(c4 + 1) * 4, 1, :],
                                                                            in1=prb, op=ALU.mult),
                             reads=[DN["QK"], BK[3]], writes=[QGt[d]])

                if stop_after == "prep":
                    dbg_dump("QK", RX[:, 8196:10244].bitcast(BF16), [DN["QK"]], [128, 4096], BF16)
                    dbg_dump("ktok", RX[:, 10244:11268].bitcast(BF16), [DN["ktok"]], [128, 2048], BF16)
                    dbg_dump("vtok", RX[:, 11268:12292].bitcast(BF16), [DN["vtok"]], [128, 2048], BF16)
                    dbg_dump("KDB", RX[:, 12292:14340].bitcast(BF16), [DN["KDB"]], [128, 4096], BF16)
                    dbg_dump("QG", QGbuf, QGt, [128, 4096], BF16)
                    raise _Stop()
                def units_setup(pairs):
                    for sl, c in pairs:
                        P.op("pe", lambda e, sl=sl, c=c: e.matmul(p1v[sl][:, 0, :], lhsT=QK[:, c, 0, :], rhs=QKf[:, c, :], start=True, stop=True),
                             reads=[DN["QK"]], writes=[BK[sl]])
                    for sl, c in pairs:
                        kq = p1v[sl][:, 0, :].rearrange("p (a b) -> p a b", a=2)
                        P.op("dve", lambda e, sl=sl, kq=kq: e.tensor_tensor(out=KQA[sl], in0=kq, in1=mskA, op=ALU.mult), reads=[BK[sl], CBt], writes=[KQAt[sl]])
                        P.op("dve", lambda e, sl=sl, kq=kq: e.tensor_tensor(out=KQB[sl], in0=kq, in1=mskB, op=ALU.mult), reads=[BK[sl], CBt], writes=[KQBt[sl]])
                        for d in range(2):
                            Lm = cm["Lf"] if d == 0 else cm["Lb"]
                            P.op("dve", lambda e, sl=sl, c=c, d=d, Lm=Lm: e.tensor_scalar(out=Dg[sl][:, d, :], in0=Lm, scalar1=gg[:, c, dh[d]:dh[d] + 1], scalar2=None, op0=ALU.mult),
                                 reads=[CMt, SM["g"]], writes=[Dgt[sl]])
                    for sl, c in pairs:
                        for d in range(2):
                            Sm = cm["SUf"] if d == 0 else cm["SUb"]
                            pe_ = p1v[sl][:, 1, d * 128:(d + 1) * 128]
                            P.op("pe", lambda e, sl=sl, d=d, Sm=Sm, pe_=pe_: e.matmul(pe_, lhsT=Sm, rhs=Dg[sl][:, d, :], start=True, stop=True),
                                 reads=[CMt, Dgt[sl]], writes=[BK[sl]])
                    for sl, c in pairs:
                        for d in range(2):
                            pe_ = p1v[sl][:, 1, d * 128:(d + 1) * 128]
                            P.op("act", lambda e, sl=sl, c=c, d=d, pe_=pe_: e.activation(out=D2[sl][:, d, :], in_=pe_, func=AF.Exp, bias=lnb[:, c, dh[d]:dh[d] + 1]),
                                 reads=[BK[sl], SM["lnb"]], writes=[D2t[sl]])
                    for sl, c in pairs:
                        for d in range(2):
                            kq_ = KQA[sl] if d == 0 else KQB[sl]
                            kqt = KQAt[sl] if d == 0 else KQBt[sl]
                            P.op("dve", lambda e, sl=sl, d=d, kq_=kq_: e.tensor_tensor(out=YQr[sl][0][:, d, 0:128], in0=kq_[:, 0, :], in1=D2[sl][:, d, :], op=ALU.mult),
                                 reads=[kqt, D2t[sl]], writes=[YQt[sl][0]])
                    for sl, c in pairs:
                        for d in range(2):
                            P.op("pe", lambda e, sl=sl, d=d: e.transpose(p2f[sl][:, d, :], YQ[sl][0][:, d, 0:128], cm["ident"]),
                                 reads=[YQt[sl][0], CMt], writes=[BK[4 + sl]])
                    for sl, c in pairs:
                        P.op("act", lambda e, sl=sl: e.copy(out=YTr[sl][0], in_=p2f[sl]), reads=[BK[4 + sl]], writes=[YTt[sl][0]])
                    for sl, c in pairs:
                        for d in range(2):
                            kq_ = KQA[sl] if d == 0 else KQB[sl]
                            kqt = KQAt[sl] if d == 0 else KQBt[sl]
                            P.op("dve", lambda e, sl=sl, c=c, d=d, kq_=kq_: e.tensor_tensor(out=ATT[:, c, d, :], in0=kq_[:, 1, :], in1=D2[sl][:, d, :], op=ALU.mult),
                                 reads=[kqt, D2t[sl]], writes=[ATTt[c], DR["vs"], DR["sqb"]])

                def unit_level_mm(sl, c, k):
                    ci = k % 2
                    yq, yt = YQr[sl][ci], YTr[sl][ci]
                    rd = [YQt[sl][ci], YTt[sl][ci]]
                    for g_ in range(2):
                        if k == 0:
                            P.op("pe", lambda e, g_=g_: e.matmul(p1v[sl][:, g_, 0:128], lhsT=yt[:, g_, :], rhs=yq[:, g_, 0:128], start=True, stop=True),
                                 reads=rd, writes=[BK[sl]])
                        elif k < 6:
                            P.op("pe", lambda e, g_=g_: e.matmul(p1v[sl][:, g_, :], lhsT=yt[:, g_, :], rhs=yq[:, g_, :], start=True, stop=True),
                                 reads=rd, writes=[BK[sl]])
                        else:
                            P.op("pe", lambda e, g_=g_: e.matmul(p1v[sl][:, g_, 128:256], lhsT=yt[:, g_, :], rhs=yq[:, g_, 128:256], start=True, stop=True),
                                 reads=rd, writes=[BK[sl]])
                    if k < 6:
                        for g_ in range(2):
                            P.op("pe", lambda e, g_=g_: e.matmul(p2f[sl][:, g_, :], lhsT=yq[:, g_, 0:128], rhs=yt[:, g_, :], start=True, stop=True),
                                 reads=rd, writes=[BK[4 + sl]])

                def unit_level_ev(sl, c, k):
                    ci = k % 2
                    ni = 1 - ci
                    yq = YQ[sl][ci]
                    if k < 6:
                        P.op("act", lambda e: e.copy(out=YQr[sl][ni][:, :, 0:128], in_=p1v[sl][:, :, 0:128]), reads=[BK[sl]], writes=[YQt[sl][ni]])
                        P.op("act", lambda e: e.copy(out=YTr[sl][ni], in_=p2f[sl]), reads=[BK[4 + sl]], writes=[YTt[sl][ni]])
                        if k == 0:
                            idb = cm["ident"].unsqueeze(1).to_broadcast([128, 2, 128])
                            P.op("dve", lambda e: e.tensor_tensor(out=YQr[sl][ni][:, :, 128:256], in0=yq[:, :, 0:128], in1=idb, op=ALU.add),
                                 reads=[YQt[sl][ci], CMt], writes=[YQt[sl][ni]])
                        else:
                            P.op("dve", lambda e: e.tensor_tensor(out=YQr[sl][ni][:, :, 128:256], in0=yq[:, :, 128:256], in1=p1v[sl][:, :, 128:256], op=ALU.add),
                                 reads=[YQt[sl][ci], BK[sl]], writes=[YQt[sl][ni]])
                    else:
                        P.op("dve", lambda e: e.tensor_tensor(out=T2T[:, c, :, :], in0=yq[:, :, 128:256], in1=p1v[sl][:, :, 128:256], op=ALU.add),
                             reads=[YQt[sl][ci], BK[sl]], writes=[T2Tt[c], DR["qs"], DR["ks"]])

                for d in range(2):
                    P.op("pool", lambda e, d=d: e.memset(S32[d], 0.0), writes=[DN["S%d" % d]])
                    P.op("pool", lambda e, d=d: e.memset(Sbf[d], 0.0), writes=[DN["Sb%d" % d]])

                def chain_ops(step, d):
                    c = step if d == 0 else 15 - step
                    bA_ = 2 + 4 * d
                    bB_ = 3 + 4 * d
                    pk = bank(bA_)[:, 0:128]
                    px = bank(bA_)[:, 128:256]
                    po = bank(bB_)[:, 0:128]
                    pd = bank(bB_)[:, 128:256]
                    tA, tB = BK[bA_], BK[bB_]
                    St, Sbt, rt, xt_ = DN["S%d" % d], DN["Sb%d" % d], DN["r%d" % d], DN["x%d" % d]
                    eg = egl[:, c, dh[d]:dh[d] + 1]
                    oc = OT[:, c * 128:(c + 1) * 128]
                    ops = [
                        lambda: P.op("pe", lambda e: e.matmul(pk, lhsT=QK[:, c, 0, :], rhs=Sbf[d], start=True, stop=True), reads=[DN["QK"], Sbt], writes=[tA]),
                        lambda: P.op("dve", lambda e: e.scalar_tensor_tensor(out=rbf[d], in0=pk, scalar=colC[:, c, dh[d]:dh[d] + 1], in1=vtok[:, c, :], op0=ALU.mult, op1=ALU.add),
                                     reads=[tA, SM["colC"], DN["vtok"]], writes=[rt]),
                        lambda: P.op("pe", lambda e: e.matmul(px, lhsT=T2T[:, c, d, :], rhs=rbf[d], start=True, stop=True), reads=[T2Tt[c], rt], writes=[tA]),
                        lambda: P.op("act", lambda e: e.copy(out=xbf[d], in_=px), reads=[tA], writes=[xt_]),
                        lambda: P.op("pe", lambda e: e.matmul(pd, lhsT=KDB[:, d, c, :], rhs=xbf[d], start=True, stop=True), reads=[DN["KDB"], xt_], writes=[tB]),
                        lambda: P.op("pe", lambda e: e.matmul(po, lhsT=Sbf[d], rhs=QG[:, d, c, :], start=True, stop=False), reads=[Sbt, QGt[d]], writes=[tB]),
                        lambda: P.op("pe", lambda e: e.matmul(po, lhsT=xbf[d], rhs=ATT[:, c, d, :], start=False, stop=True), reads=[xt_, ATTt[c]], writes=[tB]),
                        lambda: P.op("dve", lambda e: e.scalar_tensor_tensor(out=Sbf[d], in0=S32[d], scalar=eg, in1=pd, op0=ALU.mult, op1=ALU.add),
                                     reads=[St, SM["egl"], tB], writes=[Sbt]),
                        lambda: P.op("dve", lambda e: e.scalar_tensor_tensor(out=S32[d], in0=S32[d], scalar=eg, in1=pd, op0=ALU.mult, op1=ALU.add),
                                     reads=[St, SM["egl"], tB], writes=[St]),
                    ]
                    if step < 8:
                        ops.append(lambda: P.op("act", lambda e: e.copy(out=oc, in_=po), reads=[tB], writes=[OTt[c], W["rxp"]]))
                    else:
                        ops.append(lambda: P.op("dve", lambda e: e.tensor_tensor(out=oc, in0=oc, in1=po, op=ALU.add), reads=[tB, OTt[c]], writes=[OTt[c]]))
                    return ops

                pending = []

                def emit_some(n):
                    for _ in range(n):
                        if pending:
                            pending.pop(0)()

                def sched(step):
                    a_, b_ = chain_ops(step, 0), chain_ops(step, 1)
                    for i_ in range(len(a_)):
                        pending.append(a_[i_])
                        pending.append(b_[i_])

                for p in range(8):
                    cs = [p, 15 - p]
                    units_setup([(0, cs[0]), (1, cs[1])])
                    for k in range(7):
                        for sl in range(2):
                            unit_level_mm(sl, cs[sl], k)
                        emit_some(2)
                        for sl in range(2):
                            unit_level_ev(sl, cs[sl], k)
                        emit_some(2)
                    emit_some(len(pending))
                    sched(p)
                emit_some(len(pending))
                for step in range(8, 16):
                    sched(step)
                    emit_some(len(pending))

                if stop_after == "chain":
                    dbg_dump("OT", OT, OTt, [128, 2048], F32)
                    raise _Stop()
                P.op("act", lambda e: e.activation(out=sqb, in_=OT, func=AF.Square), reads=OTt, writes=[DR["sqb"]] + ATTt)
                for tt in range(4):
                    P.op("pe", lambda e, tt=tt: e.matmul(bank(tt), lhsT=ones_bf, rhs=sqb[:, tt * 512:(tt + 1) * 512], start=True, stop=True),
                         reads=[CBt, DR["sqb"]], writes=[BK[tt]])
                P.op("act", lambda e: e.activation(out=F0, in_=bA, func=AF.Ln, bias=EPS, scale=1.0 / 128.0), reads=bAt, writes=[W["F0"]])
                P.op("act", lambda e: e.activation(out=F0, in_=F0, func=AF.Exp, scale=-0.5), reads=[W["F0"]], writes=[W["F0"]])
                wv, wt = load_w(win[l, :, :, C_Z + hd * 128:C_Z + (hd + 1) * 128], [128, 8, 128])
                bB = bank(4, 4)
                proj(bB, BK[4:8], wv, wt, hfn, Ht, 8)
                P.op("act", lambda e: e.activation(out=F1, in_=bB, func=AF.Tanh, scale=0.5), reads=BK[4:8], writes=[W["F1"]])
                P.op("dve", lambda e: e.scalar_tensor_tensor(out=F2, in0=F1, scalar=1.0, in1=bB, op0=ALU.add, op1=ALU.mult),
                     reads=[W["F1"]] + BK[4:8], writes=[W["F2"]])
                P.op("dve", lambda e: e.scalar_tensor_tensor(out=F1, in0=OT, scalar=dnnorm[:, l:l + 1], in1=F0, op0=ALU.mult, op1=ALU.mult),
                     reads=OTt + [VEC, W["F0"]], writes=[W["F1"]])
                P.op("dve", lambda e, hd=hd: e.scalar_tensor_tensor(out=RYb[:, hd, :], in0=F1, scalar=0.5, in1=F2, op0=ALU.mult, op1=ALU.mult),
                     reads=[W["F1"], W["F2"]], writes=[RYt[hd]])
            if s == 0 and l == 0:
                dbg_dump("yd", RYb[:, 0:8, :], RYt[0:8], [128, 8, SEQ], BF16)
            if stop_after == "dn":
                raise _Stop()

            MIXB = [RYtail[:, 0:4096].rearrange("p (c t) -> p c t", c=8), RYtail[:, 4096:8192].rearrange("p (c t) -> p c t", c=8)]
            MIXt = [T(), T()]
            P.alias(MIXt, list(DR.values()) + T2Tt + ATTt)
            thg = [QGbuf[:, 0:1024].bitcast(F32), QGbuf[:, 1024:2048].bitcast(F32)]
            mst = [QGbuf[:, 2048:2560], QGbuf[:, 2560:3072]]
            thg_t = [T(), T()]
            mst_t = [T(), T()]
            P.alias(thg_t + mst_t, QGt)
            allwork = list(W.values()) + list(DN.values()) + OTt
            P.alias(Xt, allwork)
            bi = [0]

            def nb():
                b = bi[0]
                bi[0] = (b + 1) % 8
                return bank(b), BK[b]
            for tt in range(4):
                mb, mbt = MIXB[tt % 2], MIXt[tt % 2]
                for n in range(8):
                    wv, wt = load_w(wdn[l, :, :, n * 128:(n + 1) * 128], [128, 8, 128])
                    wv2, wt2 = load_w(win[l, :, :, C_GD + n * 128:C_GD + (n + 1) * 128], [128, 8, 128])
                    py, pyt = nb()
                    pg, pgt = nb()
                    for k in range(8):
                        P.op("pe", lambda e, k=k, py=py, wv=wv: e.matmul(py, lhsT=wv[:, k, :], rhs=RYb[:, k, tt * 512:(tt + 1) * 512], start=(k == 0), stop=(k == 7)),
                             reads=[wt, RYt[k]], writes=[pyt])
                    for k in range(8):
                        P.op("pe", lambda e, k=k, pg=pg, wv2=wv2: e.matmul(pg, lhsT=wv2[:, k, :], rhs=RH[:, k, tt * 512:(tt + 1) * 512], start=(k == 0), stop=(k == 7)),
                             reads=[wt2, Ht[k]], writes=[pgt])
                    j = n % 2
                    P.op("act", lambda e, j=j, pg=pg: e.activation(out=thg[j], in_=pg, func=AF.Tanh, scale=0.5), reads=[pgt], writes=[thg_t[j]])
                    P.dma("sp", mst[j], mixspill[n, :, tt * 512:(tt + 1) * 512], reads=[MSt[n]], writes=[mst_t[j]])
                    P.op("dve", lambda e, j=j, py=py: e.scalar_tensor_tensor(out=thg[j], in0=thg[j], scalar=1.0, in1=py, op0=ALU.add, op1=ALU.mult),
                         reads=[thg_t[j], pyt], writes=[thg_t[j]])
                    P.op("dve", lambda e, j=j, n=n, mb=mb: e.tensor_tensor(out=mb[:, n, :], in0=thg[j], in1=mst[j], op=ALU.add),
                         reads=[thg_t[j], mst_t[j]], writes=[mbt])
                for n in range(8):
                    wv, wt = load_w(wout[l, :, :, n * 128:(n + 1) * 128], [128, 8, 128])
                    po_, pot = nb()
                    for k in range(8):
                        P.op("pe", lambda e, k=k, po_=po_, wv=wv, mb=mb: e.matmul(po_, lhsT=wv[:, k, :], rhs=mb[:, k, :], start=(k == 0), stop=(k == 7)),
                             reads=[wt, mbt], writes=[pot])
                    xs = Xv[:, n, tt * 512:(tt + 1) * 512]
                    P.dma("sp", xs, xspill[:, n, tt * 512:(tt + 1) * 512], reads=[XSt[n]], writes=[Xt[n]])
                    P.op("dve", lambda e, xs=xs, po_=po_: e.scalar_tensor_tensor(out=xs, in0=po_, scalar=0.5, in1=xs, op0=ALU.mult, op1=ALU.add),
                         reads=[pot, Xt[n]], writes=[Xt[n]])
            P.alias(RYt, MIXt + RYt)
            P.alias(QGt, thg_t + mst_t)
            if s == 0 and l == 0:
                dbg_dump("xmix", Xv, Xt, [128, 8, SEQ], F32)
            if stop_after == "mix":
                raise _Stop()

            rstd, tmp_t = rms_stats(1.0 / D)
            for c in range(8):
                g = norms[:, (l * 2 + 1) * 8 + c:(l * 2 + 1) * 8 + c + 1]
                P.op("dve", lambda e, c=c, g=g: e.scalar_tensor_tensor(out=RH[:, c, :], in0=Xv[:, c, :], scalar=g, in1=rstd, op0=ALU.mult, op1=ALU.mult),
                     reads=[Xt[c], tmp_t[9], VEC], writes=[Ht[c]])
            ACTb = RY[:, 0:22528].rearrange("p (f t) -> p f t", f=22)
            ACt = [T() for _ in range(22)]
            P.alias(ACt, tmp_t)
            thf = [QGbuf[:, 0:2048].bitcast(F32), QGbuf[:, 2048:4096].bitcast(F32)]
            thf_t = [T(), T()]
            P.alias(thf_t, QGt)
            for half in range(2):
                t0 = half * 1024
                for f in range(22):
                    wv, wt = load_w(wgu[l, :, :, f * 128:(f + 1) * 128], [128, 8, 128])
                    wv2, wt2 = load_w(wgu[l, :, :, DFF + f * 128:DFF + (f + 1) * 128], [128, 8, 128])
                    pb, pt = grp()
                    for k in range(8):
                        for q_ in range(2):
                            P.op("pe", lambda e, k=k, q_=q_, pb=pb, wv=wv: e.matmul(pb[:, q_ * 512:(q_ + 1) * 512], lhsT=wv[:, k, :],
                                                                                     rhs=RH[:, k, t0 + q_ * 512:t0 + (q_ + 1) * 512], start=(k == 0), stop=(k == 7)),
                                 reads=[wt, Ht[k]], writes=[pt[q_]])
                    for k in range(8):
                        for q_ in range(2):
                            P.op("pe", lambda e, k=k, q_=q_, pb=pb, wv2=wv2: e.matmul(pb[:, 1024 + q_ * 512:1024 + (q_ + 1) * 512], lhsT=wv2[:, k, :],
                                                                                       rhs=RH[:, k, t0 + q_ * 512:t0 + (q_ + 1) * 512], start=(k == 0), stop=(k == 7)),
                                 reads=[wt2, Ht[k]], writes=[pt[2 + q_]])
                    j = f % 2
                    P.op("act", lambda e, j=j, pb=pb: e.activation(out=thf[j], in_=pb[:, 0:1024], func=AF.Tanh, scale=0.5), reads=pt[0:2], writes=[thf_t[j]])
                    P.op("dve", lambda e, j=j, pb=pb: e.scalar_tensor_tensor(out=thf[j], in0=thf[j], scalar=1.0, in1=pb[:, 0:1024], op0=ALU.add, op1=ALU.mult),
                         reads=[thf_t[j]] + pt[0:2], writes=[thf_t[j]])
                    P.op("dve", lambda e, j=j, pb=pb, f=f: e.scalar_tensor_tensor(out=ACTb[:, f, :], in0=thf[j], scalar=0.5, in1=pb[:, 1024:2048], op0=ALU.mult, op1=ALU.mult),
                         reads=[thf_t[j]] + pt[2:4], writes=[ACt[f]])
                for n in range(8):
                    wv, wt = load_w(wdown[l, :, 0:11, n * 128:(n + 1) * 128], [128, 11, 128])
                    wv2, wt2 = load_w(wdown[l, :, 11:22, n * 128:(n + 1) * 128], [128, 11, 128])
                    pb, pt = grp()
                    for f in range(22):
                        for q_ in range(2):
                            w_, wt_ = (wv, wt) if f < 11 else (wv2, wt2)
                            P.op("pe", lambda e, f=f, q_=q_, pb=pb, w_=w_: e.matmul(pb[:, q_ * 512:(q_ + 1) * 512], lhsT=w_[:, f % 11, :],
                                                                                     rhs=ACTb[:, f, q_ * 512:(q_ + 1) * 512], start=(f == 0), stop=(f == 21)),
                                 reads=[wt_, ACt[f]], writes=[pt[q_]])
                    xs = Xv[:, n, t0:t0 + 1024]
                    P.op("dve", lambda e, xs=xs, pb=pb: e.tensor_tensor(out=xs, in0=xs, in1=pb[:, 0:1024], op=ALU.add), reads=[Xt[n]] + pt[0:2], writes=[Xt[n]])
            P.alias(RYt, ACt)
            P.alias(QGt, thf_t)
            if s == 0:
                dbg_dump("xl%d" % l, Xv, Xt, [128, 8, SEQ], F32)
        if stop_after:
            raise _Stop()

        rstd, tmp_t = rms_stats(1.0 / D)
        ost = [RYf[:, 0:2048], RYf[:, 2048:4096]]
        ost_t = [T(), T()]
        P.alias(ost_t, tmp_t[0:8])
        for c in range(8):
            j = c % 2
            P.op("dve", lambda e, c=c, j=j: e.scalar_tensor_tensor(out=ost[j], in0=Xv[:, c, :], scalar=fnorm[:, c:c + 1], in1=rstd, op0=ALU.mult, op1=ALU.mult),
                 reads=[Xt[c], tmp_t[9], VEC], writes=[ost_t[j]])
            P.dma("sp", outT[s, :, c, :], ost[j], reads=[ost_t[j]], is_output=True)
        P.alias(RYt, tmp_t + ost_t)

    return


def prep_weights(w_in, rg_conv_w, rg_conv_b, rg_wa, rg_ba, rg_wx, rg_bx, rg_lambda, w_rnn_proj, dn_conv_w,
                 dn_a_log, dn_dt_bias, dn_norm, w_dn_proj, w_out, mix_norm, ffn_norm, w_gate_up, w_down, final_norm):
    L = NLAYER
    f = lambda a: np.ascontiguousarray(np.asarray(a, dtype=np.float32))
    m = {}
    m["win"] = f(np.asarray(w_in).reshape(L, 8, 128, NIN).transpose(0, 2, 1, 3))
    wa = np.asarray(rg_wa)
    wx = np.asarray(rg_wx)
    g = np.stack([wa[:, 0], wa[:, 1], wx[:, 0], wx[:, 1]], axis=3)
    m["gatew"] = f(g)
    cwv = np.asarray(rg_conv_w).reshape(L, 4, 12, 128).transpose(3, 0, 2, 1)
    pv = lambda a: np.asarray(a).reshape(L, 12, 128).transpose(2, 0, 1)[..., None]
    pv2 = lambda a, d: np.asarray(a)[:, d].reshape(L, 12, 128).transpose(2, 0, 1)[..., None]
    rgv = np.concatenate([cwv, pv(rg_conv_b), pv2(rg_ba, 0), pv2(rg_ba, 1), pv2(rg_bx, 0), pv2(rg_bx, 1),
                          pv2(rg_lambda, 0), pv2(rg_lambda, 1)], axis=3)
    m["rgv"] = f(rgv.reshape(128, L * 12, 11))
    m["wrnn"] = f(np.asarray(w_rnn_proj).reshape(L, 12, 128, D).transpose(0, 2, 1, 3))
    m["dncw"] = f(np.asarray(dn_conv_w).reshape(L, 4, 24, 128).transpose(3, 0, 2, 1).reshape(128, L * 24, 4))
    dv = np.stack([np.asarray(dn_a_log).reshape(L, 16), np.asarray(dn_dt_bias).reshape(L, 16)], axis=1)
    m["dnv"] = f(np.broadcast_to(dv[None], (128, L, 2, 16)))
    m["dnnorm"] = f(np.asarray(dn_norm).T)
    m["wdn"] = f(np.asarray(w_dn_proj).reshape(L, 8, 128, D).transpose(0, 2, 1, 3))
    m["wout"] = f(np.asarray(w_out).reshape(L, 8, 128, D).transpose(0, 2, 1, 3))
    nm = np.stack([np.asarray(mix_norm).reshape(L, 8, 128), np.asarray(ffn_norm).reshape(L, 8, 128)], axis=1)
    m["norms"] = f(nm.transpose(3, 0, 1, 2).reshape(128, L * 2 * 8))
    m["fnorm"] = f(np.asarray(final_norm).reshape(8, 128).T)
    m["wgu"] = f(np.asarray(w_gate_up).reshape(L, 8, 128, 2 * DFF).transpose(0, 2, 1, 3))
    m["wdown"] = f(np.asarray(w_down).reshape(L, 22, 128, D).transpose(0, 2, 1, 3))
    m["consts"] = make_consts()
    return m


def prep_x(xs):
    n = xs.shape[0]
    return np.ascontiguousarray(np.asarray(xs, dtype=np.float32).transpose(0, 2, 1).reshape(n, 8, 128, SEQ).transpose(0, 2, 1, 3))


def unprep_x(o):
    n = o.shape[0]
    return np.ascontiguousarray(o.transpose(0, 2, 1, 3).reshape(n, D, SEQ).transpose(0, 2, 1))


def kernel(x, mix_norm, w_in, rg_conv_w, rg_conv_b, rg_wa, rg_ba, rg_wx, rg_bx, rg_lambda,
           w_rnn_proj, dn_conv_w, dn_a_log, dn_dt_bias, dn_norm, w_dn_proj, w_out,
           ffn_norm, w_gate_up, w_down, final_norm):
    x = np.asarray(x)
    B = x.shape[0]
    nseq = B // NCORES
    wm = prep_weights(w_in, rg_conv_w, rg_conv_b, rg_wa, rg_ba, rg_wx, rg_bx, rg_lambda, w_rnn_proj, dn_conv_w,
                      dn_a_log, dn_dt_bias, dn_norm, w_dn_proj, w_out, mix_norm, ffn_norm, w_gate_up, w_down, final_norm)
    nc = bass.Bass("TRN2", target_bir_lowering=False)
    build(nc, NSEQ=nseq)
    in_maps = []
    for cidx in range(NCORES):
        mm = dict(wm)
        mm["xT"] = prep_x(x[cidx * nseq:(cidx + 1) * nseq])
        in_maps.append(mm)
    res = run_bass_kernel_spmd(nc, in_maps, core_ids=list(range(NCORES)))
    outs = [unprep_x(np.asarray(r["outT"])) for r in res.results]
    return np.concatenate(outs, axis=0).astype(np.float32)
```

```python
import numpy as np
import concourse.bass as bass
import concourse.mybir as mybir
from concourse.bass_utils import run_bass_kernel_spmd

F32 = mybir.dt.float32
F32R = mybir.dt.float32
BF16 = mybir.dt.bfloat16
ALU = mybir.AluOpType
AF = mybir.ActivationFunctionType

NCORES = 8
D = 1024
SEQ = 2048
NLAYER = 4
DRNN = 1536
DFF = 2816
NIN = 9248
C_RY = 1536
C_QKV = 3072
C_Z = 6144
C_BA = 7168
C_GR = 7200
C_GD = 8224
EPS = 1e-6
NEG = -30000.0


class T:
    __slots__ = ("ap", "w", "r", "ps")

    def __init__(self, ap=None, ps=False):
        self.ap = ap
        self.w = None
        self.r = {}
        self.ps = ps


class _Rec:
    def __init__(self):
        self.call = None

    def __getattr__(self, name):
        def f(*a, **k):
            self.call = (name, a, k)
            return self
        return f


class Eng:
    def __init__(self, name, sem):
        self.name = name
        self.sem = sem
        self.key = ("e", name)
        self.count = 0
        self.waited = {}
        self.ops = []


class Prog:
    NDMASEM = 24

    def __init__(self, nc):
        self.nc = nc
        self.sems = {}
        self.E = {}
        for n in ("pe", "dve", "act", "pool", "sp"):
            s = nc.alloc_semaphore(name="s_" + n)
            e = Eng(n, s)
            self.E[n] = e
            self.sems[e.key] = s
        self.dsem = []
        for i in range(self.NDMASEM):
            s = nc.alloc_semaphore(name="d%d" % i)
            self.dsem.append([s, 0])
            self.sems[("d", i)] = s
        self.dnext = 0
        self.out_tokens = []
        self.ninst = 0

    def _deps(self, reads, writes):
        need = {}
        for t in reads:
            if t.w is not None:
                k, v = t.w
                if need.get(k, 0) < v:
                    need[k] = v
        for t in writes:
            if t.w is not None:
                k, v = t.w
                if need.get(k, 0) < v:
                    need[k] = v
            for k, v in t.r.items():
                if need.get(k, 0) < v:
                    need[k] = v
        return need

    def _emit_waits(self, e, need, skip_self=False):
        for k, v in need.items():
            if skip_self and k == e.key:
                continue
            if e.waited.get(k, 0) < v:
                e.waited[k] = v
                e.ops.append(("w", k, v))

    def _mark(self, tok, reads, writes):
        k, v = tok
        for t in reads:
            if t.r.get(k, 0) < v:
                t.r[k] = v
        for t in writes:
            t.w = tok
            t.r = {}

    def op(self, eng, fn, reads=(), writes=()):
        e = self.E[eng]
        psr = [t for t in reads if t.ps]
        if psr:
            reads = [t for t in reads if not t.ps]
            writes = list(writes) + psr
        need = self._deps(reads, writes)
        self._emit_waits(e, need, skip_self=(eng == "pe"))
        e.count += 1
        tok = (e.key, e.count)
        rec = _Rec()
        fn(rec)
        e.ops.append(("i", rec.call))
        self._mark(tok, reads, writes)
        self.ninst += 1
        return tok

    def dma(self, q, out_ap, in_ap, reads=(), writes=(), is_output=False):
        e = self.E[q]
        i = self.dnext
        self.dnext = (self.dnext + 1) % self.NDMASEM
        ds = self.dsem[i]
        need = self._deps(reads, writes)
        k = ("d", i)
        if ds[1] > 0 and need.get(k, 0) < ds[1]:
            need[k] = ds[1]
        self._emit_waits(e, need)
        ds[1] += 16
        tok = (k, ds[1])
        e.ops.append(("d", out_ap, in_ap, k))
        self._mark(tok, reads, writes)
        if is_output:
            self.out_tokens.append(tok)
        self.ninst += 1
        return tok

    def alias(self, new_tiles, old_tiles):
        acc = {}
        for t in old_tiles:
            if t.w is not None:
                k, v = t.w
                if acc.get(k, 0) < v:
                    acc[k] = v
            for k, v in t.r.items():
                if acc.get(k, 0) < v:
                    acc[k] = v
        for t in new_tiles:
            t.w = None
            t.r = dict(acc)

    def finish(self):
        e = self.E["sp"]
        need = {}
        for k, v in self.out_tokens:
            if need.get(k, 0) < v:
                need[k] = v
        self._emit_waits(e, need)
        nc = self.nc
        sems = self.sems

        def run(e, eng):
            for o in e.ops:
                if o[0] == "w":
                    eng.wait_ge(sems[o[1]], o[2])
                elif o[0] == "i":
                    name, a, k = o[1]
                    getattr(eng, name)(*a, **k).then_inc(e.sem, 1)
                else:
                    eng.dma_start(out=o[1], in_=o[2]).then_inc(sems[o[3]], 16)

        with nc.Block() as block:
            @block.tensor
            def _(eng):
                run(self.E["pe"], eng)

            @block.vector
            def _(eng):
                run(self.E["dve"], eng)

            @block.scalar
            def _(eng):
                run(self.E["act"], eng)

            @block.gpsimd
            def _(eng):
                run(self.E["pool"], eng)

            @block.sync
            def _(eng):
                run(self.E["sp"], eng)


CONST_NAMES = ["ident", "ones", "Lf", "Lb", "SUf", "SUb", "nsf", "nsb", "inf", "inb"]


def make_consts():
    t = np.arange(128)[:, None]
    i = np.arange(128)[None, :]
    c = {
        "ident": (t == i),
        "ones": np.ones((128, 128)),
        "Lf": (t <= i),
        "Lb": (t >= i),
        "SUf": (t > i),
        "SUb": (t < i),
        "nsf": -1.0 * (i > t),
        "nsb": -1.0 * (i < t),
        "inf": (i >= t),
        "inb": (i <= t),
    }
    return np.ascontiguousarray(np.stack([np.asarray(c[n], dtype=np.float32) for n in CONST_NAMES], axis=1))


class _Stop(Exception):
    pass


def build(nc, NSEQ=4, NL=NLAYER, dbg=None, stop_after=None, heads=8, rg_blocks=12):
    P = Prog(nc)
    try:
        _build(P, nc, NSEQ, NL, dbg, stop_after, heads, rg_blocks)
    except _Stop:
        pass
    P.finish()
    return P, None


def _build(P, nc, NSEQ, NL, dbg, stop_after, heads, rg_blocks):
    dbg = dbg or []
    dbg_out = {}

    def din(name, shape, dt=F32):
        return nc.dram_tensor(name, shape, dt, kind="ExternalInput").ap()

    xT = din("xT", [NSEQ, 128, 8, SEQ])
    outT = nc.dram_tensor("outT", [NSEQ, 128, 8, SEQ], F32, kind="ExternalOutput").ap()
    win = din("win", [NLAYER, 128, 8, NIN])
    gatew = din("gatew", [NLAYER, 12, 128, 4, 128])
    rgv_d = din("rgv", [128, NLAYER * 12, 11])
    wrnn = din("wrnn", [NLAYER, 128, 12, D])
    dncw_d = din("dncw", [128, NLAYER * 24, 4])
    dnv_d = din("dnv", [128, NLAYER, 2, 16])
    dnnorm_d = din("dnnorm", [128, NLAYER])
    wdn = din("wdn", [NLAYER, 128, 8, D])
    wout = din("wout", [NLAYER, 128, 8, D])
    norms_d = din("norms", [128, NLAYER * 2 * 8])
    fnorm_d = din("fnorm", [128, 8])
    wgu = din("wgu", [NLAYER, 128, 8, 2 * DFF])
    wdown = din("wdown", [NLAYER, 128, 22, D])
    consts_d = din("consts", [128, len(CONST_NAMES), 128])
    xspill = nc.dram_tensor("xspill", [128, 8, SEQ], F32, kind="Internal").ap()
    mixspill = nc.dram_tensor("mixspill", [8, 128, SEQ], BF16, kind="Internal").ap()
    XSt = [T() for _ in range(8)]
    MSt = [T() for _ in range(8)]

    def sb(name, shape, dt=F32):
        return nc.alloc_sbuf_tensor("s_" + name, shape, dt).ap()

    def dbg_dump(name, ap, tiles, shape, dt=F32):
        if name not in dbg:
            return
        o = nc.dram_tensor("dbg_" + name, shape, dt, kind="ExternalOutput").ap()
        P.dma("sp", o, ap, reads=tiles, is_output=True)
        dbg_out[name] = o

    RX = sb("RX", [128, 16384], F32)
    RH = sb("RH", [128, 8, SEQ], BF16)
    RY = sb("RY", [128, 24576], BF16)
    WP = [sb("WP%d" % i, [128, 2048], BF16) for i in range(3)]
    WPt = [T() for _ in range(3)]
    GW = [sb("GW%d" % i, [128, 4, 128], BF16) for i in range(2)]
    GWt = [T() for _ in range(2)]
    wp_i = [0]
    gw_i = [0]
    CM = sb("CM", [128, len(CONST_NAMES), 128], F32)
    CMt = T()
    cm = {n: CM[:, i, :] for i, n in enumerate(CONST_NAMES)}
    ones_bf = sb("ones_bf", [128, 128], BF16)
    ident_bf = sb("ident_bf", [128, 128], BF16)
    CBt = T()
    mskA = sb("mskA", [128, 2, 128], F32)
    mskB = sb("mskB", [128, 2, 128], F32)
    rgv = sb("rgv", [128, NLAYER * 12, 11], F32)
    rgd = sb("rgd", [128, NLAYER * 12, 10], F32)
    rgtmp = sb("rgtmp", [128, NLAYER * 12, 2], F32)
    dncw = sb("dncw", [128, NLAYER * 24, 4], F32)
    dnv = sb("dnv", [128, NLAYER, 2, 16], F32)
    nega = sb("nega", [128, NLAYER, 16], F32)
    dnnorm = sb("dnnorm", [128, NLAYER], F32)
    norms = sb("norms", [128, NLAYER * 2 * 8], F32)
    fnorm = sb("fnorm", [128, 8], F32)
    VEC = T()
    QGbuf = sb("QGbuf", [128, 4096], BF16)
    SMbuf = sb("SMbuf", [128, 512 + 8 * 256], F32)
    YQ = [[sb("YQ%d_%d" % (sl, j), [128, 2, 256], F32) for j in range(2)] for sl in range(2)]
    YTb = [[sb("YT%d_%d" % (sl, j), [128, 2, 128], F32) for j in range(2)] for sl in range(2)]
    KQA = [sb("KQA%d" % sl, [128, 2, 128], BF16) for sl in range(2)]
    KQB = [sb("KQB%d" % sl, [128, 2, 128], BF16) for sl in range(2)]
    YQt = [[T() for _ in range(2)] for _ in range(4)]
    YTt = [[T() for _ in range(2)] for _ in range(4)]
    KQAt = [T() for _ in range(4)]
    KQBt = [T() for _ in range(4)]
    Dgt = [T() for _ in range(4)]
    D2t = [T() for _ in range(4)]

    PS = nc.alloc_psum_tensor("PS", [128, 4096], F32).ap()
    BK = [T(ps=True) for _ in range(8)]

    def bank(b, n=1):
        return PS[:, b * 512:(b + n) * 512]

    P.dma("sp", CM, consts_d, writes=[CMt])
    for dst, src in ((rgv, rgv_d), (dncw, dncw_d), (dnv, dnv_d), (dnnorm, dnnorm_d), (norms, norms_d), (fnorm, fnorm_d)):
        P.dma("sp", dst, src, writes=[VEC])
    P.op("dve", lambda e: e.tensor_copy(out=ones_bf, in_=cm["ones"]), reads=[CMt], writes=[CBt])
    P.op("dve", lambda e: e.tensor_copy(out=ident_bf, in_=cm["ident"]), reads=[CMt], writes=[CBt])
    P.op("dve", lambda e: e.tensor_copy(out=mskA[:, 0, :], in_=cm["nsf"]), reads=[CMt], writes=[CBt])
    P.op("dve", lambda e: e.tensor_copy(out=mskA[:, 1, :], in_=cm["inf"]), reads=[CMt], writes=[CBt])
    P.op("dve", lambda e: e.tensor_copy(out=mskB[:, 0, :], in_=cm["nsb"]), reads=[CMt], writes=[CBt])
    P.op("dve", lambda e: e.tensor_copy(out=mskB[:, 1, :], in_=cm["inb"]), reads=[CMt], writes=[CBt])
    P.op("act", lambda e: e.mul(out=rgd[:, :, 0:4], in_=rgv[:, :, 5:9], mul=0.5), reads=[VEC], writes=[VEC])
    P.op("act", lambda e: e.activation(out=rgtmp, in_=rgv[:, :, 9:11], func=AF.Exp, scale=-1.0), reads=[VEC], writes=[VEC])
    P.op("act", lambda e: e.activation(out=rgtmp, in_=rgtmp, func=AF.Ln, bias=1.0), reads=[VEC], writes=[VEC])
    P.op("act", lambda e: e.mul(out=rgd[:, :, 4:6], in_=rgtmp, mul=-4.0), reads=[VEC], writes=[VEC])
    P.op("act", lambda e: e.mul(out=rgd[:, :, 6:8], in_=rgtmp, mul=-8.0), reads=[VEC], writes=[VEC])
    P.op("act", lambda e: e.mul(out=rgd[:, :, 8:10], in_=rgtmp, mul=4.0), reads=[VEC], writes=[VEC])
    P.op("act", lambda e: e.activation(out=nega, in_=dnv[:, :, 0, :], func=AF.Exp), reads=[VEC], writes=[VEC])
    P.op("dve", lambda e: e.tensor_scalar(out=nega, in0=nega, scalar1=-1.0, scalar2=None, op0=ALU.mult), reads=[VEC], writes=[VEC])

    def load_w(src_ap, shape):
        i = wp_i[0]
        wp_i[0] = (i + 1) % len(WP)
        n = int(np.prod(shape[1:]))
        view = WP[i][:, 0:n].rearrange("p (a b) -> p a b", a=shape[1])
        P.dma("pool", view, src_ap, writes=[WPt[i]])
        return view, WPt[i]

    def proj(pb, ptiles, wv, wt, rhs_fn, rhs_tiles, nk, ncols=SEQ):
        nt = ncols // 512
        for k in range(nk):
            for tt in range(nt):
                P.op("pe", lambda e, tt=tt, k=k: e.matmul(pb[:, tt * 512:(tt + 1) * 512], lhsT=wv[:, k, :], rhs=rhs_fn(k, tt),
                                                            start=(k == 0), stop=(k == nk - 1)),
                     reads=[wt] + rhs_tiles, writes=[ptiles[tt]])

    QGt = [T(), T()]
    SM = {nm: T() for nm in ["bt", "lnb", "g", "gc", "colE", "colC", "colD", "egl", "tmp"]}
    Xv = RX.rearrange("p (c t) -> p c t", c=8)
    Xt = [T() for _ in range(8)]
    Ht = [T() for _ in range(8)]
    RYt = [T() for _ in range(12)]
    RYb = RY.rearrange("p (c t) -> p c t", c=12)
    RYf = RY.bitcast(F32)
    hfn = lambda k, tt: RH[:, k, tt * 512:(tt + 1) * 512]
    pgrp = [0]

    def grp():
        g_ = pgrp[0]
        pgrp[0] ^= 1
        return bank(4 * g_, 4), BK[4 * g_:4 * g_ + 4]

    def rms_stats(scale):
        tmp_t = [T() for _ in range(10)]
        P.alias(tmp_t, RYt)
        lnv = RYf[:, 8192:10240]
        rstd = RYf[:, 10240:12288]
        for c in range(8):
            P.op("act", lambda e, c=c: e.activation(out=RYb[:, c, :], in_=Xv[:, c, :], func=AF.Square), reads=[Xt[c]], writes=[tmp_t[c]])
        for tt in range(4):
            for c in range(8):
                P.op("pe", lambda e, c=c, tt=tt: e.matmul(bank(tt), lhsT=ones_bf, rhs=RYb[:, c, tt * 512:(tt + 1) * 512],
                                                            start=(c == 0), stop=(c == 7)),
                     reads=[CBt, tmp_t[c]], writes=[BK[tt]])
        P.op("act", lambda e: e.activation(out=lnv, in_=bank(0, 4), func=AF.Ln, bias=EPS, scale=scale), reads=BK[0:4], writes=[tmp_t[8]])
        P.op("act", lambda e: e.activation(out=rstd, in_=lnv, func=AF.Exp, scale=-0.5), reads=[tmp_t[8]], writes=[tmp_t[9]])
        return rstd, tmp_t

    rxp = RX[:, 0:2052]
    F0 = RX[:, 2052:4100]
    F1 = RX[:, 4100:6148]
    F2 = RX[:, 6148:8196]
    HF = RX[:, 8196:10244]
    HB = RX[:, 10244:12292]
    ub = RX[:, 12292:13316].bitcast(BF16)
    thi = RX[:, 13316:14340].bitcast(BF16)
    sqv = RX[:, 14340:15364].bitcast(BF16)
    QK = RX[:, 8196:10244].bitcast(BF16).rearrange("p (c two d) -> p c two d", c=16, two=2)
    QKf = RX[:, 8196:10244].bitcast(BF16).rearrange("p (c n) -> p c n", c=16)
    ktok = RX[:, 10244:11268].bitcast(BF16).rearrange("p (c d) -> p c d", c=16)
    vtok = RX[:, 11268:12292].bitcast(BF16).rearrange("p (c d) -> p c d", c=16)
    KDB = RX[:, 12292:14340].bitcast(BF16).rearrange("p (two c d) -> p two c d", two=2, c=16)
    tb = 14340
    S32 = [RX[:, tb:tb + 128], RX[:, tb + 128:tb + 256]]
    Sbf = [RX[:, tb + 256:tb + 320].bitcast(BF16), RX[:, tb + 320:tb + 384].bitcast(BF16)]
    rbf = [RX[:, tb + 384:tb + 448].bitcast(BF16), RX[:, tb + 448:tb + 512].bitcast(BF16)]
    xbf = [RX[:, tb + 512:tb + 576].bitcast(BF16), RX[:, tb + 576:tb + 640].bitcast(BF16)]
    NS = 4
    Dg = [RX[:, tb + 640:tb + 896].rearrange("p (a b) -> p a b", a=2), RX[:, tb + 896:tb + 1152].rearrange("p (a b) -> p a b", a=2)]
    D2 = [RX[:, tb + 1152:tb + 1280].bitcast(BF16).rearrange("p (a b) -> p a b", a=2),
          RX[:, tb + 1280:tb + 1408].bitcast(BF16).rearrange("p (a b) -> p a b", a=2)]
    for b0 in (2052, 4228):
        YQ.append([RX[:, b0:b0 + 512].rearrange("p (a b) -> p a b", a=2), RX[:, b0 + 512:b0 + 1024].rearrange("p (a b) -> p a b", a=2)])
        YTb.append([RX[:, b0 + 1024:b0 + 1280].rearrange("p (a b) -> p a b", a=2), RX[:, b0 + 1280:b0 + 1536].rearrange("p (a b) -> p a b", a=2)])
        Dg.append(RX[:, b0 + 1536:b0 + 1792].rearrange("p (a b) -> p a b", a=2))
        KQA.append(RX[:, b0 + 1792:b0 + 1920].bitcast(BF16).rearrange("p (a b) -> p a b", a=2))
        KQB.append(RX[:, b0 + 1920:b0 + 2048].bitcast(BF16).rearrange("p (a b) -> p a b", a=2))
        D2.append(RX[:, b0 + 2048:b0 + 2176].bitcast(BF16).rearrange("p (a b) -> p a b", a=2))
    YQr = [[a_.bitcast(F32R) for a_ in sl_] for sl_ in YQ[0:NS]]
    YTr = [[a_.bitcast(F32R) for a_ in sl_] for sl_ in YTb[0:NS]]
    Dgq = [RX[:, tb + 1408:tb + 1536], RX[:, tb + 1536:tb + 1664]]
    QG = QGbuf.rearrange("p (two c d) -> p two c d", two=2, c=16)
    RYtail = RY[:, 16384:24576]
    qsb, ksb, vsb, sqb = (RYtail[:, 0:2048], RYtail[:, 2048:4096], RYtail[:, 4096:6144], RYtail[:, 6144:8192])
    T2T = RYtail[:, 0:4096].rearrange("p (c two d) -> p c two d", c=16, two=2)
    ATT = RYtail[:, 4096:8192].rearrange("p (c two d) -> p c two d", c=16, two=2)
    OT = RX[:, 0:2048]
    bt = SMbuf[:, 0:512].rearrange("p (c k) -> p c k", c=16)
    smv = lambda i: SMbuf[:, 512 + 256 * i:512 + 256 * (i + 1)].rearrange("p (c k) -> p c k", c=16)
    lnb, gg, gc, colE, colC, colD, egl, stmp = [smv(i) for i in range(8)]

    for s in range(NSEQ):
        for c in range(8):
            P.dma("sp", Xv[:, c, :], xT[s, :, c, :], writes=[Xt[c]])

        for l in range(NL):
            rstd, tmp_t = rms_stats(1.0 / D)
            for c in range(8):
                g = norms[:, (l * 2 + 0) * 8 + c:(l * 2 + 0) * 8 + c + 1]
                P.op("dve", lambda e, c=c, g=g: e.scalar_tensor_tensor(out=RH[:, c, :], in0=Xv[:, c, :], scalar=g, in1=rstd,
                                                                         op0=ALU.mult, op1=ALU.mult),
                     reads=[Xt[c], tmp_t[9], VEC], writes=[Ht[c]])
            P.alias(RYt, tmp_t)
            if s == 0 and l == 0:
                dbg_dump("h", RH, Ht, [128, 8, SEQ], BF16)
            for c in range(8):
                P.dma("sp", xspill[:, c, :], Xv[:, c, :], reads=[Xt[c]], writes=[XSt[c]])
            W = {k_: T() for k_ in ["rxp", "F0", "F1", "F2", "HF", "HB", "ub", "thi", "sq"]}
            P.alias(list(W.values()), Xt)
            W["F1b"] = T()
            W["F2b"] = T()
            P.alias([W["F1b"]], QGt)
            P.alias([W["F2b"]], list(SM.values()))
            F1d = [F1, QGbuf.bitcast(F32)]
            F2d = [F2, SMbuf[:, 0:2048]]
            F1k = ["F1", "F1b"]
            F2k = ["F2", "F2b"]
            P.op("pool", lambda e: e.memset(rxp[:, 0:2], 0.0), writes=[W["rxp"]])
            P.op("pool", lambda e: e.memset(rxp[:, 2050:2052], 0.0), writes=[W["rxp"]])

            for n in range(rg_blocks):
                ln_ = l * 12 + n
                wv, wt = load_w(win[l, :, :, n * 128:(n + 1) * 128], [128, 8, 128])
                gi = gw_i[0]
                gw_i[0] ^= 1
                P.dma("pool", GW[gi], gatew[l, n], writes=[GWt[gi]])
                pb, pt = grp()
                proj(pb, pt, wv, wt, hfn, Ht, 8)
                P.op("act", lambda e, pb=pb: e.copy(out=rxp[:, 2:2050], in_=pb), reads=pt, writes=[W["rxp"]])
                cw = lambda j, ln_=ln_: rgv[:, ln_, j:j + 1]
                P.op("dve", lambda e, cw=cw: e.tensor_scalar(out=F0, in0=rxp[:, 0:2048], scalar1=cw(0), scalar2=cw(4), op0=ALU.mult, op1=ALU.add),
                     reads=[W["rxp"], VEC], writes=[W["F0"]])
                P.op("dve", lambda e, cw=cw: e.scalar_tensor_tensor(out=F1, in0=rxp[:, 1:2049], scalar=cw(1), in1=F0, op0=ALU.mult, op1=ALU.add),
                     reads=[W["rxp"], VEC, W["F0"]], writes=[W["F1"]])
                P.op("dve", lambda e, cw=cw: e.scalar_tensor_tensor(out=F0, in0=rxp[:, 2:2050], scalar=cw(2), in1=F1, op0=ALU.mult, op1=ALU.add),
                     reads=[W["rxp"], VEC, W["F1"]], writes=[W["F0"]])
                P.op("dve", lambda e, cw=cw: e.scalar_tensor_tensor(out=ub, in0=rxp[:, 3:2051], scalar=cw(3), in1=F0, op0=ALU.mult, op1=ALU.add),
                     reads=[W["rxp"], VEC, W["F0"]], writes=[W["ub"]])
                ufn = lambda k, tt: ub[:, tt * 512:(tt + 1) * 512]
                gv = GW[gi]
                for d in range(2):
                    pb, pt = grp()
                    proj(pb, pt, gv[:, d:d + 1, :], GWt[gi], ufn, [W["ub"]], 1)
                    P.op("act", lambda e, pb=pb, d=d, ln_=ln_: e.activation(out=F0, in_=pb, func=AF.Tanh, scale=0.5, bias=rgd[:, ln_, d:d + 1]),
                         reads=pt + [VEC], writes=[W["F0"]])
                    pb, pt = grp()
                    proj(pb, pt, gv[:, 2 + d:3 + d, :], GWt[gi], ufn, [W["ub"]], 1)
                    P.op("act", lambda e, pb=pb, d=d, ln_=ln_: e.activation(out=thi, in_=pb, func=AF.Tanh, scale=0.5, bias=rgd[:, ln_, 2 + d:3 + d]),
                         reads=pt + [VEC], writes=[W["thi"]])
                    F1x, F2x, k1, k2 = F1d[d], F2d[d], F1k[d], F2k[d]
                    P.op("act", lambda e, d=d, ln_=ln_, F1x=F1x: e.activation(out=F1x, in_=F0, func=AF.Exp, scale=rgd[:, ln_, 4 + d:5 + d], bias=rgd[:, ln_, 4 + d:5 + d]),
                         reads=[W["F0"], VEC], writes=[W[k1]])
                    P.op("act", lambda e, d=d, ln_=ln_, F2x=F2x: e.activation(out=F2x, in_=F0, func=AF.Tanh, scale=rgd[:, ln_, 8 + d:9 + d], bias=rgd[:, ln_, 8 + d:9 + d]),
                         reads=[W["F0"], VEC], writes=[W[k2]])
                    P.op("act", lambda e, d=d, ln_=ln_: e.activation(out=F0, in_=F0, func=AF.Exp, scale=rgd[:, ln_, 6 + d:7 + d], bias=rgd[:, ln_, 6 + d:7 + d]),
                         reads=[W["F0"], VEC], writes=[W["F0"]])
                    P.op("dve", lambda e, F2x=F2x: e.scalar_tensor_tensor(out=F0, in0=F0, scalar=1.0, in1=F2x, op0=ALU.add, op1=ALU.mult),
                         reads=[W["F0"], W[k2]], writes=[W["F0"]])
                    P.op("act", lambda e: e.activation(out=F0, in_=F0, func=AF.Ln), reads=[W["F0"]], writes=[W["F0"]])
                    P.op("act", lambda e: e.activation(out=sqv, in_=F0, func=AF.Exp, scale=0.5), reads=[W["F0"]], writes=[W["sq"]])
                    P.op("dve", lambda e, F2x=F2x: e.scalar_tensor_tensor(out=F2x, in0=thi, scalar=1.0, in1=ub, op0=ALU.add, op1=ALU.mult),
                         reads=[W["thi"], W["ub"]], writes=[W[k2]])
                    P.op("dve", lambda e, F2x=F2x: e.scalar_tensor_tensor(out=F2x, in0=F2x, scalar=0.5, in1=sqv, op0=ALU.mult, op1=ALU.mult),
                         reads=[W[k2], W["sq"]], writes=[W[k2]])
                    if d == 0:
                        P.op("dve", lambda e, F1x=F1x, F2x=F2x: e.tensor_tensor_scan(out=HF, data0=F1x, data1=F2x, initial=0.0, op0=ALU.mult, op1=ALU.add),
                             reads=[W[k1], W[k2]], writes=[W["HF"]])
                    else:
                        P.op("dve", lambda e, F1x=F1x, F2x=F2x: e.tensor_tensor_scan(out=HB[:, ::-1], data0=F1x[:, ::-1], data1=F2x[:, ::-1], initial=0.0,
                                                                     op0=ALU.mult, op1=ALU.add),
                             reads=[W[k1], W[k2]], writes=[W["HB"]])
                wv, wt = load_w(win[l, :, :, C_RY + n * 128:C_RY + (n + 1) * 128], [128, 8, 128])
                pb, pt = grp()
                proj(pb, pt, wv, wt, hfn, Ht, 8)
                P.op("act", lambda e, pb=pb: e.activation(out=F0, in_=pb, func=AF.Square), reads=pt, writes=[W["F0"]])
                P.op("dve", lambda e: e.tensor_scalar(out=F0, in0=F0, scalar1=0.044715, scalar2=1.0, op0=ALU.mult, op1=ALU.add),
                     reads=[W["F0"]], writes=[W["F0"]])
                P.op("dve", lambda e, pb=pb: e.tensor_tensor(out=F0, in0=F0, in1=pb, op=ALU.mult), reads=[W["F0"]] + pt, writes=[W["F0"]])
                P.op("act", lambda e: e.activation(out=F1, in_=F0, func=AF.Tanh, scale=0.7978845608028654), reads=[W["F0"]], writes=[W["F1"]])
                P.op("dve", lambda e: e.tensor_tensor(out=F2, in0=HF, in1=HB, op=ALU.add), reads=[W["HF"], W["HB"]], writes=[W["F2"]])
                P.op("dve", lambda e, pb=pb: e.scalar_tensor_tensor(out=F0, in0=F1, scalar=1.0, in1=pb, op0=ALU.add, op1=ALU.mult),
                     reads=[W["F1"]] + pt, writes=[W["F0"]])
                P.op("dve", lambda e, n=n: e.scalar_tensor_tensor(out=RYb[:, n, :], in0=F0, scalar=0.5, in1=F2, op0=ALU.mult, op1=ALU.mult),
                     reads=[W["F0"], W["F2"]], writes=[RYt[n]])
            if s == 0 and l == 0:
                dbg_dump("yr", RYb, RYt, [128, 12, SEQ], BF16)
            if stop_after == "rg":
                raise _Stop()

            P.alias(QGt, [W["F1b"]])
            P.alias(list(SM.values()), [W["F2b"]])
            stg = [RX[:, 8196:9220].bitcast(BF16), RX[:, 9220:10244].bitcast(BF16)]
            stg_t = [T(), T()]
            P.alias(stg_t, [W["HF"]])
            yfn = lambda k, tt: RYb[:, k, tt * 512:(tt + 1) * 512]
            for n in range(8):
                wv, wt = load_w(wrnn[l, :, :, n * 128:(n + 1) * 128], [128, 12, 128])
                pb, pt = grp()
                proj(pb, pt, wv, wt, yfn, RYt, 12)
                wv2, wt2 = load_w(win[l, :, :, C_GR + n * 128:C_GR + (n + 1) * 128], [128, 8, 128])
                pb2, pt2 = grp()
                proj(pb2, pt2, wv2, wt2, hfn, Ht, 8)
                P.op("act", lambda e, pb2=pb2: e.activation(out=F0, in_=pb2, func=AF.Tanh, scale=0.5), reads=pt2, writes=[W["F0"]])
                si = n % 2
                P.op("dve", lambda e, pb=pb, si=si: e.scalar_tensor_tensor(out=stg[si], in0=F0, scalar=1.0, in1=pb, op0=ALU.add, op1=ALU.mult),
                     reads=[W["F0"]] + pt, writes=[stg_t[si]])
                P.dma("sp", mixspill[n], stg[si], reads=[stg_t[si]], writes=[MSt[n]])

            DN = {nm: T() for nm in ["QK", "ktok", "vtok", "KDB", "S0", "S1", "Sb0", "Sb1", "r0", "r1", "x0", "x1",
                                     "Dgq0", "Dgq1"]}
            P.alias(list(DN.values()) + Dgt[0:2] + D2t[0:2], [W["HF"], W["HB"], W["ub"], W["thi"], W["sq"]] + stg_t)
            DR = {nm: T() for nm in ["qs", "ks", "vs", "sqb"]}
            P.alias(list(DR.values()), RYt[8:12])
            T2Tt = [T() for _ in range(16)]
            ATTt = [T() for _ in range(16)]
            OTt = [T() for _ in range(16)]
            wv, wt = load_w(win[l, :, :, C_BA:C_BA + 32], [128, 8, 32])
            pbt = bank(0).rearrange("p (c k) -> p c k", c=16)
            for c in range(16):
                for k in range(8):
                    P.op("pe", lambda e, c=c, k=k: e.matmul(pbt[:, c, :], lhsT=RH[:, k, c * 128:(c + 1) * 128], rhs=wv[:, k, :],
                                                              start=(k == 0), stop=(k == 7)),
                         reads=[Ht[k], wt], writes=[BK[0]])
            P.op("act", lambda e: e.copy(out=bt, in_=pbt), reads=[BK[0]], writes=[SM["bt"]])
            P.op("act", lambda e: e.activation(out=lnb, in_=bt[:, :, 0:16], func=AF.Exp, scale=-1.0), reads=[SM["bt"]], writes=[SM["lnb"]])
            P.op("act", lambda e: e.activation(out=lnb, in_=lnb, func=AF.Ln, bias=1.0), reads=[SM["lnb"]], writes=[SM["lnb"]])
            P.op("dve", lambda e: e.tensor_scalar(out=lnb, in0=lnb, scalar1=-1.0, scalar2=None, op0=ALU.mult), reads=[SM["lnb"]], writes=[SM["lnb"]])
            dtb = dnv[:, l, 1, :].unsqueeze(1).to_broadcast([128, 16, 16])
            ngb = nega[:, l, :].unsqueeze(1).to_broadcast([128, 16, 16])
            P.op("dve", lambda e: e.tensor_tensor(out=gg, in0=bt[:, :, 16:32], in1=dtb, op=ALU.add), reads=[SM["bt"], VEC], writes=[SM["g"]])
            P.op("act", lambda e: e.activation(out=gg, in_=gg, func=AF.Exp), reads=[SM["g"]], writes=[SM["g"]])
            P.op("act", lambda e: e.activation(out=gg, in_=gg, func=AF.Ln, bias=1.0), reads=[SM["g"]], writes=[SM["g"]])
            P.op("dve", lambda e: e.tensor_tensor(out=gg, in0=gg, in1=ngb, op=ALU.mult), reads=[SM["g"], VEC], writes=[SM["g"]])
            pgc = bank(1)[:, 0:256].rearrange("p (c k) -> p c k", c=16)
            pgl = bank(1)[:, 256:512].rearrange("p (c k) -> p c k", c=16)
            P.op("pe", lambda e: e.matmul(pgc[:, :, 0:8], lhsT=cm["Lf"], rhs=gg[:, :, 0:8], start=True, stop=True), reads=[CMt, SM["g"]], writes=[BK[1]])
            P.op("pe", lambda e: e.matmul(pgc[:, :, 8:16], lhsT=cm["Lb"], rhs=gg[:, :, 8:16], start=True, stop=True), reads=[CMt, SM["g"]], writes=[BK[1]])
            P.op("pe", lambda e: e.matmul(pgl, lhsT=cm["ones"], rhs=gg, start=True, stop=True), reads=[CMt, SM["g"]], writes=[BK[1]])
            P.op("act", lambda e: e.copy(out=gc, in_=pgc), reads=[BK[1]], writes=[SM["gc"]])
            P.op("act", lambda e: e.activation(out=colE, in_=pgc, func=AF.Exp), reads=[BK[1]], writes=[SM["colE"]])
            P.op("act", lambda e: e.activation(out=egl, in_=pgl, func=AF.Exp), reads=[BK[1]], writes=[SM["egl"]])
            P.op("dve", lambda e: e.tensor_scalar(out=colC, in0=colE, scalar1=-1.0, scalar2=None, op0=ALU.mult), reads=[SM["colE"]], writes=[SM["colC"]])
            P.op("dve", lambda e: e.tensor_tensor(out=stmp, in0=pgl, in1=gc, op=ALU.subtract), reads=[BK[1], SM["gc"]], writes=[SM["tmp"]])
            P.op("dve", lambda e: e.tensor_tensor(out=stmp, in0=stmp, in1=lnb, op=ALU.add), reads=[SM["tmp"], SM["lnb"]], writes=[SM["tmp"]])
            P.op("act", lambda e: e.activation(out=colD, in_=stmp, func=AF.Exp), reads=[SM["tmp"]], writes=[SM["colD"]])
            if s == 0 and l == 0:
                dbg_dump("sm", SMbuf, list(SM.values()), [128, 512 + 8 * 256], F32)
            if stop_after == "sm":
                raise _Stop()

            p1v = [bank(sl).rearrange("p (g n) -> p g n", g=2) for sl in range(NS)]
            p2f = [bank(4 + sl)[:, 0:256].rearrange("p (g n) -> p g n", g=2) for sl in range(NS)]
            bA = bank(0, 4)
            bAt = BK[0:4]

            for hd in range(heads):
                dh = [hd, 8 + hd]
                for blk, dst, dkey in ((hd, qsb, "qs"), (8 + hd, ksb, "ks"), (16 + hd, vsb, "vs")):
                    wv, wt = load_w(win[l, :, :, C_QKV + blk * 128:C_QKV + (blk + 1) * 128], [128, 8, 128])
                    proj(bA, bAt, wv, wt, hfn, Ht, 8)
                    P.op("act", lambda e: e.copy(out=rxp[:, 2:2050], in_=bA), reads=bAt, writes=[W["rxp"]] + OTt)
                    cw = lambda j, r_=l * 24 + blk: dncw[:, r_, j:j + 1]
                    P.op("dve", lambda e, cw=cw: e.tensor_scalar(out=F0, in0=rxp[:, 0:2048], scalar1=cw(0), scalar2=None, op0=ALU.mult),
                         reads=[W["rxp"], VEC], writes=[W["F0"]])
                    P.op("dve", lambda e, cw=cw: e.scalar_tensor_tensor(out=F1, in0=rxp[:, 1:2049], scalar=cw(1), in1=F0, op0=ALU.mult, op1=ALU.add),
                         reads=[W["rxp"], VEC, W["F0"]], writes=[W["F1"]])
                    P.op("dve", lambda e, cw=cw: e.scalar_tensor_tensor(out=F0, in0=rxp[:, 2:2050], scalar=cw(2), in1=F1, op0=ALU.mult, op1=ALU.add),
                         reads=[W["rxp"], VEC, W["F1"]], writes=[W["F0"]])
                    P.op("dve", lambda e, cw=cw: e.scalar_tensor_tensor(out=F1, in0=rxp[:, 3:2051], scalar=cw(3), in1=F0, op0=ALU.mult, op1=ALU.add),
                         reads=[W["rxp"], VEC, W["F0"]], writes=[W["F1"]])
                    P.op("act", lambda e: e.activation(out=F2, in_=F1, func=AF.Tanh, scale=0.5), reads=[W["F1"]], writes=[W["F2"]])
                    P.op("dve", lambda e, dst=dst: e.scalar_tensor_tensor(out=dst, in0=F2, scalar=1.0, in1=F1, op0=ALU.add, op1=ALU.mult),
                         reads=[W["F2"], W["F1"]], writes=[DR[dkey]] + T2Tt + ATTt)
                for src, skey, slot, scl in ((ksb, "ks", 0, 1.0), (qsb, "qs", 1, 128.0 ** -0.5)):
                    P.op("act", lambda e, src=src: e.activation(out=sqb, in_=src, func=AF.Square), reads=[DR[skey]], writes=[DR["sqb"]])
                    for tt in range(4):
                        P.op("pe", lambda e, tt=tt: e.matmul(bank(tt), lhsT=ones_bf, rhs=sqb[:, tt * 512:(tt + 1) * 512], start=True, stop=True),
                             reads=[CBt, DR["sqb"]], writes=[BK[tt]])
                    P.op("act", lambda e: e.activation(out=F0, in_=bA, func=AF.Ln, bias=4.0 * EPS), reads=bAt, writes=[W["F0"]])
                    P.op("act", lambda e: e.activation(out=F0, in_=F0, func=AF.Exp, scale=-0.5), reads=[W["F0"]], writes=[W["F0"]])
                    P.op("dve", lambda e, src=src, slot=slot, scl=scl: e.scalar_tensor_tensor(
                        out=QK[:, :, slot, :], in0=src.rearrange("p (c d) -> p c d", c=16), scalar=scl,
                        in1=F0.rearrange("p (c d) -> p c d", c=16), op0=ALU.mult, op1=ALU.mult),
                         reads=[DR[skey], W["F0"]], writes=[DN["QK"]])
                ptr = bank(0, 2).bitcast(BF16).rearrange("p (c d) -> p c d", c=16)
                ptr2 = bank(2, 2).bitcast(BF16).rearrange("p (c d) -> p c d", c=16)
                for c in range(16):
                    P.op("pe", lambda e, c=c: e.transpose(ptr[:, c, :], QK[:, c, 0, :], ident_bf), reads=[DN["QK"], CBt], writes=[BK[0], BK[1]])
                P.op("act", lambda e: e.copy(out=ktok, in_=ptr), reads=[BK[0], BK[1]], writes=[DN["ktok"]])
                for c in range(16):
                    P.op("pe", lambda e, c=c: e.transpose(ptr2[:, c, :], vsb[:, c * 128:(c + 1) * 128], ident_bf), reads=[DR["vs"], CBt], writes=[BK[2], BK[3]])
                P.op("act", lambda e: e.mul(out=vtok, in_=ptr2, mul=0.5), reads=[BK[2], BK[3]], writes=[DN["vtok"]])
                for d in range(2):
                    cdb = colD[:, :, dh[d]:dh[d] + 1].to_broadcast([128, 16, 128])
                    P.op("dve", lambda e, d=d, cdb=cdb: e.tensor_tensor(out=KDB[:, d], in0=ktok, in1=cdb, op=ALU.mult),
                         reads=[DN["ktok"], SM["colD"]], writes=[DN["KDB"]])
                    for c4 in range(4):
                        prb = bank(3).rearrange("p (c d) -> p c d", c=4)
                        for cc in range(4):
                            c = c4 * 4 + cc
                            j = c % 2
                            P.op("act", lambda e, c=c, d=d, j=j: e.mul(out=Dgq[j], in_=cm["ident"], mul=colE[:, c, dh[d]:dh[d] + 1]),
                                 reads=[CMt, SM["colE"]], writes=[DN["Dgq%d" % j]])
                            P.op("pe", lambda e, cc=cc, j=j: e.matmul(prb[:, cc, :], lhsT=cm["ones"], rhs=Dgq[j], start=True, stop=True),
                                 reads=[CMt, DN["Dgq%d" % j]], writes=[BK[3]])
                        P.op("dve", lambda e, d=d, c4=c4: e.tensor_tensor(out=QG[:, d, c4 * 4:(c4 + 1) * 4, :], in0=QK[:, c4 * 4:(c4 + 1) * 4, 1, :],
                                                                            in1=prb, op=ALU.mult),
                             reads=[DN["QK"], BK[3]], writes=[QGt[d]])

                if stop_after == "prep":
                    dbg_dump("QK", RX[:, 8196:10244].bitcast(BF16), [DN["QK"]], [128, 4096], BF16)
                    dbg_dump("ktok", RX[:, 10244:11268].bitcast(BF16), [DN["ktok"]], [128, 2048], BF16)
                    dbg_dump("vtok", RX[:, 11268:12292].bitcast(BF16), [DN["vtok"]], [128, 2048], BF16)
                    dbg_dump("KDB", RX[:, 12292:14340].bitcast(BF16), [DN["KDB"]], [128, 4096], BF16)
                    dbg_dump("QG", QGbuf, QGt, [128, 4096], BF16)
                    raise _Stop()
                def units_setup(pairs):
                    for sl, c in pairs:
                        P.op("pe", lambda e, sl=sl, c=c: e.matmul(p1v[sl][:, 0, :], lhsT=QK[:, c, 0, :], rhs=QKf[:, c, :], start=True, stop=True),
                             reads=[DN["QK"]], writes=[BK[sl]])
                    for sl, c in pairs:
                        kq = p1v[sl][:, 0, :].rearrange("p (a b) -> p a b", a=2)
                        P.op("dve", lambda e, sl=sl, kq=kq: e.tensor_tensor(out=KQA[sl], in0=kq, in1=mskA, op=ALU.mult), reads=[BK[sl], CBt], writes=[KQAt[sl]])
                        P.op("dve", lambda e, sl=sl, kq=kq: e.tensor_tensor(out=KQB[sl], in0=kq, in1=mskB, op=ALU.mult), reads=[BK[sl], CBt], writes=[KQBt[sl]])
                        for d in range(2):
                            Lm = cm["Lf"] if d == 0 else cm["Lb"]
                            P.op("dve", lambda e, sl=sl, c=c, d=d, Lm=Lm: e.tensor_scalar(out=Dg[sl][:, d, :], in0=Lm, scalar1=gg[:, c, dh[d]:dh[d] + 1], scalar2=None, op0=ALU.mult),
                                 reads=[CMt, SM["g"]], writes=[Dgt[sl]])
                    for sl, c in pairs:
                        for d in range(2):
                            Sm = cm["SUf"] if d == 0 else cm["SUb"]
                            pe_ = p1v[sl][:, 1, d * 128:(d + 1) * 128]
                            P.op("pe", lambda e, sl=sl, d=d, Sm=Sm, pe_=pe_: e.matmul(pe_, lhsT=Sm, rhs=Dg[sl][:, d, :], start=True, stop=True),
                                 reads=[CMt, Dgt[sl]], writes=[BK[sl]])
                    for sl, c in pairs:
                        for d in range(2):
                            pe_ = p1v[sl][:, 1, d * 128:(d + 1) * 128]
                            P.op("act", lambda e, sl=sl, c=c, d=d, pe_=pe_: e.activation(out=D2[sl][:, d, :], in_=pe_, func=AF.Exp, bias=lnb[:, c, dh[d]:dh[d] + 1]),
                                 reads=[BK[sl], SM["lnb"]], writes=[D2t[sl]])
                    for sl, c in pairs:
                        for d in range(2):
                            kq_ = KQA[sl] if d == 0 else KQB[sl]
                            kqt = KQAt[sl] if d == 0 else KQBt[sl]
                            P.op("dve", lambda e, sl=sl, d=d, kq_=kq_: e.tensor_tensor(out=YQr[sl][0][:, d, 0:128], in0=kq_[:, 0, :], in1=D2[sl][:, d, :], op=ALU.mult),
                                 reads=[kqt, D2t[sl]], writes=[YQt[sl][0]])
                    for sl, c in pairs:
                        for d in range(2):
                            P.op("pe", lambda e, sl=sl, d=d: e.transpose(p2f[sl][:, d, :], YQ[sl][0][:, d, 0:128], cm["ident"]),
                                 reads=[YQt[sl][0], CMt], writes=[BK[4 + sl]])
                    for sl, c in pairs:
                        P.op("act", lambda e, sl=sl: e.copy(out=YTr[sl][0], in_=p2f[sl]), reads=[BK[4 + sl]], writes=[YTt[sl][0]])
                    for sl, c in pairs:
                        for d in range(2):
                            kq_ = KQA[sl] if d == 0 else KQB[sl]
                            kqt = KQAt[sl] if d == 0 else KQBt[sl]
                            P.op("dve", lambda e, sl=sl, c=c, d=d, kq_=kq_: e.tensor_tensor(out=ATT[:, c, d, :], in0=kq_[:, 1, :], in1=D2[sl][:, d, :], op=ALU.mult),
                                 reads=[kqt, D2t[sl]], writes=[ATTt[c], DR["vs"], DR["sqb"]])

                def unit_level_mm(sl, c, k):
                    ci = k % 2
                    yq, yt = YQr[sl][ci], YTr[sl][ci]
                    rd = [YQt[sl][ci], YTt[sl][ci]]
                    for g_ in range(2):
                        if k == 0:
                            P.op("pe", lambda e, g_=g_: e.matmul(p1v[sl][:, g_, 0:128], lhsT=yt[:, g_, :], rhs=yq[:, g_, 0:128], start=True, stop=True),
                                 reads=rd, writes=[BK[sl]])
                        elif k < 6:
                            P.op("pe", lambda e, g_=g_: e.matmul(p1v[sl][:, g_, :], lhsT=yt[:, g_, :], rhs=yq[:, g_, :], start=True, stop=True),
                                 reads=rd, writes=[BK[sl]])
                        else:
                            P.op("pe", lambda e, g_=g_: e.matmul(p1v[sl][:, g_, 128:256], lhsT=yt[:, g_, :], rhs=yq[:, g_, 128:256], start=True, stop=True),
                                 reads=rd, writes=[BK[sl]])
                    if k < 6:
                        for g_ in range(2):
                            P.op("pe", lambda e, g_=g_: e.matmul(p2f[sl][:, g_, :], lhsT=yq[:, g_, 0:128], rhs=yt[:, g_, :], start=True, stop=True),
                                 reads=rd, writes=[BK[4 + sl]])

                def unit_level_ev(sl, c, k):
                    ci = k % 2
                    ni = 1 - ci
                    yq = YQ[sl][ci]
                    if k < 6:
                        P.op("act", lambda e: e.copy(out=YQr[sl][ni][:, :, 0:128], in_=p1v[sl][:, :, 0:128]), reads=[BK[sl]], writes=[YQt[sl][ni]])
                        P.op("act", lambda e: e.copy(out=YTr[sl][ni], in_=p2f[sl]), reads=[BK[4 + sl]], writes=[YTt[sl][ni]])
                        if k == 0:
                            idb = cm["ident"].unsqueeze(1).to_broadcast([128, 2, 128])
                            P.op("dve", lambda e: e.tensor_tensor(out=YQr[sl][ni][:, :, 128:256], in0=yq[:, :, 0:128], in1=idb, op=ALU.add),
                                 reads=[YQt[sl][ci], CMt], writes=[YQt[sl][ni]])
                        else:
                            P.op("dve", lambda e: e.tensor_tensor(out=YQr[sl][ni][:, :, 128:256], in0=yq[:, :, 128:256], in1=p1v[sl][:, :, 128:256], op=ALU.add),
                                 reads=[YQt[sl][ci], BK[sl]], writes=[YQt[sl][ni]])
                    else:
                        P.op("dve", lambda e: e.tensor_tensor(out=T2T[:, c, :, :], in0=yq[:, :, 128:256], in1=p1v[sl][:, :, 128:256], op=ALU.add),
                             reads=[YQt[sl][ci], BK[sl]], writes=[T2Tt[c], DR["qs"], DR["ks"]])

                for d in range(2):
                    P.op("pool", lambda e, d=d: e.memset(S32[d], 0.0), writes=[DN["S%d" % d]])
                    P.op("pool", lambda e, d=d: e.memset(Sbf[d], 0.0), writes=[DN["Sb%d" % d]])

                def chain_ops(step, d):
                    c = step if d == 0 else 15 - step
                    bA_ = 2 + 4 * d
                    bB_ = 3 + 4 * d
                    pk = bank(bA_)[:, 0:128]
                    px = bank(bA_)[:, 128:256]
                    po = bank(bB_)[:, 0:128]
                    pd = bank(bB_)[:, 128:256]
                    tA, tB = BK[bA_], BK[bB_]
                    St, Sbt, rt, xt_ = DN["S%d" % d], DN["Sb%d" % d], DN["r%d" % d], DN["x%d" % d]
                    eg = egl[:, c, dh[d]:dh[d] + 1]
                    oc = OT[:, c * 128:(c + 1) * 128]
                    ops = [
                        lambda: P.op("pe", lambda e: e.matmul(pk, lhsT=QK[:, c, 0, :], rhs=Sbf[d], start=True, stop=True), reads=[DN["QK"], Sbt], writes=[tA]),
                        lambda: P.op("dve", lambda e: e.scalar_tensor_tensor(out=rbf[d], in0=pk, scalar=colC[:, c, dh[d]:dh[d] + 1], in1=vtok[:, c, :], op0=ALU.mult, op1=ALU.add),
                                     reads=[tA, SM["colC"], DN["vtok"]], writes=[rt]),
                        lambda: P.op("pe", lambda e: e.matmul(px, lhsT=T2T[:, c, d, :], rhs=rbf[d], start=True, stop=True), reads=[T2Tt[c], rt], writes=[tA]),
                        lambda: P.op("act", lambda e: e.copy(out=xbf[d], in_=px), reads=[tA], writes=[xt_]),
                        lambda: P.op("pe", lambda e: e.matmul(pd, lhsT=KDB[:, d, c, :], rhs=xbf[d], start=True, stop=True), reads=[DN["KDB"], xt_], writes=[tB]),
                        lambda: P.op("pe", lambda e: e.matmul(po, lhsT=Sbf[d], rhs=QG[:, d, c, :], start=True, stop=False), reads=[Sbt, QGt[d]], writes=[tB]),
                        lambda: P.op("pe", lambda e: e.matmul(po, lhsT=xbf[d], rhs=ATT[:, c, d, :], start=False, stop=True), reads=[xt_, ATTt[c]], writes=[tB]),
                        lambda: P.op("dve", lambda e: e.scalar_tensor_tensor(out=Sbf[d], in0=S32[d], scalar=eg, in1=pd, op0=ALU.mult, op1=ALU.add),
                                     reads=[St, SM["egl"], tB], writes=[Sbt]),
                        lambda: P.op("dve", lambda e: e.scalar_tensor_tensor(out=S32[d], in0=S32[d], scalar=eg, in1=pd, op0=ALU.mult, op1=ALU.add),
                                     reads=[St, SM["egl"], tB], writes=[St]),
                    ]
                    if step < 8:
                        ops.append(lambda: P.op("act", lambda e: e.copy(out=oc, in_=po), reads=[tB], writes=[OTt[c], W["rxp"]]))
                    else:
                        ops.append(lambda: P.op("dve", lambda e: e.tensor_tensor(out=oc, in0=oc, in1=po, op=ALU.add), reads=[tB, OTt[c]], writes=[OTt[c]]))
                    return ops

                pending = []

                def emit_some(n):
                    for _ in range(n):
                        if pending:
                            pending.pop(0)()

                def sched(step):
                    a_, b_ = chain_ops(step, 0), chain_ops(step, 1)
                    for i_ in range(len(a_)):
                        pending.append(a_[i_])
                        pending.append(b_[i_])

                slot23 = [YQt[2][0], YQt[2][1], YTt[2][0], YTt[2][1], KQAt[2], KQBt[2], Dgt[2], D2t[2],
                          YQt[3][0], YQt[3][1], YTt[3][0], YTt[3][1], KQAt[3], KQBt[3], Dgt[3], D2t[3]]
                if NS > 2:
                    P.alias(slot23, [W["F0"], W["F1"], W["F2"]])
                for q4 in range(4):
                    cs = [4 * q4 + i_ for i_ in range(4)]
                    units_setup([(sl, cs[sl]) for sl in range(4)])
                    for k in range(7):
                        for sl in range(4):
                            unit_level_mm(sl, cs[sl], k)
                        for sl in range(4):
                            unit_level_ev(sl, cs[sl], k)
                if NS > 2:
                    P.alias([W["F0"], W["F1"], W["F2"]], slot23)
                for step in range(16):
                    sched(step)
                    emit_some(len(pending))

                if stop_after == "chain":
                    dbg_dump("OT", OT, OTt, [128, 2048], F32)
                    raise _Stop()
                P.op("act", lambda e: e.activation(out=sqb, in_=OT, func=AF.Square), reads=OTt, writes=[DR["sqb"]] + ATTt)
                for tt in range(4):
                    P.op("pe", lambda e, tt=tt: e.matmul(bank(tt), lhsT=ones_bf, rhs=sqb[:, tt * 512:(tt + 1) * 512], start=True, stop=True),
                         reads=[CBt, DR["sqb"]], writes=[BK[tt]])
                P.op("act", lambda e: e.activation(out=F0, in_=bA, func=AF.Ln, bias=EPS, scale=1.0 / 128.0), reads=bAt, writes=[W["F0"]])
                P.op("act", lambda e: e.activation(out=F0, in_=F0, func=AF.Exp, scale=-0.5), reads=[W["F0"]], writes=[W["F0"]])
                wv, wt = load_w(win[l, :, :, C_Z + hd * 128:C_Z + (hd + 1) * 128], [128, 8, 128])
                bB = bank(4, 4)
                proj(bB, BK[4:8], wv, wt, hfn, Ht, 8)
                P.op("act", lambda e: e.activation(out=F1, in_=bB, func=AF.Tanh, scale=0.5), reads=BK[4:8], writes=[W["F1"]])
                P.op("dve", lambda e: e.scalar_tensor_tensor(out=F2, in0=F1, scalar=1.0, in1=bB, op0=ALU.add, op1=ALU.mult),
                     reads=[W["F1"]] + BK[4:8], writes=[W["F2"]])
                P.op("dve", lambda e: e.scalar_tensor_tensor(out=F1, in0=OT, scalar=dnnorm[:, l:l + 1], in1=F0, op0=ALU.mult, op1=ALU.mult),
                     reads=OTt + [VEC, W["F0"]], writes=[W["F1"]])
                P.op("dve", lambda e, hd=hd: e.scalar_tensor_tensor(out=RYb[:, hd, :], in0=F1, scalar=0.5, in1=F2, op0=ALU.mult, op1=ALU.mult),
                     reads=[W["F1"], W["F2"]], writes=[RYt[hd]])
            if s == 0 and l == 0:
                dbg_dump("yd", RYb[:, 0:8, :], RYt[0:8], [128, 8, SEQ], BF16)
            if stop_after == "dn":
                raise _Stop()

            MIXB = [RYtail[:, 0:4096].rearrange("p (c t) -> p c t", c=8), RYtail[:, 4096:8192].rearrange("p (c t) -> p c t", c=8)]
            MIXt = [T(), T()]
            P.alias(MIXt, list(DR.values()) + T2Tt + ATTt)
            thg = [QGbuf[:, 0:1024].bitcast(F32), QGbuf[:, 1024:2048].bitcast(F32)]
            mst = [QGbuf[:, 2048:2560], QGbuf[:, 2560:3072]]
            thg_t = [T(), T()]
            mst_t = [T(), T()]
            P.alias(thg_t + mst_t, QGt)
            allwork = list(W.values()) + list(DN.values()) + OTt
            P.alias(Xt, allwork)
            bi = [0]

            def nb():
                b = bi[0]
                bi[0] = (b + 1) % 8
                return bank(b), BK[b]
            for tt in range(4):
                mb, mbt = MIXB[tt % 2], MIXt[tt % 2]
                for n in range(8):
                    wv, wt = load_w(wdn[l, :, :, n * 128:(n + 1) * 128], [128, 8, 128])
                    wv2, wt2 = load_w(win[l, :, :, C_GD + n * 128:C_GD + (n + 1) * 128], [128, 8, 128])
                    py, pyt = nb()
                    pg, pgt = nb()
                    for k in range(8):
                        P.op("pe", lambda e, k=k, py=py, wv=wv: e.matmul(py, lhsT=wv[:, k, :], rhs=RYb[:, k, tt * 512:(tt + 1) * 512], start=(k == 0), stop=(k == 7)),
                             reads=[wt, RYt[k]], writes=[pyt])
                    for k in range(8):
                        P.op("pe", lambda e, k=k, pg=pg, wv2=wv2: e.matmul(pg, lhsT=wv2[:, k, :], rhs=RH[:, k, tt * 512:(tt + 1) * 512], start=(k == 0), stop=(k == 7)),
                             reads=[wt2, Ht[k]], writes=[pgt])
                    j = n % 2
                    P.op("act", lambda e, j=j, pg=pg: e.activation(out=thg[j], in_=pg, func=AF.Tanh, scale=0.5), reads=[pgt], writes=[thg_t[j]])
                    P.dma("sp", mst[j], mixspill[n, :, tt * 512:(tt + 1) * 512], reads=[MSt[n]], writes=[mst_t[j]])
                    P.op("dve", lambda e, j=j, py=py: e.scalar_tensor_tensor(out=thg[j], in0=thg[j], scalar=1.0, in1=py, op0=ALU.add, op1=ALU.mult),
                         reads=[thg_t[j], pyt], writes=[thg_t[j]])
                    P.op("dve", lambda e, j=j, n=n, mb=mb: e.tensor_tensor(out=mb[:, n, :], in0=thg[j], in1=mst[j], op=ALU.add),
                         reads=[thg_t[j], mst_t[j]], writes=[mbt])
                for n in range(8):
                    wv, wt = load_w(wout[l, :, :, n * 128:(n + 1) * 128], [128, 8, 128])
                    po_, pot = nb()
                    for k in range(8):
                        P.op("pe", lambda e, k=k, po_=po_, wv=wv, mb=mb: e.matmul(po_, lhsT=wv[:, k, :], rhs=mb[:, k, :], start=(k == 0), stop=(k == 7)),
                             reads=[wt, mbt], writes=[pot])
                    xs = Xv[:, n, tt * 512:(tt + 1) * 512]
                    P.dma("sp", xs, xspill[:, n, tt * 512:(tt + 1) * 512], reads=[XSt[n]], writes=[Xt[n]])
                    P.op("dve", lambda e, xs=xs, po_=po_: e.scalar_tensor_tensor(out=xs, in0=po_, scalar=0.5, in1=xs, op0=ALU.mult, op1=ALU.add),
                         reads=[pot, Xt[n]], writes=[Xt[n]])
            P.alias(RYt, MIXt + RYt)
            P.alias(QGt, thg_t + mst_t)
            if s == 0 and l == 0:
                dbg_dump("xmix", Xv, Xt, [128, 8, SEQ], F32)
            if stop_after == "mix":
                raise _Stop()

            rstd, tmp_t = rms_stats(1.0 / D)
            for c in range(8):
                g = norms[:, (l * 2 + 1) * 8 + c:(l * 2 + 1) * 8 + c + 1]
                P.op("dve", lambda e, c=c, g=g: e.scalar_tensor_tensor(out=RH[:, c, :], in0=Xv[:, c, :], scalar=g, in1=rstd, op0=ALU.mult, op1=ALU.mult),
                     reads=[Xt[c], tmp_t[9], VEC], writes=[Ht[c]])
            ACTb = RY[:, 0:22528].rearrange("p (f t) -> p f t", f=22)
            ACt = [T() for _ in range(22)]
            P.alias(ACt, tmp_t)
            thf = [QGbuf[:, 0:2048].bitcast(F32), QGbuf[:, 2048:4096].bitcast(F32)]
            thf_t = [T(), T()]
            P.alias(thf_t, QGt)
            for half in range(2):
                t0 = half * 1024
                for f in range(22):
                    wv, wt = load_w(wgu[l, :, :, f * 128:(f + 1) * 128], [128, 8, 128])
                    wv2, wt2 = load_w(wgu[l, :, :, DFF + f * 128:DFF + (f + 1) * 128], [128, 8, 128])
                    pb, pt = grp()
                    for k in range(8):
                        for q_ in range(2):
                            P.op("pe", lambda e, k=k, q_=q_, pb=pb, wv=wv: e.matmul(pb[:, q_ * 512:(q_ + 1) * 512], lhsT=wv[:, k, :],
                                                                                     rhs=RH[:, k, t0 + q_ * 512:t0 + (q_ + 1) * 512], start=(k == 0), stop=(k == 7)),
                                 reads=[wt, Ht[k]], writes=[pt[q_]])
                    for k in range(8):
                        for q_ in range(2):
                            P.op("pe", lambda e, k=k, q_=q_, pb=pb, wv2=wv2: e.matmul(pb[:, 1024 + q_ * 512:1024 + (q_ + 1) * 512], lhsT=wv2[:, k, :],
                                                                                       rhs=RH[:, k, t0 + q_ * 512:t0 + (q_ + 1) * 512], start=(k == 0), stop=(k == 7)),
                                 reads=[wt2, Ht[k]], writes=[pt[2 + q_]])
                    j = f % 2
                    P.op("act", lambda e, j=j, pb=pb: e.activation(out=thf[j], in_=pb[:, 0:1024], func=AF.Tanh, scale=0.5), reads=pt[0:2], writes=[thf_t[j]])
                    P.op("dve", lambda e, j=j, pb=pb: e.scalar_tensor_tensor(out=thf[j], in0=thf[j], scalar=1.0, in1=pb[:, 0:1024], op0=ALU.add, op1=ALU.mult),
                         reads=[thf_t[j]] + pt[0:2], writes=[thf_t[j]])
                    P.op("dve", lambda e, j=j, pb=pb, f=f: e.scalar_tensor_tensor(out=ACTb[:, f, :], in0=thf[j], scalar=0.5, in1=pb[:, 1024:2048], op0=ALU.mult, op1=ALU.mult),
                         reads=[thf_t[j]] + pt[2:4], writes=[ACt[f]])
                for n in range(8):
                    wv, wt = load_w(wdown[l, :, 0:11, n * 128:(n + 1) * 128], [128, 11, 128])
                    wv2, wt2 = load_w(wdown[l, :, 11:22, n * 128:(n + 1) * 128], [128, 11, 128])
                    pb, pt = grp()
                    for f in range(22):
                        for q_ in range(2):
                            w_, wt_ = (wv, wt) if f < 11 else (wv2, wt2)
                            P.op("pe", lambda e, f=f, q_=q_, pb=pb, w_=w_: e.matmul(pb[:, q_ * 512:(q_ + 1) * 512], lhsT=w_[:, f % 11, :],
                                                                                     rhs=ACTb[:, f, q_ * 512:(q_ + 1) * 512], start=(f == 0), stop=(f == 21)),
                                 reads=[wt_, ACt[f]], writes=[pt[q_]])
                    xs = Xv[:, n, t0:t0 + 1024]
                    P.op("dve", lambda e, xs=xs, pb=pb: e.tensor_tensor(out=xs, in0=xs, in1=pb[:, 0:1024], op=ALU.add), reads=[Xt[n]] + pt[0:2], writes=[Xt[n]])
            P.alias(RYt, ACt)
            P.alias(QGt, thf_t)
            if s == 0:
                dbg_dump("xl%d" % l, Xv, Xt, [128, 8, SEQ], F32)
        if stop_after:
            raise _Stop()

        rstd, tmp_t = rms_stats(1.0 / D)
        ost = [RYf[:, 0:2048], RYf[:, 2048:4096]]
        ost_t = [T(), T()]
        P.alias(ost_t, tmp_t[0:8])
        for c in range(8):
            j = c % 2
            P.op("dve", lambda e, c=c, j=j: e.scalar_tensor_tensor(out=ost[j], in0=Xv[:, c, :], scalar=fnorm[:, c:c + 1], in1=rstd, op0=ALU.mult, op1=ALU.mult),
                 reads=[Xt[c], tmp_t[9], VEC], writes=[ost_t[j]])
            P.dma("sp", outT[s, :, c, :], ost[j], reads=[ost_t[j]], is_output=True)
        P.alias(RYt, tmp_t + ost_t)

    return


def prep_weights(w_in, rg_conv_w, rg_conv_b, rg_wa, rg_ba, rg_wx, rg_bx, rg_lambda, w_rnn_proj, dn_conv_w,
                 dn_a_log, dn_dt_bias, dn_norm, w_dn_proj, w_out, mix_norm, ffn_norm, w_gate_up, w_down, final_norm):
    L = NLAYER
    f = lambda a: np.ascontiguousarray(np.asarray(a, dtype=np.float32))
    m = {}
    m["win"] = f(np.asarray(w_in).reshape(L, 8, 128, NIN).transpose(0, 2, 1, 3))
    wa = np.asarray(rg_wa)
    wx = np.asarray(rg_wx)
    g = np.stack([wa[:, 0], wa[:, 1], wx[:, 0], wx[:, 1]], axis=3)
    m["gatew"] = f(g)
    cwv = np.asarray(rg_conv_w).reshape(L, 4, 12, 128).transpose(3, 0, 2, 1)
    pv = lambda a: np.asarray(a).reshape(L, 12, 128).transpose(2, 0, 1)[..., None]
    pv2 = lambda a, d: np.asarray(a)[:, d].reshape(L, 12, 128).transpose(2, 0, 1)[..., None]
    rgv = np.concatenate([cwv, pv(rg_conv_b), pv2(rg_ba, 0), pv2(rg_ba, 1), pv2(rg_bx, 0), pv2(rg_bx, 1),
                          pv2(rg_lambda, 0), pv2(rg_lambda, 1)], axis=3)
    m["rgv"] = f(rgv.reshape(128, L * 12, 11))
    m["wrnn"] = f(np.asarray(w_rnn_proj).reshape(L, 12, 128, D).transpose(0, 2, 1, 3))
    m["dncw"] = f(np.asarray(dn_conv_w).reshape(L, 4, 24, 128).transpose(3, 0, 2, 1).reshape(128, L * 24, 4))
    dv = np.stack([np.asarray(dn_a_log).reshape(L, 16), np.asarray(dn_dt_bias).reshape(L, 16)], axis=1)
    m["dnv"] = f(np.broadcast_to(dv[None], (128, L, 2, 16)))
    m["dnnorm"] = f(np.asarray(dn_norm).T)
    m["wdn"] = f(np.asarray(w_dn_proj).reshape(L, 8, 128, D).transpose(0, 2, 1, 3))
    m["wout"] = f(np.asarray(w_out).reshape(L, 8, 128, D).transpose(0, 2, 1, 3))
    nm = np.stack([np.asarray(mix_norm).reshape(L, 8, 128), np.asarray(ffn_norm).reshape(L, 8, 128)], axis=1)
    m["norms"] = f(nm.transpose(3, 0, 1, 2).reshape(128, L * 2 * 8))
    m["fnorm"] = f(np.asarray(final_norm).reshape(8, 128).T)
    m["wgu"] = f(np.asarray(w_gate_up).reshape(L, 8, 128, 2 * DFF).transpose(0, 2, 1, 3))
    m["wdown"] = f(np.asarray(w_down).reshape(L, 22, 128, D).transpose(0, 2, 1, 3))
    m["consts"] = make_consts()
    return m


def prep_x(xs):
    n = xs.shape[0]
    return np.ascontiguousarray(np.asarray(xs, dtype=np.float32).transpose(0, 2, 1).reshape(n, 8, 128, SEQ).transpose(0, 2, 1, 3))


def unprep_x(o):
    n = o.shape[0]
    return np.ascontiguousarray(o.transpose(0, 2, 1, 3).reshape(n, D, SEQ).transpose(0, 2, 1))


def kernel(x, mix_norm, w_in, rg_conv_w, rg_conv_b, rg_wa, rg_ba, rg_wx, rg_bx, rg_lambda,
           w_rnn_proj, dn_conv_w, dn_a_log, dn_dt_bias, dn_norm, w_dn_proj, w_out,
           ffn_norm, w_gate_up, w_down, final_norm):
    x = np.asarray(x)
    B = x.shape[0]
    nseq = B // NCORES
    wm = prep_weights(w_in, rg_conv_w, rg_conv_b, rg_wa, rg_ba, rg_wx, rg_bx, rg_lambda, w_rnn_proj, dn_conv_w,
                      dn_a_log, dn_dt_bias, dn_norm, w_dn_proj, w_out, mix_norm, ffn_norm, w_gate_up, w_down, final_norm)
    nc = bass.Bass("TRN2", target_bir_lowering=False)
    build(nc, NSEQ=nseq)
    in_maps = []
    for cidx in range(NCORES):
        mm = dict(wm)
        mm["xT"] = prep_x(x[cidx * nseq:(cidx + 1) * nseq])
        in_maps.append(mm)
    res = run_bass_kernel_spmd(nc, in_maps, core_ids=list(range(NCORES)))
    outs = [unprep_x(np.asarray(r["outT"])) for r in res.results]
    return np.concatenate(outs, axis=0).astype(np.float32)
```
